# Optimizing a Trainium2 kernel written in Bass

```python
import math
import jax, jax.numpy as jnp
from jax import lax
import numpy as np

D_MODEL = 1024
BATCH = 4
SEQ = 8192
DEPTH = 4

CHUNK = 64
EPS = 1e-6
GDN_HEAD_DIM = 128
GDN_HEADS = D_MODEL // GDN_HEAD_DIM
GDN_W = GDN_HEADS * GDN_HEAD_DIM
GDN_CONV = 4
ATT_HEAD_DIM = 128
ATT_HEADS = D_MODEL // ATT_HEAD_DIM
ATT_W = ATT_HEADS * ATT_HEAD_DIM
PAST_CHUNKS = 8
BAND = (PAST_CHUNKS + 1) * CHUNK
MAX_REL_DIST = 256
D_FF = 4 * D_MODEL
IN_SPLITS = (3 * GDN_W, 4 * GDN_W, 4 * GDN_W + GDN_HEADS, 4 * GDN_W + 2 * GDN_HEADS,
             4 * GDN_W + 2 * GDN_HEADS + 3 * ATT_W)
IN_WIDTH = 4 * GDN_W + 2 * GDN_HEADS + 3 * ATT_W + 2 * D_MODEL

kernel_name = "hybrid_gdn_chunkattn_adaln_encoder"


def rms_norm(x, gain):
    xf = x.astype(jnp.float32)
    y = xf * lax.rsqrt(jnp.mean(xf * xf, axis=-1, keepdims=True) + EPS)
    return (y * gain.astype(jnp.float32)).astype(x.dtype)


def l2_normalize(x):
    xf = x.astype(jnp.float32)
    return xf * lax.rsqrt(jnp.sum(xf * xf, axis=-1, keepdims=True) + EPS)


def causal_depthwise_conv(x, w):
    k_len, ch = w.shape
    return lax.conv_general_dilated(
        x, w[:, None, :], window_strides=(1,), padding=[(k_len - 1, 0)],
        dimension_numbers=('NWC', 'WIO', 'NWC'), feature_group_count=ch)


def chunk_gated_delta_rule(q, k, v, g, beta):
    B, S, H, DK = q.shape
    DV = v.shape[-1]
    N = S // CHUNK
    f32 = jnp.float32

    def to_chunks(t):
        t = t.astype(f32).reshape((B, N, CHUNK, H) + t.shape[3:])
        return jnp.moveaxis(t, 3, 1)

    q = to_chunks(q) * (DK ** -0.5)
    k = to_chunks(k)
    v = to_chunks(v)
    g = to_chunks(g)
    beta = to_chunks(beta)
    G = jnp.cumsum(g, axis=-1)
    pos = jnp.arange(CHUNK)
    incl = pos[:, None] >= pos[None, :]
    strict = pos[:, None] > pos[None, :]
    decay = jnp.exp(jnp.where(incl, G[..., :, None] - G[..., None, :], -jnp.inf))
    kk = jnp.einsum('bhnid,bhnjd->bhnij', k * beta[..., None], k)
    a_mat = jnp.where(strict, kk * decay, 0.0) + jnp.eye(CHUNK, dtype=f32)
    rhs = jnp.concatenate([v * beta[..., None], k * (beta * jnp.exp(G))[..., None]], axis=-1)
    sol = lax.linalg.triangular_solve(a_mat, rhs, left_side=True, lower=True, unit_diagonal=True)
    u, w = sol[..., :DV], sol[..., DV:]
    attn = jnp.einsum('bhnid,bhnjd->bhnij', q, k) * decay
    q_dec = q * jnp.exp(G)[..., None]
    k_dec = k * jnp.exp(G[..., -1:] - G)[..., None]
    chunk_decay = jnp.exp(G[..., -1])
    xs = tuple(jnp.moveaxis(t, 2, 0) for t in (q_dec, k_dec, u, w, attn, chunk_decay))

    def step(state, inp):
        qd, kd, uu, ww, at, cd = inp
        v_new = uu - jnp.einsum('bhck,bhkv->bhcv', ww, state)
        o = jnp.einsum('bhck,bhkv->bhcv', qd, state) + jnp.einsum('bhij,bhjv->bhiv', at, v_new)
        state = state * cd[..., None, None] + jnp.einsum('bhck,bhcv->bhkv', kd, v_new)
        return state, o

    s0 = jnp.zeros((B, H, DK, DV), f32)
    _, o = lax.scan(step, s0, xs)
    return jnp.transpose(o, (1, 0, 3, 2, 4)).reshape(B, S, H, DV)


def gated_deltanet_branch(qkv, gate, a, b, conv_w, a_log, dt_bias, o_norm):
    B, S, _ = qkv.shape
    qkv = jax.nn.silu(causal_depthwise_conv(qkv, conv_w))
    q, k, v = jnp.split(qkv, 3, axis=-1)
    q = l2_normalize(q.reshape(B, S, GDN_HEADS, GDN_HEAD_DIM))
    k = l2_normalize(k.reshape(B, S, GDN_HEADS, GDN_HEAD_DIM))
    v = v.reshape(B, S, GDN_HEADS, GDN_HEAD_DIM)
    beta = jax.nn.sigmoid(b.astype(jnp.float32))
    g = -jnp.exp(a_log.astype(jnp.float32)) * jax.nn.softplus(
        a.astype(jnp.float32) + dt_bias.astype(jnp.float32))
    o = chunk_gated_delta_rule(q, k, v, g, beta)
    o = rms_norm(o, o_norm) * jax.nn.silu(
        gate.reshape(B, S, GDN_HEADS, GDN_HEAD_DIM).astype(jnp.float32))
    return o.reshape(B, S, GDN_W).astype(qkv.dtype)


def chunk_band_attention(q, k, v, rel_bias):
    B, S, _ = q.shape
    N = S // CHUNK
    dtype = q.dtype
    q = q.reshape(B, N, CHUNK, ATT_HEADS, ATT_HEAD_DIM).transpose(1, 0, 2, 3, 4) * (ATT_HEAD_DIM ** -0.5)
    pad = ((0, 0), (PAST_CHUNKS * CHUNK, 0), (0, 0), (0, 0))
    k = jnp.pad(k.reshape(B, S, ATT_HEADS, ATT_HEAD_DIM), pad)
    v = jnp.pad(v.reshape(B, S, ATT_HEADS, ATT_HEAD_DIM), pad)
    qi = jnp.arange(CHUNK)[:, None]
    kj = jnp.arange(BAND)[None, :]
    rel = jnp.clip(qi - kj + PAST_CHUNKS * CHUNK, -MAX_REL_DIST, MAX_REL_DIST) + MAX_REL_DIST
    bias = rel_bias[:, rel].astype(jnp.float32)

    def one_chunk(args):
        n, qn = args
        kb = lax.dynamic_slice_in_dim(k, n * CHUNK, BAND, axis=1)
        vb = lax.dynamic_slice_in_dim(v, n * CHUNK, BAND, axis=1)
        s = jnp.einsum('bqhd,bkhd->bhqk', qn, kb).astype(jnp.float32) + bias
        valid = kj >= (PAST_CHUNKS - n) * CHUNK
        s = jnp.where(valid, s, -jnp.inf)
        p = jax.nn.softmax(s, axis=-1).astype(dtype)
        return jnp.einsum('bhqk,bkhd->bqhd', p, vb)

    o = lax.map(one_chunk, (jnp.arange(N), q))
    return o.transpose(1, 0, 2, 3, 4).reshape(B, S, ATT_W)


def setup_inputs(seed: int = 0) -> dict:
    key = jax.random.key(seed)
    ks = jax.random.split(key, 16)
    f32 = jnp.float32
    L = DEPTH

    def nrm(k, shape, s):
        return jax.random.normal(k, shape, f32) * s

    x = nrm(ks[0], (BATCH, SEQ, D_MODEL), 1.0)
    c = nrm(ks[1], (BATCH, D_MODEL), 1.0)
    w_ada = nrm(ks[2], (L, D_MODEL, 6 * D_MODEL), 0.5 * D_MODEL ** -0.5)
    b_ada = nrm(ks[3], (L, 6 * D_MODEL), 0.02)
    norm_mix = 1.0 + nrm(ks[4], (L, D_MODEL), 0.02)
    norm_mlp = 1.0 + nrm(ks[5], (L, D_MODEL), 0.02)
    w_in = nrm(ks[6], (L, D_MODEL, IN_WIDTH), D_MODEL ** -0.5)
    conv_w = nrm(ks[7], (L, GDN_CONV, 3 * GDN_W), GDN_CONV ** -0.5)
    a_log = jnp.log(jax.random.uniform(ks[8], (L, GDN_HEADS), f32, 1.0, 16.0))
    dt = jnp.exp(jax.random.uniform(ks[9], (L, GDN_HEADS), f32, math.log(1e-3), math.log(1e-1)))
    dt_bias = dt + jnp.log(-jnp.expm1(-dt))
    gdn_norm = 1.0 + nrm(ks[10], (L, GDN_HEAD_DIM), 0.02)
    rel_bias = nrm(ks[11], (L, ATT_HEADS, 2 * MAX_REL_DIST + 1), 0.2)
    w_out = nrm(ks[12], (L, D_MODEL, D_MODEL), D_MODEL ** -0.5)
    w_ff_in = nrm(ks[13], (L, D_MODEL, D_FF), D_MODEL ** -0.5)
    w_ff_out = nrm(ks[14], (L, D_FF, D_MODEL), D_FF ** -0.5)
    final_norm = 1.0 + nrm(ks[15], (D_MODEL,), 0.02)
    return {"x": x, "c": c, "w_ada": w_ada, "b_ada": b_ada, "norm_mix": norm_mix,
            "norm_mlp": norm_mlp, "w_in": w_in, "conv_w": conv_w, "a_log": a_log,
            "dt_bias": dt_bias, "gdn_norm": gdn_norm, "rel_bias": rel_bias, "w_out": w_out,
            "w_ff_in": w_ff_in, "w_ff_out": w_ff_out, "final_norm": final_norm}


def reference(x, c, w_ada, b_ada, norm_mix, norm_mlp, w_in, conv_w, a_log, dt_bias,
              gdn_norm, rel_bias, w_out, w_ff_in, w_ff_out, final_norm):
    c_act = jax.nn.silu(c)
    for l in range(DEPTH):
        mod = (c_act @ w_ada[l] + b_ada[l])[:, None, :]
        sh1, sc1, gt1, sh2, sc2, gt2 = jnp.split(mod, 6, axis=-1)
        h = rms_norm(x, norm_mix[l]) * (1.0 + sc1) + sh1
        z = h @ w_in[l]
        gdn_qkv, gdn_gate, gdn_a, gdn_b, att_qkv, br_gates = jnp.split(z, IN_SPLITS, axis=-1)
        o_a = gated_deltanet_branch(gdn_qkv, gdn_gate, gdn_a, gdn_b, conv_w[l],
                                    a_log[l], dt_bias[l], gdn_norm[l])
        att_q, att_k, att_v = jnp.split(att_qkv, 3, axis=-1)
        o_b = chunk_band_attention(att_q, att_k, att_v, rel_bias[l])
        g_a, g_b = jnp.split(jax.nn.sigmoid(br_gates), 2, axis=-1)
        x = x + gt1 * ((g_a * o_a + g_b * o_b) @ w_out[l])
        h = rms_norm(x, norm_mlp[l]) * (1.0 + sc2) + sh2
        x = x + gt2 * (jnp.square(jax.nn.relu(h @ w_ff_in[l])) @ w_ff_out[l])
    return rms_norm(x, final_norm)
```

```python
import numpy as np
from contextlib import ExitStack
import concourse.bass as bass
import concourse.mybir as mybir
from concourse.bass_utils import run_bass_kernel_spmd

F32 = mybir.dt.float32
BF16 = mybir.dt.bfloat16
AF = mybir.ActivationFunctionType
ALU = mybir.AluOpType
AX = mybir.AxisListType

D = 1024
H = 8
HD = 128
KC = 8
DFF = 4096
INW = 9232
EPS = 1e-6
LIM = 24000
NEG = -30000.0

C_ID, C_ONE, C_MA, C_ST, C_TRI, C_AM = 0, 128, 256, 384, 512, 640
C_S0, C_S1 = 1280, 1408
C_MD, C_MC1, C_MC2, C_MDT, C_MC1T, C_MC2T = 1536, 1664, 1792, 1920, 2048, 2176
C_N = 2304


class Buf:
    __slots__ = ("name", "w", "r")

    def __init__(self, name):
        self.name = name
        self.w = None
        self.r = {}


class Src:
    def __init__(self, mk, name, unit):
        self.mk = mk
        self.name = name
        self.unit = unit
        self.sems = []
        self.count = 0
        self.finals = []

    def _roll(self):
        if not self.sems or self.count + self.unit > LIM:
            if self.sems:
                self.finals.append(self.count)
            self.sems.append(self.mk.new_sem())
            self.count = 0

    def next_event(self):
        self._roll()
        self.count += self.unit
        return (self, len(self.sems) - 1, self.count)

    def peek_event(self):
        self._roll()
        return (self, len(self.sems) - 1, self.count + self.unit)


class Eng(Src):
    def __init__(self, mk, name, key, same_sync):
        super().__init__(mk, name, 1)
        self.key = key
        self.same_sync = same_sync
        self.thunks = []
        self.waited = {}
        self.is_eng = True


class MK:
    def __init__(self, nc, es):
        self.nc = nc
        self.es = es
        self.nsem = 0
        self.PE = Eng(self, "pe", "tensor", False)
        self.ACT = Eng(self, "act", "scalar", True)
        self.DVE = Eng(self, "dve", "vector", True)
        self.POOL = Eng(self, "pool", "gpsimd", True)
        self.SP = Eng(self, "sp", "sync", False)
        self.engs = [self.PE, self.ACT, self.DVE, self.POOL, self.SP]
        self.nops = 0
        self.uid = 0
        self.streams = {}

    def new_sem(self):
        self.nsem += 1
        return self.es.enter_context(self.nc.semaphore(f"s{self.nsem}"))

    def stream(self, name):
        if name in self.streams:
            return self.streams[name]
        s = Src(self, name, 16)
        s.is_eng = False
        s.last = None
        self.streams[name] = s
        return s

    def _wait(self, E, ev):
        src, ep, val = ev
        if src is E and not E.same_sync:
            return
        key = (id(src), ep)
        if E.waited.get(key, 0) >= val:
            return
        if src.is_eng:
            for (sid, e2), v in E.waited.items():
                if sid == id(src) and e2 > ep:
                    return
        E.waited[key] = val
        sem = src.sems[ep]
        E.thunks.append(lambda eng, sem=sem, val=val: eng.wait_ge(sem, val))

    def _deps(self, E, reads, writes):
        for b in reads:
            if b.w is not None:
                self._wait(E, b.w)
        for b in writes:
            if b.w is not None:
                self._wait(E, b.w)
            for ev in b.r.values():
                self._wait(E, ev)

    def op(self, E, fn, reads=(), writes=(), signal=True):
        self.nops += 1
        self._deps(E, reads, writes)
        if signal:
            ev = E.next_event()
            sem = ev[0].sems[ev[1]]
            E.thunks.append(lambda eng, fn=fn, sem=sem: fn(eng).then_inc(sem, 1))
        else:
            assert E is self.PE
            ev = E.peek_event()
            E.thunks.append(lambda eng, fn=fn: fn(eng))
        for b in writes:
            b.w = ev
            b.r = {}
        for b in reads:
            b.r[id(E)] = ev

    def dma(self, Q, st, out, in_, reads=(), writes=(), **kw):
        self.nops += 1
        if isinstance(st, str):
            st = self.stream(st)
        self._deps(Q, reads, writes)
        if st.last is not None:
            self._wait(Q, st.last)
        ev = st.next_event()
        st.last = ev
        sem = ev[0].sems[ev[1]]
        Q.thunks.append(lambda eng, out=out, in_=in_, sem=sem, kw=kw:
                        eng.dma_start(out=out, in_=in_, **kw).then_inc(sem, 16))
        for b in writes:
            b.w = ev
            b.r = {}
        for b in reads:
            b.r[id(st)] = ev

    def final_wait(self, E, bufs):
        for b in bufs:
            if b.w is not None:
                self._wait(E, b.w)

    def barrier(self):
        evs = []
        for E in self.engs:
            if E.sems and E.count > 0:
                evs.append((E, len(E.sems) - 1, E.count))
        for s in self.streams.values():
            if s.last is not None:
                evs.append(s.last)
        for E in self.engs:
            for ev in evs:
                if ev[0] is not E:
                    self._wait(E, ev)

    def flush(self):
        self.barrier()
        nc = self.nc
        with nc.Block() as block:
            for E in self.engs:
                th = E.thunks
                E.thunks = []
                if not th:
                    continue

                def mkfn(th):
                    def f(eng):
                        for t in th:
                            t(eng)
                    return f
                getattr(block, E.key)(mkfn(th))


class _Stop(Exception):
    pass


DBG = {"p2_stage": 99}


def build_program(S, L, debug=False, stop=None):
    assert S % 512 == 0
    NT = S // 128
    nc = bass.Bass("TRN2", target_bir_lowering=False)
    es = ExitStack()
    mk = MK(nc, es)
    PE, ACT, DVE, POOL, SP = mk.PE, mk.ACT, mk.DVE, mk.POOL, mk.SP

    def din(name, shape, dt=F32):
        return nc.dram_tensor(name, list(shape), dt, kind="ExternalInput").ap()

    def dscr(name, shape, dt=F32):
        kind = "ExternalOutput" if debug else "Internal"
        return nc.dram_tensor(name, list(shape), dt, kind=kind).ap()

    x_in = din("x", [S, D])
    cT_in = din("cT", [128, KC])
    wada_in = din("w_ada", [L, D, 6 * D])
    bada_in = din("b_ada", [L, 6 * D])
    nmix_in = din("norm_mix", [L, D])
    nmlp_in = din("norm_mlp", [L, D])
    win_in = din("w_in", [L, D, INW])
    cw_in = din("cw", [128, L * 96])
    alog_in = din("a_log", [1, L * H])
    dtb_in = din("dt_bias", [1, L * H])
    gnorm_in = din("gdn_norm", [1, L * HD])
    biasT_in = din("biasT", [L, 128, H * 640])
    wout_in = din("w_out", [L, D, D])
    w1_in = din("w_ff_in", [L, D, DFF])
    w2_in = din("w_ff_out", [L, DFF, D])
    fnorm_in = din("final_norm", [1, D])
    consts_in = din("consts", [128, C_N])
    out_d = nc.dram_tensor("out", [S, D], F32, kind="ExternalOutput").ap()

    xs_d = dscr("xs", [S, D])
    zg_d = dscr("zg", [3 * D, S])
    zqk_d = dscr("zqk", [2 * D, S], BF16)
    ztok_d = dscr("ztok", [S, 3088])
    va_d = dscr("va", [S, D], BF16)
    m_d = dscr("m", [S, D], BF16)
    mod_d = dscr("modd", [128, 6 * D])

    B_xs = [Buf(f"xs{t}") for t in range(NT)]
    B_zg, B_zqk, B_ztok, B_va, B_m, B_mod = (Buf("zg"), Buf("zqk"), Buf("ztok"), Buf("va"),
                                               Buf("m"), Buf("mod"))
    B_out = Buf("out")

    def sb(name, shape, dt=F32, stack=None):
        mk.uid += 1
        return (stack or es).enter_context(nc.sbuf_tensor(f"{name}_u{mk.uid}", list(shape), dt))

    def ps(name, shape, dt=F32, stack=None):
        mk.uid += 1
        return (stack or es).enter_context(nc.psum_tensor(f"{name}_u{mk.uid}", list(shape), dt))

    consts = sb("consts", [128, C_N])
    B_consts = Buf("consts")
    ident_bf = sb("ident_bf", [128, 128], BF16)
    B_identbf = Buf("identbf")
    cw_sb = sb("cw", [128, L * 96])
    B_cw = Buf("cw")
    crep = sb("crep", [128, KC, 128])
    B_crep = Buf("crep")
    st_c = mk.stream("const")
    mk.dma(SP, st_c, consts[:], consts_in[:, :], writes=[B_consts])
    mk.dma(SP, st_c, cw_sb[:], cw_in[:, :], writes=[B_cw])
    mk.op(ACT, lambda e: e.activation(out=ident_bf[:], in_=consts[:, C_ID:C_ID + 128], func=AF.Copy),
          reads=[B_consts], writes=[B_identbf])
    ident = consts[:, C_ID:C_ID + 128]
    ones = consts[:, C_ONE:C_ONE + 128]
    maskA = consts[:, C_MA:C_MA + 128]
    strict = consts[:, C_ST:C_ST + 128]
    tri = consts[:, C_TRI:C_TRI + 128]

    with ExitStack() as ps0:
        ct = sb("ct", [128, KC], stack=ps0)
        cs = sb("cs", [128, KC], stack=ps0)
        B_ct, B_cs = Buf("ct"), Buf("cs")
        mk.dma(SP, st_c, ct[:], cT_in[:, :], writes=[B_ct])
        mk.op(ACT, lambda e: e.activation(out=cs[:], in_=ct[:], func=AF.Silu), reads=[B_ct], writes=[B_cs])
        for kc in range(KC):
            mk.op(DVE, lambda e, kc=kc: e.tensor_scalar(out=crep[:, kc, :], in0=ones, scalar1=cs[:, kc:kc + 1],
                                                         scalar2=None, op0=ALU.mult),
                  reads=[B_cs, B_consts], writes=[B_crep])
        mk.flush()

    def ck(name):
        if stop == name:
            mk.final_wait(SP, B_xs + [B_zg, B_zqk, B_ztok, B_va, B_m, B_mod])
            mk.flush()
            raise _Stop()

    try:
        _layers(locals())
    except _Stop:
        pass
    es.close()
    return nc, mk


def _layers(G):
    (nc, mk, es, S, L, NT, PE, ACT, DVE, POOL, SP, sb, ps, ck) = [G[k] for k in
        ("nc", "mk", "es", "S", "L", "NT", "PE", "ACT", "DVE", "POOL", "SP", "sb", "ps", "ck")]
    (x_in, wada_in, bada_in, nmix_in, nmlp_in, win_in, alog_in, dtb_in, gnorm_in, biasT_in, wout_in, w1_in, w2_in,
     fnorm_in, out_d, xs_d, zg_d, zqk_d, ztok_d, va_d, m_d, mod_d) = [G[k] for k in
        ("x_in", "wada_in", "bada_in", "nmix_in", "nmlp_in", "win_in", "alog_in", "dtb_in", "gnorm_in", "biasT_in",
         "wout_in", "w1_in", "w2_in", "fnorm_in", "out_d", "xs_d", "zg_d", "zqk_d", "ztok_d", "va_d", "m_d", "mod_d")]
    (B_xs, B_zg, B_zqk, B_ztok, B_va, B_m, B_mod, B_out, consts, B_consts, ident_bf, B_identbf, cw_sb, B_cw, crep,
     B_crep, ident, ones) = [G[k] for k in
        ("B_xs", "B_zg", "B_zqk", "B_ztok", "B_va", "B_m", "B_mod", "B_out", "consts", "B_consts", "ident_bf",
         "B_identbf", "cw_sb", "B_cw", "crep", "B_crep", "ident", "ones")]
    for l in range(L):
        x_src = x_in if l == 0 else xs_d

        with ExitStack() as ph:
            wa = [sb(f"wa{i}", [128, KC, 512], stack=ph) for i in range(2)]
            B_wa = [Buf(f"wa{i}") for i in range(2)]
            modb = sb("modb", [128, 6 * D], stack=ph)
            B_modb = Buf("modb")
            brow = sb("brow", [128, 6 * D], stack=ph)
            B_brow = Buf("brow")
            nrow = sb("nrow", [128, 2 * D], stack=ph)
            B_nrow = Buf("nrow")
            pm = [ps(f"pm{i}", [128, 512], stack=ph) for i in range(2)]
            B_pm = [Buf(f"pm{i}") for i in range(2)]
            st_w = mk.stream("wada")
            st_m = mk.stream("modst")
            mk.dma(SP, st_m, brow[:], bada_in[l:l + 1, :].partition_broadcast(128), writes=[B_brow])
            mk.dma(SP, st_m, nrow[:, 0:D], nmix_in[l:l + 1, :].partition_broadcast(128), writes=[B_nrow])
            mk.dma(SP, st_m, nrow[:, D:2 * D], nmlp_in[l:l + 1, :].partition_broadcast(128), writes=[B_nrow])
            wav = wada_in[l].rearrange("(kc p) n -> p kc n", p=128)
            for nb in range(12):
                i = nb % 2
                mk.dma(SP, f"wada{i}", wa[i][:], wav[:, :, nb * 512:(nb + 1) * 512], writes=[B_wa[i]])
                for kc in range(KC):
                    mk.op(PE, lambda e, i=i, kc=kc: e.matmul(pm[i][:], lhsT=crep[:, kc, :], rhs=wa[i][:, kc, :],
                                                             start=(kc == 0), stop=(kc == KC - 1)),
                          reads=[B_crep, B_wa[i]], writes=[B_pm[i]], signal=(kc == KC - 1))
                mk.op(DVE, lambda e, i=i, nb=nb: e.tensor_tensor(out=modb[:, nb * 512:(nb + 1) * 512], in0=pm[i][:],
                                                                   in1=brow[:, nb * 512:(nb + 1) * 512], op=ALU.add),
                      reads=[B_pm[i], B_brow], writes=[B_modb])
            mk.op(DVE, lambda e: e.tensor_scalar(out=nrow[:], in0=nrow[:], scalar1=float(D) ** 0.5, scalar2=None,
                                                 op0=ALU.mult), writes=[B_nrow])
            mk.op(DVE, lambda e: e.scalar_tensor_tensor(out=modb[:, D:2 * D], in0=modb[:, D:2 * D], scalar=1.0,
                                                         in1=nrow[:, 0:D], op0=ALU.add, op1=ALU.mult),
                  reads=[B_nrow], writes=[B_modb])
            mk.op(DVE, lambda e: e.scalar_tensor_tensor(out=modb[:, 4 * D:5 * D], in0=modb[:, 4 * D:5 * D], scalar=1.0,
                                                         in1=nrow[:, D:2 * D], op0=ALU.add, op1=ALU.mult),
                  reads=[B_nrow], writes=[B_modb])
            mk.dma(SP, st_m, mod_d[:, :], modb[:], reads=[B_modb], writes=[B_mod])
            mk.flush()
        ck(f"mod{l}")

        with ExitStack() as ph:
            hT = sb("hT", [128, KC, S], BF16, stack=ph)
            B_hT = Buf("hT")
            m1 = sb("m1", [128, 2 * D], stack=ph)
            B_m1 = Buf("m1")
            st_l = mk.stream("p1l")
            st_s = mk.stream("p1s")
            st_wc = mk.stream("p1w")
            mk.dma(SP, "p1m", m1[:], mod_d[:, 0:2 * D], reads=[B_mod], writes=[B_m1])
            with ExitStack() as p1a:
                xt = [sb(f"xt{i}", [128, D], stack=p1a) for i in range(2)]
                B_xt = [Buf(f"xt{i}") for i in range(2)]
                junk = sb("junk", [128, D], stack=p1a)
                B_junk = Buf("junk")
                h1 = sb("h1", [128, D], stack=p1a)
                B_h1 = Buf("h1")
                hb = [sb(f"hb{i}", [128, D], BF16, stack=p1a) for i in range(2)]
                B_hb = [Buf(f"hb{i}") for i in range(2)]
                ssq = sb("ssq", [128, 2], stack=p1a)
                B_ssq = Buf("ssq")
                ptr = [ps(f"ptr{i}", [128, KC, 128], BF16, stack=p1a) for i in range(2)]
                B_ptr = [Buf(f"ptr{i}") for i in range(2)]
                for t in range(NT):
                    i = t % 2
                    mk.dma(SP, f"p1x{i}", xt[i][:], x_src[t * 128:(t + 1) * 128, :], reads=[B_xs[t]] if l > 0 else [],
                           writes=[B_xt[i]])
                    mk.op(ACT, lambda e, i=i: e.activation(out=junk[:], in_=xt[i][:], func=AF.Square,
                                                           accum_out=ssq[:, 0:1]),
                          reads=[B_xt[i]], writes=[B_junk, B_ssq])
                    mk.op(ACT, lambda e: e.activation(out=ssq[:, 1:2], in_=ssq[:, 0:1], func=AF.Sqrt, bias=EPS * D),
                          reads=[B_ssq], writes=[B_ssq])
                    mk.op(DVE, lambda e: e.reciprocal(out=ssq[:, 1:2], in_=ssq[:, 1:2]), reads=[B_ssq], writes=[B_ssq])
                    mk.op(DVE, lambda e, i=i: e.scalar_tensor_tensor(out=h1[:], in0=xt[i][:], scalar=ssq[:, 1:2],
                                                                     in1=m1[:, D:2 * D], op0=ALU.mult, op1=ALU.mult),
                          reads=[B_xt[i], B_ssq, B_m1], writes=[B_h1])
                    mk.op(POOL, lambda e, i=i: e.tensor_tensor(out=hb[i][:], in0=h1[:], in1=m1[:, 0:D], op=ALU.add),
                          reads=[B_h1, B_m1], writes=[B_hb[i]])
                    for kc in range(KC):
                        mk.op(PE, lambda e, i=i, kc=kc: e.transpose(ptr[i][:, kc, :], hb[i][:, kc * 128:(kc + 1) * 128],
                                                                    ident_bf[:]),
                              reads=[B_hb[i], B_identbf], writes=[B_ptr[i]], signal=(kc == KC - 1))
                    mk.op(ACT, lambda e, i=i, t=t: e.activation(out=hT[:, :, t * 128:(t + 1) * 128], in_=ptr[i][:],
                                                                func=AF.Copy),
                          reads=[B_ptr[i]], writes=[B_hT])
                mk.flush()
            winv = win_in[l].rearrange("(kc p) n -> p kc n", p=128)
            with ExitStack() as p1b:
                wb = [sb(f"wb{i}", [128, KC, 128], BF16, stack=p1b) for i in range(2)]
                B_wb = [Buf(f"wb{i}") for i in range(2)]
                pz = [ps(f"pz{i}", [128, 512], stack=p1b) for i in range(3)]
                B_pz = [Buf(f"pz{i}") for i in range(3)]
                pn = [ps(f"pn{i}", [128, 512], stack=p1b) for i in range(2)]
                B_pn = [Buf(f"pn{i}") for i in range(2)]
                xc = [sb(f"xc{i}", [128, 515], stack=p1b) for i in range(2)]
                B_xc = [Buf(f"xc{i}") for i in range(2)]
                yc = [sb(f"yc{i}", [128, 512], stack=p1b) for i in range(2)]
                B_yc = [Buf(f"yc{i}") for i in range(2)]
                sq = [sb(f"sq{i}", [128, 512], stack=p1b) for i in range(2)]
                B_sq = [Buf(f"sq{i}") for i in range(2)]
                rn = [sb(f"rn{i}", [128, 512], stack=p1b) for i in range(2)]
                B_rn = [Buf(f"rn{i}") for i in range(2)]
                NST = min(S, 2048)
                stg = [sb(f"stg{i}", [128, NST], stack=p1b) for i in range(2)]
                B_stg = [Buf(f"stg{i}") for i in range(2)]
                stgb = [sb(f"stgb{i}", [128, NST], BF16, stack=p1b) for i in range(2)]
                B_stgb = [Buf(f"stgb{i}") for i in range(2)]
                nchunk = S // 512
                cpst = NST // 512
                sidx = 0
                wi = 0
                pzi = 0
                for g in range(24 + 16):
                    gdn = g < 24
                    col0 = g * 128 if gdn else 4112 + (g - 24) * 128
                    i = wi % 2
                    wi += 1
                    mk.dma(POOL, f"p1w{i}", wb[i][:], winv[:, :, col0:col0 + 128], writes=[B_wb[i]])
                    for c in range(nchunk):
                        pi = pzi % 3
                        pzi += 1
                        for kc in range(KC):
                            mk.op(PE, lambda e, i=i, kc=kc, c=c, pi=pi: e.matmul(
                                pz[pi][:], lhsT=wb[i][:, kc, :], rhs=hT[:, kc, c * 512:(c + 1) * 512],
                                start=(kc == 0), stop=(kc == KC - 1)),
                                reads=[B_wb[i], B_hT], writes=[B_pz[pi]], signal=(kc == KC - 1))
                        sl = (sidx // cpst) % 2
                        so = (sidx % cpst) * 512
                        if gdn:
                            j = c % 2
                            if c == 0:
                                mk.op(POOL, lambda e, j=j: e.memset(xc[j][:, 0:3], 0.0), writes=[B_xc[j]])
                            mk.op(ACT, lambda e, j=j, pi=pi: e.activation(out=xc[j][:, 3:515], in_=pz[pi][:], func=AF.Copy),
                                  reads=[B_pz[pi]], writes=[B_xc[j]])
                            if c + 1 < nchunk:
                                mk.op(POOL, lambda e, j=j: e.tensor_copy(out=xc[1 - j][:, 0:3], in_=xc[j][:, 512:515]),
                                      reads=[B_xc[j]], writes=[B_xc[1 - j]])
                            CE = DVE if (c % 2 == 0) else POOL
                            cwb = l * 96 + g * 4
                            mk.op(CE, lambda e, j=j, cwb=cwb: e.tensor_scalar(
                                out=yc[j][:], in0=xc[j][:, 0:512], scalar1=cw_sb[:, cwb:cwb + 1], scalar2=None,
                                op0=ALU.mult), reads=[B_xc[j], B_cw], writes=[B_yc[j]])
                            for tp in range(1, 4):
                                if CE is DVE:
                                    mk.op(CE, lambda e, j=j, cwb=cwb, tp=tp: e.scalar_tensor_tensor(
                                        out=yc[j][:], in0=xc[j][:, tp:tp + 512], scalar=cw_sb[:, cwb + tp:cwb + tp + 1],
                                        in1=yc[j][:], op0=ALU.mult, op1=ALU.add),
                                        reads=[B_xc[j], B_cw], writes=[B_yc[j]])
                                else:
                                    mk.op(CE, lambda e, j=j, cwb=cwb, tp=tp: e.tensor_scalar(
                                        out=sq[j][:], in0=xc[j][:, tp:tp + 512], scalar1=cw_sb[:, cwb + tp:cwb + tp + 1],
                                        scalar2=None, op0=ALU.mult), reads=[B_xc[j], B_cw], writes=[B_sq[j]])
                                    mk.op(CE, lambda e, j=j: e.tensor_tensor(out=yc[j][:], in0=yc[j][:], in1=sq[j][:],
                                                                            op=ALU.add),
                                          reads=[B_sq[j]], writes=[B_yc[j]])
                            if g >= 16:
                                mk.op(ACT, lambda e, j=j, sl=sl, so=so: e.activation(
                                    out=stg[sl][:, so:so + 512], in_=yc[j][:], func=AF.Silu),
                                    reads=[B_yc[j]], writes=[B_stg[sl]])
                            else:
                                mk.op(ACT, lambda e, j=j: e.activation(out=yc[j][:], in_=yc[j][:], func=AF.Silu),
                                      reads=[], writes=[B_yc[j]])
                                isq = g < 8
                                s_in = (float(HD) ** 0.5) if isq else 1.0
                                eps2 = EPS * HD if isq else EPS
                                mk.op(ACT, lambda e, j=j, s_in=s_in: e.activation(out=sq[j][:], in_=yc[j][:], func=AF.Square,
                                                                                  scale=s_in),
                                      reads=[B_yc[j]], writes=[B_sq[j]])
                                mk.op(PE, lambda e, j=j: e.matmul(pn[j][:], lhsT=ones, rhs=sq[j][:], start=True, stop=True),
                                      reads=[B_sq[j], B_consts], writes=[B_pn[j]])
                                mk.op(ACT, lambda e, j=j, eps2=eps2: e.activation(out=rn[j][:], in_=pn[j][:], func=AF.Sqrt,
                                                                                  bias=eps2),
                                      reads=[B_pn[j]], writes=[B_rn[j]])
                                mk.op(DVE, lambda e, j=j: e.reciprocal(out=rn[j][:], in_=rn[j][:]), writes=[B_rn[j]])
                                mk.op(POOL, lambda e, j=j, sl=sl, so=so: e.tensor_tensor(
                                    out=stg[sl][:, so:so + 512], in0=yc[j][:], in1=rn[j][:], op=ALU.mult),
                                    reads=[B_yc[j], B_rn[j]], writes=[B_stg[sl]])
                        else:
                            mk.op(ACT, lambda e, pi=pi, sl=sl, so=so: e.activation(
                                out=stgb[sl][:, so:so + 512], in_=pz[pi][:], func=AF.Copy),
                                reads=[B_pz[pi]], writes=[B_stgb[sl]])
                        sidx += 1
                        if sidx % cpst == 0:
                            t0 = (c + 1) * 512 - NST
                            if gdn:
                                mk.dma(SP, f"p1s{sl}", zg_d[g * 128:(g + 1) * 128, t0:t0 + NST], stg[sl][:],
                                       reads=[B_stg[sl]], writes=[B_zg])
                            else:
                                mk.dma(SP, f"p1sb{sl}", zqk_d[(g - 24) * 128:(g - 23) * 128, t0:t0 + NST], stgb[sl][:],
                                       reads=[B_stgb[sl]], writes=[B_zqk])
                mk.flush()
            with ExitStack() as p1c:
                wt = [sb(f"wt{i}", [128, KC, 512], BF16, stack=p1c) for i in range(2)]
                B_wt = [Buf(f"wt{i}") for i in range(2)]
                pz = [ps(f"pzt{i}", [128, 512], stack=p1c) for i in range(3)]
                B_pz = [Buf(f"pzt{i}") for i in range(3)]
                stt = [sb(f"stt{i}", [128, 4, 512], stack=p1c) for i in range(2)]
                B_stt = [Buf(f"stt{i}") for i in range(2)]
                sttb = [sb(f"sttb{i}", [128, 4, 512], BF16, stack=p1c) for i in range(2)]
                B_sttb = [Buf(f"sttb{i}") for i in range(2)]
                blocks = [(3072, 512, AF.Silu, 0, 0), (3584, 512, AF.Silu, 0, 512), (4096, 16, AF.Copy, 0, 1024),
                          (6160, 512, AF.Copy, 1, 0), (6672, 512, AF.Copy, 1, 512)]
                for q in range(4):
                    blocks.append((7184 + q * 512, 512, AF.Sigmoid, 0, 1040 + q * 512))
                pzi = 0
                sidx = 0
                ztv = ztok_d.rearrange("(n p) c -> p n c", p=128)
                vav = va_d.rearrange("(n p) c -> p n c", p=128)
                for bi, (col0, ncol, func, dest, dcol) in enumerate(blocks):
                    i = bi % 2
                    mk.dma(POOL, f"p1wt{i}", wt[i][:, :, 0:ncol], winv[:, :, col0:col0 + ncol], writes=[B_wt[i]])
                    for t in range(NT):
                        pi = pzi % 3
                        pzi += 1
                        for kc in range(KC):
                            mk.op(PE, lambda e, i=i, kc=kc, t=t, pi=pi, ncol=ncol: e.matmul(
                                pz[pi][:, 0:ncol], lhsT=hT[:, kc, t * 128:(t + 1) * 128], rhs=wt[i][:, kc, 0:ncol],
                                start=(kc == 0), stop=(kc == KC - 1)),
                                reads=[B_wt[i], B_hT], writes=[B_pz[pi]], signal=(kc == KC - 1))
                        sl = (sidx // 4) % 2
                        so = sidx % 4
                        sidx += 1
                        if dest == 0:
                            mk.op(ACT, lambda e, pi=pi, sl=sl, so=so, ncol=ncol, func=func: e.activation(
                                out=stt[sl][:, so, 0:ncol], in_=pz[pi][:, 0:ncol], func=func),
                                reads=[B_pz[pi]], writes=[B_stt[sl]])
                        else:
                            mk.op(ACT, lambda e, pi=pi, sl=sl, so=so, ncol=ncol, func=func: e.activation(
                                out=sttb[sl][:, so, 0:ncol], in_=pz[pi][:, 0:ncol], func=func),
                                reads=[B_pz[pi]], writes=[B_sttb[sl]])
                        if sidx % 4 == 0:
                            n0 = t - 3
                            if dest == 0:
                                mk.dma(SP, f"p1st{sl}", ztv[:, n0:n0 + 4, dcol:dcol + ncol], stt[sl][:, :, 0:ncol],
                                       reads=[B_stt[sl]], writes=[B_ztok])
                            else:
                                mk.dma(SP, f"p1stb{sl}", vav[:, n0:n0 + 4, dcol:dcol + ncol], sttb[sl][:, :, 0:ncol],
                                       reads=[B_sttb[sl]], writes=[B_va])
                mk.flush()
        ck(f"p1{l}")

        with ExitStack() as ph:
            p2_mixers(nc, mk, ph, l, S, NT, dict(
                consts=consts, B_consts=B_consts, ident_bf=ident_bf, B_identbf=B_identbf,
                zg_d=zg_d, zqk_d=zqk_d, ztok_d=ztok_d, va_d=va_d, m_d=m_d,
                B_zg=B_zg, B_zqk=B_zqk, B_ztok=B_ztok, B_va=B_va, B_m=B_m,
                alog_in=alog_in, dtb_in=dtb_in, gnorm_in=gnorm_in, biasT_in=biasT_in))
            mk.flush()
        ck(f"p2{l}")

        with ExitStack() as ph:
            wo = sb("wo", [128, KC, D], BF16, stack=ph)
            B_wo = Buf("wo")
            gt1 = sb("gt1", [128, D], stack=ph)
            B_gt1 = Buf("gt1")
            st_l = mk.stream("p3al")
            st_s = mk.stream("p3as")
            st_wc = mk.stream("p3aw")
            wov = wout_in[l].rearrange("(kc p) n -> p kc n", p=128)
            for hf in range(2):
                mk.dma(POOL, f"p3aw{hf}", wo[:, :, hf * 512:(hf + 1) * 512], wov[:, :, hf * 512:(hf + 1) * 512], writes=[B_wo])
            mk.dma(SP, "p3ag", gt1[:], mod_d[:, 2 * D:3 * D], reads=[B_mod], writes=[B_gt1])
            mt = [sb(f"mt{i}", [128, D], BF16, stack=ph) for i in range(2)]
            B_mt = [Buf(f"mt{i}") for i in range(2)]
            xt = [sb(f"xa{i}", [128, D], stack=ph) for i in range(2)]
            B_xt = [Buf(f"xa{i}") for i in range(2)]
            mT = [sb(f"mT{i}", [128, KC, 128], BF16, stack=ph) for i in range(2)]
            B_mT = [Buf(f"mT{i}") for i in range(2)]
            ptr = [ps(f"ptra{i}", [128, KC, 128], BF16, stack=ph) for i in range(2)]
            B_ptr = [Buf(f"ptra{i}") for i in range(2)]
            py = [ps(f"pya{i}", [128, 512], stack=ph) for i in range(4)]
            B_py = [Buf(f"pya{i}") for i in range(4)]
            tmp = [sb(f"tmpa{i}", [128, D], stack=ph) for i in range(2)]
            B_tmp = [Buf(f"tmpa{i}") for i in range(2)]
            xo = [sb(f"xoa{i}", [128, D], stack=ph) for i in range(2)]
            B_xo = [Buf(f"xoa{i}") for i in range(2)]
            for t in range(NT):
                i = t % 2
                mk.dma(SP, f"p3am{i}", mt[i][:], m_d[t * 128:(t + 1) * 128, :], reads=[B_m], writes=[B_mt[i]])
                mk.dma(SP, f"p3ax{i}", xt[i][:], x_src[t * 128:(t + 1) * 128, :], reads=[B_xs[t]] if l > 0 else [],
                       writes=[B_xt[i]])
                for kc in range(KC):
                    mk.op(PE, lambda e, i=i, kc=kc: e.transpose(ptr[i][:, kc, :], mt[i][:, kc * 128:(kc + 1) * 128], ident_bf[:]),
                          reads=[B_mt[i], B_identbf], writes=[B_ptr[i]], signal=(kc == KC - 1))
                mk.op(ACT, lambda e, i=i: e.activation(out=mT[i][:], in_=ptr[i][:], func=AF.Copy),
                      reads=[B_ptr[i]], writes=[B_mT[i]])
                for hf in range(2):
                    pi = i * 2 + hf
                    for kc in range(KC):
                        mk.op(PE, lambda e, i=i, kc=kc, hf=hf, pi=pi: e.matmul(
                            py[pi][:], lhsT=mT[i][:, kc, :], rhs=wo[:, kc, hf * 512:(hf + 1) * 512],
                            start=(kc == 0), stop=(kc == KC - 1)),
                            reads=[B_mT[i], B_wo], writes=[B_py[pi]], signal=(kc == KC - 1))
                    mk.op(DVE, lambda e, i=i, hf=hf, pi=pi: e.tensor_tensor(
                        out=tmp[i][:, hf * 512:(hf + 1) * 512], in0=py[pi][:], in1=gt1[:, hf * 512:(hf + 1) * 512],
                        op=ALU.mult), reads=[B_py[pi], B_gt1], writes=[B_tmp[i]])
                mk.op(POOL, lambda e, i=i: e.tensor_tensor(out=xo[i][:], in0=tmp[i][:], in1=xt[i][:], op=ALU.add),
                      reads=[B_tmp[i], B_xt[i]], writes=[B_xo[i]])
                mk.dma(SP, f"p3as{i}", xs_d[t * 128:(t + 1) * 128, :], xo[i][:], reads=[B_xo[i]], writes=[B_xs[t]])
            mk.flush()
        ck(f"p3a{l}")

        with ExitStack() as ph:
            last = (l == L - 1)
            w1 = sb("w1", [128, KC, DFF], BF16, stack=ph)
            w2 = sb("w2", [128, 32, D], BF16, stack=ph)
            B_w1, B_w2 = Buf("w1"), Buf("w2")
            m2 = sb("m2", [128, 3 * D], stack=ph)
            B_m2 = Buf("m2")
            st_l = mk.stream("p3bl")
            st_s = mk.stream("p3bs")
            st_wc = mk.stream("p3bw")
            w1v = w1_in[l].rearrange("(kc p) n -> p kc n", p=128)
            w2v = w2_in[l].rearrange("(fc p) n -> p fc n", p=128)
            for q in range(8):
                mk.dma(POOL, f"p3bw{q % 4}", w1[:, :, q * 512:(q + 1) * 512], w1v[:, :, q * 512:(q + 1) * 512], writes=[B_w1])
            for q in range(8):
                mk.dma(POOL, f"p3bw{q % 4}", w2[:, q * 4:(q + 1) * 4, :], w2v[:, q * 4:(q + 1) * 4, :], writes=[B_w2])
            mk.dma(SP, "p3bm", m2[:], mod_d[:, 3 * D:6 * D], reads=[B_mod], writes=[B_m2])
            if last:
                fn = sb("fnrm", [128, D], stack=ph)
                B_fn = Buf("fn")
                mk.dma(SP, "p3bf", fn[:], fnorm_in[0:1, :].partition_broadcast(128), writes=[B_fn])
                mk.op(DVE, lambda e: e.tensor_scalar(out=fn[:], in0=fn[:], scalar1=float(D) ** 0.5, scalar2=None,
                                                     op0=ALU.mult), writes=[B_fn])
            xt = [sb(f"xb{i}", [128, D], stack=ph) for i in range(2)]
            B_xt = [Buf(f"xb{i}") for i in range(2)]
            h1 = sb("h1b", [128, D], stack=ph)
            B_h1 = Buf("h1b")
            junk, B_junk = h1, B_h1
            hb = [sb(f"hbb{i}", [128, D], BF16, stack=ph) for i in range(2)]
            B_hb = [Buf(f"hbb{i}") for i in range(2)]
            hT2 = [sb(f"hT2{i}", [128, KC, 128], BF16, stack=ph) for i in range(2)]
            B_hT2 = [Buf(f"hT2{i}") for i in range(2)]
            ssq = sb("ssqb", [128, 4], stack=ph)
            B_ssq = Buf("ssqb")
            ptr = [ps(f"ptrb{i}", [128, KC, 128], BF16, stack=ph) for i in range(1)]
            B_ptr = [Buf(f"ptrb{i}") for i in range(1)]
            pa = [ps(f"pa{i}", [128, 4, 128], stack=ph) for i in range(3)]
            B_pa = [Buf(f"pa{i}") for i in range(3)]
            py = [ps(f"pyb{i}", [128, 512], stack=ph) for i in range(4)]
            B_py = [Buf(f"pyb{i}") for i in range(4)]
            rl = [sb(f"rl{i}", [128, 4, 128], stack=ph) for i in range(2)]
            B_rl = [Buf(f"rl{i}") for i in range(2)]
            aT = [sb(f"aT{i}", [128, 32, 128], BF16, stack=ph) for i in range(2)]
            B_aT = [Buf(f"aT{i}") for i in range(2)]
            tmp = [sb(f"tmpb{i}", [128, D], stack=ph) for i in range(2)]
            B_tmp = [Buf(f"tmpb{i}") for i in range(2)]
            xo, B_xo = xt, B_xt
            pai = 0
            for t in range(NT):
                i = t % 2
                mk.dma(SP, f"p3bx{i}", xt[i][:], xs_d[t * 128:(t + 1) * 128, :], reads=[B_xs[t]], writes=[B_xt[i]])
                mk.op(ACT, lambda e, i=i: e.activation(out=junk[:], in_=xt[i][:], func=AF.Square, accum_out=ssq[:, 0:1]),
                      reads=[B_xt[i]], writes=[B_junk, B_ssq])
                mk.op(ACT, lambda e: e.activation(out=ssq[:, 1:2], in_=ssq[:, 0:1], func=AF.Sqrt, bias=EPS * D),
                      reads=[B_ssq], writes=[B_ssq])
                mk.op(DVE, lambda e: e.reciprocal(out=ssq[:, 1:2], in_=ssq[:, 1:2]), reads=[B_ssq], writes=[B_ssq])
                mk.op(DVE, lambda e, i=i: e.scalar_tensor_tensor(out=h1[:], in0=xt[i][:], scalar=ssq[:, 1:2],
                                                                 in1=m2[:, D:2 * D], op0=ALU.mult, op1=ALU.mult),
                      reads=[B_xt[i], B_ssq, B_m2], writes=[B_h1])
                mk.op(POOL, lambda e, i=i: e.tensor_tensor(out=hb[i][:], in0=h1[:], in1=m2[:, 0:D], op=ALU.add),
                      reads=[B_h1, B_m2], writes=[B_hb[i]])
                for kc in range(KC):
                    mk.op(PE, lambda e, i=i, kc=kc: e.transpose(ptr[0][:, kc, :], hb[i][:, kc * 128:(kc + 1) * 128], ident_bf[:]),
                          reads=[B_hb[i], B_identbf], writes=[B_ptr[0]], signal=(kc == KC - 1))
                mk.op(ACT, lambda e, i=i: e.activation(out=hT2[i][:], in_=ptr[0][:], func=AF.Copy),
                      reads=[B_ptr[0]], writes=[B_hT2[i]])
                for fq in range(8):
                    pi = pai % 3
                    ri = pai % 2
                    pai += 1
                    for f4 in range(4):
                        fb = fq * 4 + f4
                        for kc in range(KC):
                            mk.op(PE, lambda e, i=i, kc=kc, fb=fb, f4=f4, pi=pi: e.matmul(
                                pa[pi][:, f4, :], lhsT=w1[:, kc, fb * 128:(fb + 1) * 128], rhs=hT2[i][:, kc, :],
                                start=(kc == 0), stop=(kc == KC - 1)),
                                reads=[B_w1, B_hT2[i]], writes=[B_pa[pi]], signal=(kc == KC - 1 and f4 == 3))
                    mk.op(ACT, lambda e, pi=pi, ri=ri: e.activation(out=rl[ri][:], in_=pa[pi][:], func=AF.Relu),
                          reads=[B_pa[pi]], writes=[B_rl[ri]])
                    mk.op(POOL, lambda e, i=i, ri=ri, fq=fq: e.tensor_tensor(
                        out=aT[i][:, fq * 4:(fq + 1) * 4, :], in0=rl[ri][:], in1=rl[ri][:], op=ALU.mult),
                        reads=[B_rl[ri]], writes=[B_aT[i]])
                for hf in range(2):
                    pi = i * 2 + hf
                    for fc in range(32):
                        mk.op(PE, lambda e, i=i, fc=fc, hf=hf, pi=pi: e.matmul(
                            py[pi][:], lhsT=aT[i][:, fc, :], rhs=w2[:, fc, hf * 512:(hf + 1) * 512],
                            start=(fc == 0), stop=(fc == 31)),
                            reads=[B_aT[i], B_w2], writes=[B_py[pi]], signal=(fc == 31))
                    mk.op(DVE, lambda e, i=i, hf=hf, pi=pi: e.tensor_tensor(
                        out=tmp[i][:, hf * 512:(hf + 1) * 512], in0=py[pi][:], in1=m2[:, 2 * D + hf * 512:2 * D + (hf + 1) * 512],
                        op=ALU.mult), reads=[B_py[pi], B_m2], writes=[B_tmp[i]])
                mk.op(POOL, lambda e, i=i: e.tensor_tensor(out=xo[i][:], in0=tmp[i][:], in1=xt[i][:], op=ALU.add),
                      reads=[B_tmp[i], B_xt[i]], writes=[B_xo[i]])
                if not last:
                    mk.dma(SP, f"p3bs{i}", xs_d[t * 128:(t + 1) * 128, :], xo[i][:], reads=[B_xo[i]], writes=[B_xs[t]])
                else:
                    mk.op(ACT, lambda e, i=i: e.activation(out=junk[:], in_=xo[i][:], func=AF.Square, accum_out=ssq[:, 2:3]),
                          reads=[B_xo[i]], writes=[B_junk, B_ssq])
                    mk.op(ACT, lambda e: e.activation(out=ssq[:, 3:4], in_=ssq[:, 2:3], func=AF.Sqrt, bias=EPS * D),
                          reads=[B_ssq], writes=[B_ssq])
                    mk.op(DVE, lambda e: e.reciprocal(out=ssq[:, 3:4], in_=ssq[:, 3:4]), reads=[B_ssq], writes=[B_ssq])
                    mk.op(DVE, lambda e, i=i: e.scalar_tensor_tensor(out=tmp[i][:], in0=xo[i][:], scalar=ssq[:, 3:4],
                                                                     in1=fn[:], op0=ALU.mult, op1=ALU.mult),
                          reads=[B_xo[i], B_ssq, B_fn], writes=[B_tmp[i]])
                    mk.dma(SP, f"p3bo{i}", out_d[t * 128:(t + 1) * 128, :], tmp[i][:], reads=[B_tmp[i]], writes=[B_out])
            if last:
                mk.final_wait(SP, [B_out])
            mk.flush()
            ck(f"p3b{l}")


def p2_mixers(nc, mk, ph, l, S, NT, g):
    PE, ACT, DVE, POOL, SP = mk.PE, mk.ACT, mk.DVE, mk.POOL, mk.SP
    consts, B_consts = g["consts"], g["B_consts"]
    ident_bf, B_identbf = g["ident_bf"], g["B_identbf"]
    ident = consts[:, C_ID:C_ID + 128]
    ones = consts[:, C_ONE:C_ONE + 128]
    maskA = consts[:, C_MA:C_MA + 128]
    strict = consts[:, C_ST:C_ST + 128]
    tri = consts[:, C_TRI:C_TRI + 128]
    sel0 = consts[:, C_S0:C_S0 + 128]
    sel1 = consts[:, C_S1:C_S1 + 128]

    def sb(name, shape, dt=F32):
        mk.uid += 1
        return ph.enter_context(nc.sbuf_tensor(f"{name}_u{mk.uid}", list(shape), dt))

    def ps(name, shape, dt=F32):
        mk.uid += 1
        return ph.enter_context(nc.psum_tensor(f"{name}_u{mk.uid}", list(shape), dt))

    st_l = mk.stream("p2l")
    st_s = mk.stream("p2s")
    st_k = mk.stream("p2k")
    expB = sb("expB", [128, H * 640])
    B_expB = Buf("expB")
    mk.dma(SP, st_k, expB[:], g["biasT_in"][l], writes=[B_expB])
    mk.op(ACT, lambda e: e.activation(out=expB[:], in_=expB[:], func=AF.Exp), writes=[B_expB])
    for h in range(H):
        mk.op(POOL, lambda e, h=h: e.tensor_tensor(out=expB[:, h * 640:(h + 1) * 640], in0=expB[:, h * 640:(h + 1) * 640],
                                                     in1=consts[:, C_AM:C_AM + 640], op=ALU.mult),
              reads=[B_consts], writes=[B_expB])
    hv = sb("hv", [128, 3 * H + 128])
    B_hv = Buf("hv")
    mk.dma(SP, st_k, hv[:, 0:H], g["dtb_in"][0:1, l * H:(l + 1) * H].partition_broadcast(128), writes=[B_hv])
    mk.dma(SP, st_k, hv[:, H:2 * H], g["alog_in"][0:1, l * H:(l + 1) * H].partition_broadcast(128), writes=[B_hv])
    mk.dma(SP, st_k, hv[:, 3 * H:3 * H + 128], g["gnorm_in"][0:1, l * HD:(l + 1) * HD].partition_broadcast(128),
           writes=[B_hv])
    mk.op(ACT, lambda e: e.activation(out=hv[:, 2 * H:3 * H], in_=hv[:, H:2 * H], func=AF.Exp), writes=[B_hv])
    mk.op(DVE, lambda e: e.tensor_scalar(out=hv[:, H:2 * H], in0=hv[:, 2 * H:3 * H], scalar1=-1.0, scalar2=None,
                                         op0=ALU.mult), writes=[B_hv])
    mk.op(DVE, lambda e: e.tensor_scalar(out=hv[:, 3 * H:3 * H + 128], in0=hv[:, 3 * H:3 * H + 128], scalar1=float(HD) ** 0.5,
                                         scalar2=None, op0=ALU.mult), writes=[B_hv])
    dtb = hv[:, 0:H]
    nA = hv[:, H:2 * H]
    gn = hv[:, 3 * H:3 * H + 128]

    zg = [sb(f"zg{i}", [128, 24, 128]) for i in range(2)]
    B_zgt = [Buf(f"zgt{i}") for i in range(2)]
    zq0 = sb("zq0", [128, H, 128], BF16)
    zq = [zq0, zq0]
    B_zq0 = Buf("zq0")
    B_zq = [B_zq0, B_zq0]
    kring = sb("kring", [128, 5, H, 128], BF16)
    B_kr = [Buf(f"kr{i}") for i in range(5)]
    vring = sb("vring", [128, 5, H, 132], BF16)
    B_vr = [Buf(f"vr{i}") for i in range(5)]
    zt = [sb(f"zt{i}", [128, 3088]) for i in range(2)]
    B_zt = [Buf(f"zt{i}") for i in range(2)]
    mtile = [sb(f"mtile{i}", [128, D], BF16) for i in range(2)]
    B_mtile = [Buf(f"mtile{i}") for i in range(2)]
    for s in range(5):
        mk.op(POOL, lambda e, s=s: e.memset(vring[:, s, :, 128:132], 1.0), writes=[B_vr[s]])

    gsc = [sb(f"gsc{i}", [128, 12 * H]) for i in range(2)]
    B_gsc = [Buf(f"gsc{i}") for i in range(2)]
    Gs = [sb(f"Gs{i}", [128, 3 * H]) for i in range(2)]
    B_Gs = [Buf(f"Gs{i}") for i in range(2)]
    pg = ps("pg", [128, 512])
    B_pgG = B_pgO = B_pgS = Buf("pg")
    psA = ps("psA", [128, 512])
    B_psA = Buf("psA")
    NQ = 12
    pq_t = [ps(f"pq{i}", [128, 4, 128]) for i in range(NQ // 4)]
    NBK = NQ // 4
    pq = [pq_t[i % NBK][:, (i // NBK) % 4, :] for i in range(NQ)]
    B_bank = [Buf(f"pqb{i}") for i in range(NBK)]
    B_pq = [B_bank[i % NBK] for i in range(NQ)]
    state = {"q": 0}

    def getq():
        i = state["q"] % NQ
        state["q"] += 1
        return pq[i], B_pq[i]

    qkb0 = sb("qkb0", [128, 24, 128], BF16)
    qkb = [qkb0, qkb0]
    B_qkb0 = Buf("qkb0")
    B_qkb = [B_qkb0, B_qkb0]
    NW = 1
    def hb(name, shape, dt=F32):
        t = [sb(f"{name}{i}", [128, H] + list(shape), dt) for i in range(NW)]
        b = [[Buf(f"{name}{i}_{h}") for h in range(H)] for i in range(NW)]
        return t, b
    grep_r = [sb(f"grepr{i}", [128, 128]) for i in range(2)]
    B_grep_r = [Buf(f"grepr{i}") for i in range(2)]
    eGr_r = [sb(f"eGrr{i}", [128, 128]) for i in range(2)]
    B_eGr_r = [Buf(f"eGrr{i}") for i in range(2)]
    egk, B_egk = hb("egk", [128], BF16)
    kdec, B_kdec = hb("kdec", [128], BF16)
    vb, B_vb = hb("vb", [128], BF16)
    tG, B_tG = hb("tG", [128])
    DT, B_DT = hb("DT", [128])
    attnT, B_attnT = hb("attnT", [128], BF16)
    X0, B_X0 = tG, B_tG
    Xa, B_Xa = hb("Xa", [128], BF16)
    XTa, B_XTa = hb("XTa", [128], BF16)
    Da, B_Da = hb("Da", [128], BF16)
    Db, B_Db = hb("Db", [128], BF16)
    DTa, B_DTa = hb("DTa", [128], BF16)
    DTb, B_DTb = hb("DTb", [128], BF16)
    C1m, B_C1m = hb("C1m", [128], BF16)
    C2m, B_C2m = hb("C2m", [128], BF16)
    C1T, B_C1T = hb("C1T", [128], BF16)
    C2T, B_C2T = hb("C2T", [128], BF16)
    Pa, B_Pa = hb("Pa", [128], BF16)
    Pb, B_Pb = hb("Pb", [128], BF16)
    PTa, B_PTa = hb("PTa", [128], BF16)
    PTb, B_PTb = hb("PTb", [128], BF16)
    Yb, B_Yb = hb("Yb", [128], BF16)
    Ypb, B_Ypb = hb("Ypb", [128], BF16)
    usb, B_usb = hb("usb", [128])
    wTb, B_wTb = hb("wTb", [128], BF16)
    qdT, B_qdT = hb("qdT", [128], BF16)
    vn, B_vn = hb("vn", [128], BF16)
    GW, B_GW = hb("GW", [128])
    t1, B_t1 = hb("t1", [128])
    mb, B_mb = hb("mb", [128])
    esb = [sb(f"esb{i}", [128, 640]) for i in range(2)]
    B_esb = [Buf(f"esb{i}") for i in range(2)]
    PT = [sb(f"PT{i}", [128, 640], BF16) for i in range(2)]
    B_PT = [Buf(f"PT{i}") for i in range(2)]
    sm, B_sm = hb("sm", [4])
    S32 = sb("S32", [128, H, 128])
    B_S32 = [Buf(f"S32_{h}") for h in range(H)]
    Sb = sb("Sb", [128, H, 3, 128], BF16)
    B_Sb = [[Buf(f"Sb{h}_{s}") for s in range(3)] for h in range(H)]
    mk.op(POOL, lambda e: e.memset(S32[:], 0.0), writes=B_S32)
    mk.op(POOL, lambda e: e.memset(Sb[:], 0.0), writes=[b for r in B_Sb for b in r])
    ptk = ps("ptk", [128, 8, 128], BF16)
    ptv = ps("ptv", [128, 8, 128], BF16)
    B_ptk, B_ptv = Buf("ptk"), Buf("ptv")
    ptb = ps("ptb", [128, 8, 128], BF16)
    B_ptb1 = Buf("ptb")
    B_ptb = [B_ptb1 for h in range(H)]

    zgv = g["zg_d"].rearrange("(g d) s -> d g s", d=128)
    zqv = g["zqk_d"].rearrange("(g d) s -> d g s", d=128)
    vav = g["va_d"].rearrange("s (h d) -> s h d", d=128)

    def do_tile(t):
        i = t % 2
        w = 0
        sl = t % 5
        ts = slice(t * 128, (t + 1) * 128)
        mk.dma(SP, f"p2zg{i}", zg[i][:], zgv[:, :, ts], reads=[g["B_zg"]], writes=[B_zgt[i]])
        mk.dma(SP, "p2zq", zq[i][:], zqv[:, 0:H, ts], reads=[g["B_zqk"]], writes=[B_zq[i]])
        mk.dma(SP, f"p2kr{sl}", kring[:, sl, :, :], zqv[:, H:2 * H, ts], reads=[g["B_zqk"]], writes=[B_kr[sl]])
        mk.dma(SP, f"p2vr{sl}", vring[:, sl, :, 0:128], vav[ts, :, :], reads=[g["B_va"]], writes=[B_vr[sl]])
        mk.dma(SP, f"p2zt{i}", zt[i][:], g["ztok_d"][ts, :], reads=[g["B_ztok"]], writes=[B_zt[i]])
        if DBG["p2_stage"] <= 1:
            return
        gs = gsc[i]
        a_ap = zt[i][:, 1024:1024 + H]
        b_ap = zt[i][:, 1024 + H:1024 + 2 * H]
        bet, nbet, xa, ax, ee, lg, gg, eG, dl, kd = [gs[:, k * H:(k + 1) * H] for k in range(10)]
        cdb = gs[:, 10 * H:12 * H]
        BG = B_gsc[i]
        mk.op(ACT, lambda e: e.activation(out=bet, in_=b_ap, func=AF.Sigmoid), reads=[B_zt[i]], writes=[BG])
        mk.op(DVE, lambda e: e.tensor_scalar(out=nbet, in0=bet, scalar1=-1.0, scalar2=None, op0=ALU.mult), writes=[BG])
        mk.op(DVE, lambda e: e.tensor_tensor(out=xa, in0=a_ap, in1=dtb, op=ALU.add), reads=[B_zt[i], B_hv], writes=[BG])
        mk.op(ACT, lambda e: e.activation(out=ax, in_=xa, func=AF.Abs), writes=[BG])
        mk.op(ACT, lambda e: e.activation(out=ee, in_=ax, func=AF.Exp, scale=-1.0), writes=[BG])
        mk.op(ACT, lambda e: e.activation(out=lg, in_=ee, func=AF.Ln, bias=1.0), writes=[BG])
        mk.op(DVE, lambda e: e.scalar_tensor_tensor(out=lg, in0=xa, scalar=0.0, in1=lg, op0=ALU.max, op1=ALU.add),
              writes=[BG])
        mk.op(DVE, lambda e: e.tensor_tensor(out=gg, in0=lg, in1=nA, op=ALU.mult), reads=[B_hv], writes=[BG])
        pG = pg[:, 0:3 * H]
        if DBG.get("skipG"):
            mk.op(POOL, lambda e: e.memset(Gs[i][:], 0.0), writes=[B_Gs[i]])
        else:
            mk.op(PE, lambda e: e.matmul(pG[:, 0:H], lhsT=tri, rhs=gg, start=True, stop=True),
                  reads=[BG, B_consts], writes=[B_pgG], signal=False)
            mk.op(PE, lambda e: e.matmul(pG[:, H:2 * H], lhsT=sel0, rhs=gg, start=True, stop=True),
                  reads=[BG, B_consts], writes=[B_pgG], signal=False)
            mk.op(PE, lambda e: e.matmul(pG[:, 2 * H:3 * H], lhsT=sel1, rhs=gg, start=True, stop=True),
                  reads=[BG, B_consts], writes=[B_pgG])
            mk.op(ACT, lambda e: e.activation(out=Gs[i][:], in_=pG, func=AF.Copy), reads=[B_pgG], writes=[B_Gs[i]])
        Gc = Gs[i][:, 0:H]
        mk.op(ACT, lambda e: e.activation(out=eG, in_=Gc, func=AF.Exp), reads=[B_Gs[i]], writes=[BG])
        mk.op(DVE, lambda e: e.tensor_tensor(out=dl[0:64, :], in0=Gs[i][0:64, H:2 * H], in1=Gs[i][0:64, 0:H],
                                             op=ALU.subtract), reads=[B_Gs[i]], writes=[BG])
        mk.op(DVE, lambda e: e.tensor_tensor(out=dl[64:128, :], in0=Gs[i][64:128, 2 * H:3 * H], in1=Gs[i][64:128, 0:H],
                                             op=ALU.subtract), reads=[B_Gs[i]], writes=[BG])
        mk.op(ACT, lambda e: e.activation(out=kd, in_=dl, func=AF.Exp), writes=[BG])
        mk.op(ACT, lambda e: e.activation(out=cdb, in_=Gs[i][:, H:3 * H], func=AF.Exp), reads=[B_Gs[i]], writes=[BG])
        mk.op(ACT, lambda e: e.activation(out=qkb[i][:], in_=zg[i][:], func=AF.Copy),
              reads=[B_zgt[i]], writes=[B_qkb[i]])
        if DBG["p2_stage"] <= 2:
            return

        m_lo = max(0, t - 4)
        mis = [m - (t - 4) for m in range(m_lo, t + 1)]
        mlo = mis[0]
        sc = HD ** -0.5

        def s_att(h):
            e2 = (t * H + h) % 2
            for mi in mis:
                m = t - 4 + mi
                dst = psA[:, mi * 128:(mi + 1) * 128] if mi < 4 else pg[:, 384:512]
                Bd = B_psA if mi < 4 else B_pgS
                mk.op(PE, lambda e, m=m, dst=dst: e.matmul(dst, lhsT=kring[:, m % 5, h, :], rhs=zq[i][:, h, :],
                                                            start=True, stop=True),
                      reads=[B_kr[m % 5], B_zq[i]], writes=[Bd], signal=(mi == mis[-1] or mi == 3))
            if DBG["p2_stage"] <= 2.1:
                return
            if mlo < 4:
                mk.op(ACT, lambda e: e.activation(out=esb[e2][:, mlo * 128:512], in_=psA[:, mlo * 128:512],
                                                  func=AF.Exp, scale=sc),
                      reads=[B_psA], writes=[B_esb[e2]])
            mk.op(ACT, lambda e: e.activation(out=esb[e2][:, 512:640], in_=pg[:, 384:512], func=AF.Exp, scale=sc),
                  reads=[B_pgS], writes=[B_esb[e2]])
            if DBG["p2_stage"] <= 2.2:
                return
            mk.op(POOL, lambda e: e.tensor_tensor(
                out=PT[e2][:, mlo * 128:640], in0=esb[e2][:, mlo * 128:640],
                in1=expB[:, h * 640 + mlo * 128:(h + 1) * 640], op=ALU.mult),
                reads=[B_esb[e2], B_expB], writes=[B_PT[e2]])
            if DBG["p2_stage"] <= 2.3:
                return
            for mi in mis:
                m = t - 4 + mi
                mk.op(PE, lambda e, m=m, mi=mi: e.matmul(pg[:, 128:258], lhsT=PT[e2][:, mi * 128:(mi + 1) * 128],
                                                         rhs=vring[:, m % 5, h, 0:130],
                                                         start=(mi == mis[0]), stop=(mi == mis[-1])),
                      reads=[B_PT[e2], B_vr[m % 5]], writes=[B_pgO], signal=(mi == mis[-1]))
            if DBG["p2_stage"] <= 2.4:
                return
            mk.op(DVE, lambda e: e.reciprocal(out=sm[w][:, h, 0:1], in_=pg[:, 256:257]),
                  reads=[B_pgO], writes=[B_sm[w][h]])
            gb_ap = zt[i][:, 1040 + 1024 + h * 128:1040 + 1024 + (h + 1) * 128]
            mk.op(DVE, lambda e: e.scalar_tensor_tensor(
                out=mb[w][:, h, :], in0=pg[:, 128:256], scalar=sm[w][:, h, 0:1], in1=gb_ap,
                op0=ALU.mult, op1=ALU.mult), reads=[B_pgO, B_sm[w][h], B_zt[i]], writes=[B_mb[w][h]])
        for h in range(H):
            s_att(h)
        if DBG["p2_stage"] <= 3:
            return

        def stage(fn):
            for h in range(H):
                fn(h)

        hq = {}

        def s_pre(h):
            pk, Bpk = ptk[:, h, :], B_ptk
            mk.op(PE, lambda e: e.transpose(pk, qkb[i][:, 8 + h, :], ident_bf[:]), reads=[B_qkb[i], B_identbf], writes=[Bpk])
            mk.op(ACT, lambda e: e.activation(out=egk[w][:, h, :], in_=pk, func=AF.Copy, scale=eG[:, h:h + 1]),
                  reads=[Bpk, BG], writes=[B_egk[w][h]])
            mk.op(DVE, lambda e: e.tensor_scalar(out=kdec[w][:, h, :], in0=pk, scalar1=kd[:, h:h + 1], scalar2=None,
                                                 op0=ALU.mult), reads=[Bpk, BG], writes=[B_kdec[w][h]])
            pv, Bpv = ptv[:, h, :], B_ptv
            mk.op(PE, lambda e: e.transpose(pv, qkb[i][:, 16 + h, :], ident_bf[:]), reads=[B_qkb[i], B_identbf], writes=[Bpv])
            mk.op(ACT, lambda e: e.activation(out=vb[w][:, h, :], in_=pv, func=AF.Copy), reads=[Bpv], writes=[B_vb[w][h]])
            gr, Bgr = grep_r[h % 2], B_grep_r[h % 2]
            er, Ber = eGr_r[h % 2], B_eGr_r[h % 2]
            mk.op(POOL, lambda e: e.tensor_scalar(out=gr[:], in0=ones, scalar1=gg[:, h:h + 1], scalar2=None,
                                                  op0=ALU.mult), reads=[BG, B_consts], writes=[Bgr])
            pgr, Bpgr = getq()
            mk.op(PE, lambda e: e.matmul(pgr, lhsT=gr[:], rhs=tri, start=True, stop=True),
                  reads=[Bgr, B_consts], writes=[Bpgr])
            mk.op(DVE, lambda e: e.scalar_tensor_tensor(out=tG[w][:, h, :], in0=pgr, scalar=Gs[i][:, h:h + 1], in1=maskA,
                                                        op0=ALU.subtract, op1=ALU.add),
                  reads=[Bpgr, B_Gs[i], B_consts], writes=[B_tG[w][h]])
            mk.op(ACT, lambda e: e.activation(out=DT[w][:, h, :], in_=tG[w][:, h, :], func=AF.Exp),
                  reads=[B_tG[w][h]], writes=[B_DT[w][h]])
            mk.op(ACT, lambda e: e.activation(out=er[:], in_=pgr, func=AF.Exp), reads=[Bpgr], writes=[Ber])
            mk.op(POOL, lambda e: e.tensor_tensor(out=qdT[w][:, h, :], in0=zg[i][:, h, :], in1=er[:], op=ALU.mult),
                  reads=[B_zgt[i], Ber], writes=[B_qdT[w][h]])
            pkk, Bpkk = getq()
            mk.op(PE, lambda e: e.matmul(pkk, lhsT=qkb[i][:, 8 + h, :], rhs=qkb[i][:, 8 + h, :], start=True, stop=True),
                  reads=[B_qkb[i]], writes=[Bpkk])
            pat, Bpat = getq()
            mk.op(PE, lambda e: e.matmul(pat, lhsT=qkb[i][:, 8 + h, :], rhs=qkb[i][:, h, :], start=True, stop=True),
                  reads=[B_qkb[i]], writes=[Bpat])
            mk.op(DVE, lambda e: e.tensor_tensor(out=attnT[w][:, h, :], in0=pat, in1=DT[w][:, h, :], op=ALU.mult),
                  reads=[Bpat, B_DT[w][h]], writes=[B_attnT[w][h]])
            mk.op(DVE, lambda e: e.scalar_tensor_tensor(out=X0[w][:, h, :], in0=pkk, scalar=nbet[:, h:h + 1],
                                                        in1=DT[w][:, h, :], op0=ALU.mult, op1=ALU.mult),
                  reads=[Bpkk, BG, B_DT[w][h]], writes=[B_X0[w][h]])
            def msk(dst, Bd, srcb, Bs, col):
                mk.op(POOL, lambda e: e.tensor_tensor(out=dst[w][:, h, :], in0=srcb[w][:, h, :],
                                                      in1=consts[:, col:col + 128], op=ALU.mult),
                      reads=[Bs[w][h], B_consts], writes=[Bd[w][h]])
            msk(Xa, B_Xa, X0, B_X0, C_ST)
            msk(Da, B_Da, X0, B_X0, C_MD)
            msk(C1m, B_C1m, X0, B_X0, C_MC1)
            msk(C2m, B_C2m, X0, B_X0, C_MC2)
            mk.op(PE, lambda e: e.transpose(ptb[:, h, :], Xa[w][:, h, :], ident_bf[:]),
                  reads=[B_Xa[w][h], B_identbf], writes=[B_ptb[h]])
            mk.op(ACT, lambda e: e.activation(out=XTa[w][:, h, :], in_=ptb[:, h, :], func=AF.Copy),
                  reads=[B_ptb[h]], writes=[B_XTa[w][h]])
            msk(DTa, B_DTa, XTa, B_XTa, C_MDT)
            msk(C1T, B_C1T, XTa, B_XTa, C_MC1T)
            msk(C2T, B_C2T, XTa, B_XTa, C_MC2T)
            mk.op(POOL, lambda e: e.tensor_tensor(out=Pa[w][:, h, :], in0=Da[w][:, h, :], in1=ident_bf[:], op=ALU.add),
                  reads=[B_Da[w][h], B_identbf], writes=[B_Pa[w][h]])
            mk.op(POOL, lambda e: e.tensor_tensor(out=PTa[w][:, h, :], in0=DTa[w][:, h, :], in1=ident_bf[:], op=ALU.add),
                  reads=[B_DTa[w][h], B_identbf], writes=[B_PTa[w][h]])
            hq[h] = dict(D=(Da, B_Da), DT=(DTa, B_DTa), Dn=(Db, B_Db), DTn=(DTb, B_DTb),
                         P=(Pa, B_Pa), Pn=(Pb, B_Pb), PT=(PTa, B_PTa), PTn=(PTb, B_PTb))

        stage(s_pre)
        if DBG["p2_stage"] <= 4:
            return

        def evac(E, dst, Bd, psrc, Bp):
            if E is ACT:
                mk.op(ACT, lambda e: e.activation(out=dst, in_=psrc, func=AF.Copy), reads=[Bp], writes=[Bd])
            else:
                mk.op(DVE, lambda e: e.tensor_copy(out=dst, in_=psrc), reads=[Bp], writes=[Bd])

        def mm(lhsT, Bl, rhs, Br):
            p_, Bp_ = getq()
            mk.op(PE, lambda e: e.matmul(p_, lhsT=lhsT, rhs=rhs, start=True, stop=True), reads=[Bl, Br], writes=[Bp_])
            return p_, Bp_

        def addto(dst, Bd, p_, Bp_, base, Bb):
            mk.op(DVE, lambda e: e.tensor_tensor(out=dst, in0=p_, in1=base, op=ALU.add), reads=[Bp_, Bb], writes=[Bd])

        for lev in range(1, 4):
            def s_sq(h, lev=lev):
                d = hq[h]
                (Dm, BD), (DT_, BDT), (Dn, BDn), (DTn, BDTn) = d["D"], d["DT"], d["Dn"], d["DTn"]
                p1, Bp1 = mm(Dm[w][:, h, :], BD[w][h], DT_[w][:, h, :], BDT[w][h])
                evac(ACT, DTn[w][:, h, :], BDTn[w][h], p1, Bp1)
                if lev < 3:
                    p2, Bp2 = mm(DT_[w][:, h, :], BDT[w][h], Dm[w][:, h, :], BD[w][h])
                    evac(DVE, Dn[w][:, h, :], BDn[w][h], p2, Bp2)
            stage(s_sq)

            def s_p(h, lev=lev):
                d = hq[h]
                (DTn, BDTn), (P, BP), (Pn, BPn), (PT, BPT), (PTn, BPTn) = d["DTn"], d["P"], d["Pn"], d["PT"], d["PTn"]
                p3, Bp3 = mm(DTn[w][:, h, :], BDTn[w][h], P[w][:, h, :], BP[w][h])
                addto(Pn[w][:, h, :], BPn[w][h], p3, Bp3, P[w][:, h, :], BP[w][h])
                p4, Bp4 = mm(P[w][:, h, :], BP[w][h], DTn[w][:, h, :], BDTn[w][h])
                addto(PTn[w][:, h, :], BPTn[w][h], p4, Bp4, PT[w][:, h, :], BPT[w][h])
                d["D"], d["Dn"] = d["Dn"], d["D"]
                d["DT"], d["DTn"] = d["DTn"], d["DT"]
                d["P"], d["Pn"] = d["Pn"], d["P"]
                d["PT"], d["PTn"] = d["PTn"], d["PT"]
            stage(s_p)

        def s_m1(h):
            d = hq[h]
            (P, BP), (Pn, BPn), (PT, BPT), (PTn, BPTn) = d["P"], d["Pn"], d["PT"], d["PTn"]
            py, Bpy = mm(C1T[w][:, h, :], B_C1T[w][h], P[w][:, h, :], BP[w][h])
            evac(ACT, Yb[w][:, h, :], B_Yb[w][h], py, Bpy)
            py2, Bpy2 = mm(C1m[w][:, h, :], B_C1m[w][h], PT[w][:, h, :], BPT[w][h])
            evac(ACT, Ypb[w][:, h, :], B_Ypb[w][h], py2, Bpy2)
            pz, Bpz = mm(PT[w][:, h, :], BPT[w][h], Yb[w][:, h, :], B_Yb[w][h])
            addto(Pn[w][:, h, :], BPn[w][h], pz, Bpz, P[w][:, h, :], BP[w][h])
            pz2, Bpz2 = mm(P[w][:, h, :], BP[w][h], Ypb[w][:, h, :], B_Ypb[w][h])
            addto(PTn[w][:, h, :], BPTn[w][h], pz2, Bpz2, PT[w][:, h, :], BPT[w][h])
            d["P"], d["Pn"] = d["Pn"], d["P"]
            d["PT"], d["PTn"] = d["PTn"], d["PT"]
        stage(s_m1)

        def s_m2(h):
            d = hq[h]
            (P, BP), (Pn, BPn), (PT, BPT) = d["P"], d["Pn"], d["PT"]
            py, Bpy = mm(C2T[w][:, h, :], B_C2T[w][h], P[w][:, h, :], BP[w][h])
            evac(ACT, Yb[w][:, h, :], B_Yb[w][h], py, Bpy)
            pz, Bpz = mm(PT[w][:, h, :], BPT[w][h], Yb[w][:, h, :], B_Yb[w][h])
            addto(Pn[w][:, h, :], BPn[w][h], pz, Bpz, P[w][:, h, :], BP[w][h])
            d["P"], d["Pn"] = d["Pn"], d["P"]
        stage(s_m2)

        if DBG["p2_stage"] <= 5:
            return

        def s_uw(h):
            (P, BP) = hq[h]["P"]
            pu, Bpu = getq()
            mk.op(PE, lambda e: e.matmul(pu, lhsT=P[w][:, h, :], rhs=vb[w][:, h, :], start=True, stop=True),
                  reads=[BP[w][h], B_vb[w][h]], writes=[Bpu])
            mk.op(ACT, lambda e: e.activation(out=usb[w][:, h, :], in_=pu, func=AF.Copy, scale=bet[:, h:h + 1]),
                  reads=[Bpu, BG], writes=[B_usb[w][h]])
            pw, Bpw = getq()
            mk.op(PE, lambda e: e.matmul(pw, lhsT=egk[w][:, h, :], rhs=P[w][:, h, :], start=True, stop=True),
                  reads=[BP[w][h], B_egk[w][h]], writes=[Bpw])
            mk.op(ACT, lambda e: e.activation(out=wTb[w][:, h, :], in_=pw, func=AF.Copy), reads=[Bpw], writes=[B_wTb[w][h]])
        stage(s_uw)
        if DBG["p2_stage"] <= 6:
            return

        pws = {}
        for c in range(2):
            n = 2 * t + c
            cur, nxt = n % 3, (n + 1) % 3
            rs = slice(c * 64, (c + 1) * 64)

            def s_ws(h, c=c, cur=cur, rs=rs):
                if c == 0:
                    pws[h] = getq()
                pw_, Bpw_ = pws[h]
                mk.op(PE, lambda e: e.matmul(pw_[rs, :], lhsT=wTb[w][:, h, rs], rhs=Sb[:, h, cur, :], start=True, stop=True),
                      reads=[B_wTb[w][h], B_Sb[h][cur]], writes=[Bpw_])
                mk.op(DVE, lambda e: e.scalar_tensor_tensor(out=vn[w][rs, h, :], in0=pw_[rs, :], scalar=nbet[rs, h:h + 1],
                                                            in1=usb[w][rs, h, :], op0=ALU.mult, op1=ALU.add),
                      reads=[Bpw_, BG, B_usb[w][h]], writes=[B_vn[w][h]])
            stage(s_ws)

            def s_ds(h, c=c, nxt=nxt, rs=rs):
                pd, Bpd = getq()
                mk.op(PE, lambda e: e.matmul(pd, lhsT=kdec[w][rs, h, :], rhs=vn[w][rs, h, :], start=True, stop=True),
                      reads=[B_kdec[w][h], B_vn[w][h]], writes=[Bpd])
                mk.op(DVE, lambda e: e.scalar_tensor_tensor(out=S32[:, h, :], in0=S32[:, h, :],
                                                            scalar=cdb[:, c * H + h:c * H + h + 1], in1=pd,
                                                            op0=ALU.mult, op1=ALU.add),
                      reads=[Bpd, BG], writes=[B_S32[h]])
                mk.op(ACT, lambda e: e.activation(out=Sb[:, h, nxt, :], in_=S32[:, h, :], func=AF.Copy),
                      reads=[B_S32[h]], writes=[B_Sb[h][nxt]])
            stage(s_ds)
        if DBG["p2_stage"] <= 7:
            return

        def s_out(h):
            po, Bpo = getq()
            s0, s1 = (2 * t) % 3, (2 * t + 1) % 3
            mk.op(PE, lambda e: e.matmul(po[0:64, :], lhsT=qdT[w][:, h, 0:64], rhs=Sb[:, h, s0, :], start=True, stop=False,
                                         skip_group_check=True),
                  reads=[B_qdT[w][h], B_Sb[h][s0]], writes=[Bpo], signal=False)
            mk.op(PE, lambda e: e.matmul(po[64:128, :], lhsT=qdT[w][:, h, 64:128], rhs=Sb[:, h, s1, :], start=True, stop=False,
                                         skip_group_check=True),
                  reads=[B_qdT[w][h], B_Sb[h][s1]], writes=[Bpo], signal=False)
            mk.op(PE, lambda e: e.matmul(po, lhsT=attnT[w][:, h, :], rhs=vn[w][:, h, :], start=False, stop=True,
                                         skip_group_check=True),
                  reads=[B_attnT[w][h], B_vn[w][h]], writes=[Bpo])
            mk.op(ACT, lambda e: e.activation(out=t1[w][:, h, :], in_=po, func=AF.Square, accum_out=sm[w][:, h, 1:2]),
                  reads=[Bpo], writes=[B_t1[w][h], B_sm[w][h]])
            mk.op(ACT, lambda e: e.activation(out=sm[w][:, h, 2:3], in_=sm[w][:, h, 1:2], func=AF.Sqrt, bias=EPS * HD),
                  writes=[B_sm[w][h]])
            mk.op(DVE, lambda e: e.reciprocal(out=sm[w][:, h, 2:3], in_=sm[w][:, h, 2:3]), writes=[B_sm[w][h]])
            ga_ap = zt[i][:, 1040 + h * 128:1040 + (h + 1) * 128]
            mk.op(POOL, lambda e: e.tensor_tensor(out=GW[w][:, h, :], in0=zt[i][:, h * 128:(h + 1) * 128], in1=gn, op=ALU.mult),
                  reads=[B_zt[i], B_hv], writes=[B_GW[w][h]])
            mk.op(POOL, lambda e: e.tensor_tensor(out=GW[w][:, h, :], in0=GW[w][:, h, :], in1=ga_ap, op=ALU.mult),
                  reads=[B_zt[i]], writes=[B_GW[w][h]])
            mk.op(DVE, lambda e: e.scalar_tensor_tensor(out=t1[w][:, h, :], in0=po, scalar=sm[w][:, h, 2:3], in1=GW[w][:, h, :],
                                                        op0=ALU.mult, op1=ALU.mult),
                  reads=[Bpo, B_sm[w][h], B_GW[w][h]], writes=[B_t1[w][h]])
            mk.op(POOL, lambda e: e.tensor_tensor(out=mtile[i][:, h * 128:(h + 1) * 128], in0=t1[w][:, h, :], in1=mb[w][:, h, :],
                                                  op=ALU.add),
                  reads=[B_t1[w][h], B_mb[w][h]], writes=[B_mtile[i]])
        stage(s_out)
        mk.dma(SP, f"p2m{i}", g["m_d"][ts, :], mtile[i][:], reads=[B_mtile[i]], writes=[g["B_m"]])

    for t in range(NT):
        do_tile(t)


def make_consts():
    c = np.zeros((128, C_N), np.float32)
    p = np.arange(128)[:, None]
    f = np.arange(128)[None, :]
    same = (p // 64) == (f // 64)
    c[:, C_ID:C_ID + 128] = np.eye(128)
    c[:, C_ONE:C_ONE + 128] = 1.0
    c[:, C_MA:C_MA + 128] = np.where(same & (f >= p), 0.0, NEG)
    c[:, C_ST:C_ST + 128] = (same & (f > p))
    c[:, C_TRI:C_TRI + 128] = (same & (p <= f))
    kl = np.arange(128)[:, None, None]
    mi = np.arange(5)[None, :, None]
    ql = np.arange(128)[None, None, :]
    dch = 2 * (mi - 4) + kl // 64 - ql // 64
    c[:, C_AM:C_AM + 640] = ((dch >= -8) & (dch <= 0)).reshape(128, 640)
    st = same & (f > p)
    bd16 = (p // 16) == (f // 16)
    bd32 = (p // 32) == (f // 32)
    md, mc1, mc2 = st & bd16, st & bd32 & ~bd16, st & ~bd32
    c[:, C_MD:C_MD + 128] = md
    c[:, C_MC1:C_MC1 + 128] = mc1
    c[:, C_MC2:C_MC2 + 128] = mc2
    c[:, C_MDT:C_MDT + 128] = md.T
    c[:, C_MC1T:C_MC1T + 128] = mc1.T
    c[:, C_MC2T:C_MC2T + 128] = mc2.T
    c[:, C_S0:C_S0 + 128] = (p < 64)
    c[:, C_S1:C_S1 + 128] = (p >= 64)
    return c


def layout_inputs(x, c, w_ada, b_ada, norm_mix, norm_mlp, w_in, conv_w, a_log, dt_bias,
                  gdn_norm, rel_bias, w_out, w_ff_in, w_ff_out, final_norm):
    f = lambda a: np.ascontiguousarray(np.asarray(a, dtype=np.float32))
    L = w_ada.shape[0]
    kl = np.arange(128)[:, None, None]
    mi = np.arange(5)[None, :, None]
    ql = np.arange(128)[None, None, :]
    idx = np.clip(ql + 128 * (4 - mi) - kl, -256, 256) + 256
    rb = np.asarray(rel_bias, np.float32)
    biasT = rb[:, :, idx]
    biasT = f(biasT.transpose(0, 2, 1, 3, 4).reshape(L, 128, H * 640))
    cw = np.asarray(conv_w, np.float32).reshape(L, 4, 24, 128).transpose(3, 0, 2, 1).reshape(128, L * 96)
    shared = dict(
        w_ada=f(w_ada), b_ada=f(b_ada), norm_mix=f(norm_mix), norm_mlp=f(norm_mlp), w_in=f(w_in),
        cw=f(cw), a_log=f(np.asarray(a_log).reshape(1, -1)), dt_bias=f(np.asarray(dt_bias).reshape(1, -1)),
        gdn_norm=f(np.asarray(gdn_norm).reshape(1, -1)), biasT=biasT, w_out=f(w_out), w_ff_in=f(w_ff_in),
        w_ff_out=f(w_ff_out), final_norm=f(np.asarray(final_norm).reshape(1, -1)), consts=make_consts())
    x = np.asarray(x, np.float32)
    c = np.asarray(c, np.float32)
    per = []
    for b in range(x.shape[0]):
        d = dict(shared)
        d["x"] = f(x[b])
        d["cT"] = f(c[b].reshape(KC, 128).T)
        per.append(d)
    return per


def kernel(**inputs):
    x = np.asarray(inputs["x"])
    B, S, _ = x.shape
    L = np.asarray(inputs["w_ada"]).shape[0]
    per = layout_inputs(**inputs)
    nc, mk = build_program(S, L)
    n = B
    in_maps = [per[j] for j in range(n)]
    res = run_bass_kernel_spmd(nc, in_maps, core_ids=list(range(n)))
    out = np.stack([np.asarray(res.results[b]["out"], np.float32) for b in range(B)], axis=0)
    return out
```

```python
import numpy as np
from contextlib import ExitStack
import concourse.bass as bass
import concourse.mybir as mybir
from concourse.bass_utils import run_bass_kernel_spmd

F32 = mybir.dt.float32
BF16 = mybir.dt.bfloat16
AF = mybir.ActivationFunctionType
ALU = mybir.AluOpType
AX = mybir.AxisListType

D = 1024
H = 8
HD = 128
KC = 8
DFF = 4096
INW = 9232
EPS = 1e-6
LIM = 24000
NEG = -30000.0

C_ID, C_ONE, C_MA, C_ST, C_TRI, C_AM = 0, 128, 256, 384, 512, 640
C_S0, C_S1 = 1280, 1408
C_MD, C_MC1, C_MC2, C_MDT, C_MC1T, C_MC2T = 1536, 1664, 1792, 1920, 2048, 2176
C_N = 2304


class Buf:
    __slots__ = ("name", "w", "r")

    def __init__(self, name):
        self.name = name
        self.w = None
        self.r = {}


class Src:
    def __init__(self, mk, name, unit):
        self.mk = mk
        self.name = name
        self.unit = unit
        self.sems = []
        self.count = 0
        self.finals = []

    def _roll(self):
        if not self.sems or self.count + self.unit > LIM:
            if self.sems:
                self.finals.append(self.count)
            self.sems.append(self.mk.new_sem())
            self.count = 0

    def next_event(self):
        self._roll()
        self.count += self.unit
        return (self, len(self.sems) - 1, self.count)

    def peek_event(self):
        self._roll()
        return (self, len(self.sems) - 1, self.count + self.unit)


class Eng(Src):
    def __init__(self, mk, name, key, same_sync):
        super().__init__(mk, name, 1)
        self.key = key
        self.same_sync = same_sync
        self.thunks = []
        self.waited = {}
        self.is_eng = True


class MK:
    def __init__(self, nc, es):
        self.nc = nc
        self.es = es
        self.nsem = 0
        self.PE = Eng(self, "pe", "tensor", False)
        self.ACT = Eng(self, "act", "scalar", True)
        self.DVE = Eng(self, "dve", "vector", True)
        self.POOL = Eng(self, "pool", "gpsimd", True)
        self.SP = Eng(self, "sp", "sync", False)
        self.engs = [self.PE, self.ACT, self.DVE, self.POOL, self.SP]
        self.nops = 0
        self.uid = 0
        self.streams = {}

    def new_sem(self):
        self.nsem += 1
        return self.es.enter_context(self.nc.semaphore(f"s{self.nsem}"))

    def stream(self, name):
        if name in self.streams:
            return self.streams[name]
        s = Src(self, name, 16)
        s.is_eng = False
        s.last = None
        self.streams[name] = s
        return s

    def _wait(self, E, ev):
        src, ep, val = ev
        if src is E and not E.same_sync:
            return
        key = (id(src), ep)
        if E.waited.get(key, 0) >= val:
            return
        if src.is_eng:
            for (sid, e2), v in E.waited.items():
                if sid == id(src) and e2 > ep:
                    return
        E.waited[key] = val
        sem = src.sems[ep]
        E.thunks.append(lambda eng, sem=sem, val=val: eng.wait_ge(sem, val))

    def _deps(self, E, reads, writes):
        for b in reads:
            if b.w is not None:
                self._wait(E, b.w)
        for b in writes:
            if b.w is not None:
                self._wait(E, b.w)
            for ev in b.r.values():
                self._wait(E, ev)

    def op(self, E, fn, reads=(), writes=(), signal=True):
        self.nops += 1
        self._deps(E, reads, writes)
        if signal:
            ev = E.next_event()
            sem = ev[0].sems[ev[1]]
            E.thunks.append(lambda eng, fn=fn, sem=sem: fn(eng).then_inc(sem, 1))
        else:
            assert E is self.PE
            ev = E.peek_event()
            E.thunks.append(lambda eng, fn=fn: fn(eng))
        for b in writes:
            b.w = ev
            b.r = {}
        for b in reads:
            b.r[id(E)] = ev

    def dma(self, Q, st, out, in_, reads=(), writes=(), **kw):
        self.nops += 1
        if isinstance(st, str):
            st = self.stream(st)
        self._deps(Q, reads, writes)
        if st.last is not None:
            self._wait(Q, st.last)
        ev = st.next_event()
        st.last = ev
        sem = ev[0].sems[ev[1]]
        Q.thunks.append(lambda eng, out=out, in_=in_, sem=sem, kw=kw:
                        eng.dma_start(out=out, in_=in_, **kw).then_inc(sem, 16))
        for b in writes:
            b.w = ev
            b.r = {}
        for b in reads:
            b.r[id(st)] = ev

    def final_wait(self, E, bufs):
        for b in bufs:
            if b.w is not None:
                self._wait(E, b.w)

    def barrier(self):
        evs = []
        for E in self.engs:
            if E.sems and E.count > 0:
                evs.append((E, len(E.sems) - 1, E.count))
        for s in self.streams.values():
            if s.last is not None:
                evs.append(s.last)
        for E in self.engs:
            for ev in evs:
                if ev[0] is not E:
                    self._wait(E, ev)

    def flush(self):
        self.barrier()
        nc = self.nc
        with nc.Block() as block:
            for E in self.engs:
                th = E.thunks
                E.thunks = []
                if not th:
                    continue

                def mkfn(th):
                    def f(eng):
                        for t in th:
                            t(eng)
                    return f
                getattr(block, E.key)(mkfn(th))


class _Stop(Exception):
    pass


DBG = {"p2_stage": 99}


def build_program(S, L, debug=False, stop=None):
    assert S % 512 == 0
    NT = S // 128
    nc = bass.Bass("TRN2", target_bir_lowering=False)
    es = ExitStack()
    mk = MK(nc, es)
    PE, ACT, DVE, POOL, SP = mk.PE, mk.ACT, mk.DVE, mk.POOL, mk.SP

    def din(name, shape, dt=F32):
        return nc.dram_tensor(name, list(shape), dt, kind="ExternalInput").ap()

    def dscr(name, shape, dt=F32):
        kind = "ExternalOutput" if debug else "Internal"
        return nc.dram_tensor(name, list(shape), dt, kind=kind).ap()

    x_in = din("x", [S, D])
    cT_in = din("cT", [128, KC])
    wada_in = din("w_ada", [L, D, 6 * D])
    bada_in = din("b_ada", [L, 6 * D])
    nmix_in = din("norm_mix", [L, D])
    nmlp_in = din("norm_mlp", [L, D])
    win_in = din("w_in", [L, D, INW])
    cw_in = din("cw", [128, L * 96])
    alog_in = din("a_log", [1, L * H])
    dtb_in = din("dt_bias", [1, L * H])
    gnorm_in = din("gdn_norm", [1, L * HD])
    biasT_in = din("biasT", [L, 128, H * 640])
    wout_in = din("w_out", [L, D, D])
    w1_in = din("w_ff_in", [L, D, DFF])
    w2_in = din("w_ff_out", [L, DFF, D])
    fnorm_in = din("final_norm", [1, D])
    consts_in = din("consts", [128, C_N])
    out_d = nc.dram_tensor("out", [S, D], F32, kind="ExternalOutput").ap()

    xs_d = dscr("xs", [S, D])
    zg_d = dscr("zg", [3 * D, S])
    zqk_d = dscr("zqk", [2 * D, S], BF16)
    ztok_d = dscr("ztok", [S, 3088])
    va_d = dscr("va", [S, D], BF16)
    m_d = dscr("m", [S, D], BF16)
    mod_d = dscr("modd", [128, 6 * D])

    B_xs = [Buf(f"xs{t}") for t in range(NT)]
    B_zg, B_zqk, B_ztok, B_va, B_m, B_mod = (Buf("zg"), Buf("zqk"), Buf("ztok"), Buf("va"),
                                               Buf("m"), Buf("mod"))
    B_out = Buf("out")

    def sb(name, shape, dt=F32, stack=None):
        mk.uid += 1
        return (stack or es).enter_context(nc.sbuf_tensor(f"{name}_u{mk.uid}", list(shape), dt))

    def ps(name, shape, dt=F32, stack=None):
        mk.uid += 1
        return (stack or es).enter_context(nc.psum_tensor(f"{name}_u{mk.uid}", list(shape), dt))

    consts = sb("consts", [128, C_N])
    B_consts = Buf("consts")
    ident_bf = sb("ident_bf", [128, 128], BF16)
    B_identbf = Buf("identbf")
    cw_sb = sb("cw", [128, L * 96])
    B_cw = Buf("cw")
    crep = sb("crep", [128, KC, 128])
    B_crep = Buf("crep")
    st_c = mk.stream("const")
    mk.dma(SP, st_c, consts[:], consts_in[:, :], writes=[B_consts])
    mk.dma(SP, st_c, cw_sb[:], cw_in[:, :], writes=[B_cw])
    mk.op(ACT, lambda e: e.activation(out=ident_bf[:], in_=consts[:, C_ID:C_ID + 128], func=AF.Copy),
          reads=[B_consts], writes=[B_identbf])
    ident = consts[:, C_ID:C_ID + 128]
    ones = consts[:, C_ONE:C_ONE + 128]
    maskA = consts[:, C_MA:C_MA + 128]
    strict = consts[:, C_ST:C_ST + 128]
    tri = consts[:, C_TRI:C_TRI + 128]

    with ExitStack() as ps0:
        ct = sb("ct", [128, KC], stack=ps0)
        cs = sb("cs", [128, KC], stack=ps0)
        B_ct, B_cs = Buf("ct"), Buf("cs")
        mk.dma(SP, st_c, ct[:], cT_in[:, :], writes=[B_ct])
        mk.op(ACT, lambda e: e.activation(out=cs[:], in_=ct[:], func=AF.Silu), reads=[B_ct], writes=[B_cs])
        for kc in range(KC):
            mk.op(DVE, lambda e, kc=kc: e.tensor_scalar(out=crep[:, kc, :], in0=ones, scalar1=cs[:, kc:kc + 1],
                                                         scalar2=None, op0=ALU.mult),
                  reads=[B_cs, B_consts], writes=[B_crep])
        mk.flush()

    def ck(name):
        if stop == name:
            mk.final_wait(SP, B_xs + [B_zg, B_zqk, B_ztok, B_va, B_m, B_mod])
            mk.flush()
            raise _Stop()

    try:
        _layers(locals())
    except _Stop:
        pass
    es.close()
    return nc, mk


def _layers(G):
    (nc, mk, es, S, L, NT, PE, ACT, DVE, POOL, SP, sb, ps, ck) = [G[k] for k in
        ("nc", "mk", "es", "S", "L", "NT", "PE", "ACT", "DVE", "POOL", "SP", "sb", "ps", "ck")]
    (x_in, wada_in, bada_in, nmix_in, nmlp_in, win_in, alog_in, dtb_in, gnorm_in, biasT_in, wout_in, w1_in, w2_in,
     fnorm_in, out_d, xs_d, zg_d, zqk_d, ztok_d, va_d, m_d, mod_d) = [G[k] for k in
        ("x_in", "wada_in", "bada_in", "nmix_in", "nmlp_in", "win_in", "alog_in", "dtb_in", "gnorm_in", "biasT_in",
         "wout_in", "w1_in", "w2_in", "fnorm_in", "out_d", "xs_d", "zg_d", "zqk_d", "ztok_d", "va_d", "m_d", "mod_d")]
    (B_xs, B_zg, B_zqk, B_ztok, B_va, B_m, B_mod, B_out, consts, B_consts, ident_bf, B_identbf, cw_sb, B_cw, crep,
     B_crep, ident, ones) = [G[k] for k in
        ("B_xs", "B_zg", "B_zqk", "B_ztok", "B_va", "B_m", "B_mod", "B_out", "consts", "B_consts", "ident_bf",
         "B_identbf", "cw_sb", "B_cw", "crep", "B_crep", "ident", "ones")]
    for l in range(L):
        x_src = x_in if l == 0 else xs_d

        with ExitStack() as ph:
            wa = [sb(f"wa{i}", [128, KC, 512], stack=ph) for i in range(2)]
            B_wa = [Buf(f"wa{i}") for i in range(2)]
            modb = sb("modb", [128, 6 * D], stack=ph)
            B_modb = Buf("modb")
            brow = sb("brow", [128, 6 * D], stack=ph)
            B_brow = Buf("brow")
            nrow = sb("nrow", [128, 2 * D], stack=ph)
            B_nrow = Buf("nrow")
            pm = [ps(f"pm{i}", [128, 512], stack=ph) for i in range(2)]
            B_pm = [Buf(f"pm{i}") for i in range(2)]
            st_w = mk.stream("wada")
            st_m = mk.stream("modst")
            mk.dma(SP, st_m, brow[:], bada_in[l:l + 1, :].partition_broadcast(128), writes=[B_brow])
            mk.dma(SP, st_m, nrow[:, 0:D], nmix_in[l:l + 1, :].partition_broadcast(128), writes=[B_nrow])
            mk.dma(SP, st_m, nrow[:, D:2 * D], nmlp_in[l:l + 1, :].partition_broadcast(128), writes=[B_nrow])
            wav = wada_in[l].rearrange("(kc p) n -> p kc n", p=128)
            for nb in range(12):
                i = nb % 2
                mk.dma(SP, f"wada{i}", wa[i][:], wav[:, :, nb * 512:(nb + 1) * 512], writes=[B_wa[i]])
                for kc in range(KC):
                    mk.op(PE, lambda e, i=i, kc=kc: e.matmul(pm[i][:], lhsT=crep[:, kc, :], rhs=wa[i][:, kc, :],
                                                             start=(kc == 0), stop=(kc == KC - 1)),
                          reads=[B_crep, B_wa[i]], writes=[B_pm[i]], signal=(kc == KC - 1))
                mk.op(DVE, lambda e, i=i, nb=nb: e.tensor_tensor(out=modb[:, nb * 512:(nb + 1) * 512], in0=pm[i][:],
                                                                   in1=brow[:, nb * 512:(nb + 1) * 512], op=ALU.add),
                      reads=[B_pm[i], B_brow], writes=[B_modb])
            mk.op(DVE, lambda e: e.tensor_scalar(out=nrow[:], in0=nrow[:], scalar1=float(D) ** 0.5, scalar2=None,
                                                 op0=ALU.mult), writes=[B_nrow])
            mk.op(DVE, lambda e: e.scalar_tensor_tensor(out=modb[:, D:2 * D], in0=modb[:, D:2 * D], scalar=1.0,
                                                         in1=nrow[:, 0:D], op0=ALU.add, op1=ALU.mult),
                  reads=[B_nrow], writes=[B_modb])
            mk.op(DVE, lambda e: e.scalar_tensor_tensor(out=modb[:, 4 * D:5 * D], in0=modb[:, 4 * D:5 * D], scalar=1.0,
                                                         in1=nrow[:, D:2 * D], op0=ALU.add, op1=ALU.mult),
                  reads=[B_nrow], writes=[B_modb])
            mk.dma(SP, st_m, mod_d[:, :], modb[:], reads=[B_modb], writes=[B_mod])
            mk.flush()
        ck(f"mod{l}")

        with ExitStack() as ph:
            hT = sb("hT", [128, KC, S], BF16, stack=ph)
            B_hT = Buf("hT")
            m1 = sb("m1", [128, 2 * D], stack=ph)
            B_m1 = Buf("m1")
            st_l = mk.stream("p1l")
            st_s = mk.stream("p1s")
            st_wc = mk.stream("p1w")
            mk.dma(SP, "p1m", m1[:], mod_d[:, 0:2 * D], reads=[B_mod], writes=[B_m1])
            with ExitStack() as p1a:
                xt = [sb(f"xt{i}", [128, D], stack=p1a) for i in range(2)]
                B_xt = [Buf(f"xt{i}") for i in range(2)]
                junk = sb("junk", [128, D], stack=p1a)
                B_junk = Buf("junk")
                h1 = sb("h1", [128, D], stack=p1a)
                B_h1 = Buf("h1")
                hb = [sb(f"hb{i}", [128, D], BF16, stack=p1a) for i in range(2)]
                B_hb = [Buf(f"hb{i}") for i in range(2)]
                ssq = sb("ssq", [128, 2], stack=p1a)
                B_ssq = Buf("ssq")
                ptr = [ps(f"ptr{i}", [128, KC, 128], BF16, stack=p1a) for i in range(2)]
                B_ptr = [Buf(f"ptr{i}") for i in range(2)]
                for t in range(NT):
                    i = t % 2
                    mk.dma(SP, f"p1x{i}", xt[i][:], x_src[t * 128:(t + 1) * 128, :], reads=[B_xs[t]] if l > 0 else [],
                           writes=[B_xt[i]])
                    mk.op(ACT, lambda e, i=i: e.activation(out=junk[:], in_=xt[i][:], func=AF.Square,
                                                           accum_out=ssq[:, 0:1]),
                          reads=[B_xt[i]], writes=[B_junk, B_ssq])
                    mk.op(ACT, lambda e: e.activation(out=ssq[:, 1:2], in_=ssq[:, 0:1], func=AF.Sqrt, bias=EPS * D),
                          reads=[B_ssq], writes=[B_ssq])
                    mk.op(DVE, lambda e: e.reciprocal(out=ssq[:, 1:2], in_=ssq[:, 1:2]), reads=[B_ssq], writes=[B_ssq])
                    mk.op(DVE, lambda e, i=i: e.scalar_tensor_tensor(out=h1[:], in0=xt[i][:], scalar=ssq[:, 1:2],
                                                                     in1=m1[:, D:2 * D], op0=ALU.mult, op1=ALU.mult),
                          reads=[B_xt[i], B_ssq, B_m1], writes=[B_h1])
                    mk.op(POOL, lambda e, i=i: e.tensor_tensor(out=hb[i][:], in0=h1[:], in1=m1[:, 0:D], op=ALU.add),
                          reads=[B_h1, B_m1], writes=[B_hb[i]])
                    for kc in range(KC):
                        mk.op(PE, lambda e, i=i, kc=kc: e.transpose(ptr[i][:, kc, :], hb[i][:, kc * 128:(kc + 1) * 128],
                                                                    ident_bf[:]),
                              reads=[B_hb[i], B_identbf], writes=[B_ptr[i]], signal=(kc == KC - 1))
                    mk.op(ACT, lambda e, i=i, t=t: e.activation(out=hT[:, :, t * 128:(t + 1) * 128], in_=ptr[i][:],
                                                                func=AF.Copy),
                          reads=[B_ptr[i]], writes=[B_hT])
                mk.flush()
            winv = win_in[l].rearrange("(kc p) n -> p kc n", p=128)
            with ExitStack() as p1b:
                wb = [sb(f"wb{i}", [128, KC, 512], BF16, stack=p1b) for i in range(2)]
                B_wb = [Buf(f"wb{i}") for i in range(2)]
                pz = [ps(f"pz{i}", [128, 512], stack=p1b) for i in range(3)]
                B_pz = [Buf(f"pz{i}") for i in range(3)]
                pn = [ps(f"pn{i}", [128, 512], stack=p1b) for i in range(2)]
                B_pn = [Buf(f"pn{i}") for i in range(2)]
                xc = [sb(f"xc{i}", [128, 515], stack=p1b) for i in range(2)]
                B_xc = [Buf(f"xc{i}") for i in range(2)]
                yc = [sb(f"yc{i}", [128, 512], stack=p1b) for i in range(2)]
                B_yc = [Buf(f"yc{i}") for i in range(2)]
                sq = [sb(f"sq{i}", [128, 512], stack=p1b) for i in range(2)]
                B_sq = [Buf(f"sq{i}") for i in range(2)]
                rn = [sb(f"rn{i}", [128, 512], stack=p1b) for i in range(2)]
                B_rn = [Buf(f"rn{i}") for i in range(2)]
                NST = min(S, 2048)
                stg = [sb(f"stg{i}", [128, NST], stack=p1b) for i in range(2)]
                B_stg = [Buf(f"stg{i}") for i in range(2)]
                stgb = [sb(f"stgb{i}", [128, NST], BF16, stack=p1b) for i in range(2)]
                B_stgb = [Buf(f"stgb{i}") for i in range(2)]
                nchunk = S // 512
                cpst = NST // 512
                sidx = 0
                wi = 0
                pzi = 0
                for g in range(24 + 16):
                    gdn = g < 24
                    col0 = g * 128 if gdn else 4112 + (g - 24) * 128
                    if g % 4 == 0:
                        wi += 1
                        i = wi % 2
                        mk.dma(POOL, f"p1w{i}", wb[i][:], winv[:, :, col0:col0 + 512], writes=[B_wb[i]])
                    wo4 = (g % 4) * 128
                    for c in range(nchunk):
                        pi = pzi % 3
                        pzi += 1
                        for kc in range(KC):
                            mk.op(PE, lambda e, i=i, kc=kc, c=c, pi=pi, wo4=wo4: e.matmul(
                                pz[pi][:], lhsT=wb[i][:, kc, wo4:wo4 + 128], rhs=hT[:, kc, c * 512:(c + 1) * 512],
                                start=(kc == 0), stop=(kc == KC - 1)),
                                reads=[B_wb[i], B_hT], writes=[B_pz[pi]], signal=(kc == KC - 1))
                        sl = (sidx // cpst) % 2
                        so = (sidx % cpst) * 512
                        if gdn:
                            j = c % 2
                            if c == 0:
                                mk.op(POOL, lambda e, j=j: e.memset(xc[j][:, 0:3], 0.0), writes=[B_xc[j]])
                            mk.op(ACT, lambda e, j=j, pi=pi: e.activation(out=xc[j][:, 3:515], in_=pz[pi][:], func=AF.Copy),
                                  reads=[B_pz[pi]], writes=[B_xc[j]])
                            if c + 1 < nchunk:
                                mk.op(POOL, lambda e, j=j: e.tensor_copy(out=xc[1 - j][:, 0:3], in_=xc[j][:, 512:515]),
                                      reads=[B_xc[j]], writes=[B_xc[1 - j]])
                            CE = DVE if (c % 2 == 0) else POOL
                            cwb = l * 96 + g * 4
                            mk.op(CE, lambda e, j=j, cwb=cwb: e.tensor_scalar(
                                out=yc[j][:], in0=xc[j][:, 0:512], scalar1=cw_sb[:, cwb:cwb + 1], scalar2=None,
                                op0=ALU.mult), reads=[B_xc[j], B_cw], writes=[B_yc[j]])
                            for tp in range(1, 4):
                                if CE is DVE:
                                    mk.op(CE, lambda e, j=j, cwb=cwb, tp=tp: e.scalar_tensor_tensor(
                                        out=yc[j][:], in0=xc[j][:, tp:tp + 512], scalar=cw_sb[:, cwb + tp:cwb + tp + 1],
                                        in1=yc[j][:], op0=ALU.mult, op1=ALU.add),
                                        reads=[B_xc[j], B_cw], writes=[B_yc[j]])
                                else:
                                    mk.op(CE, lambda e, j=j, cwb=cwb, tp=tp: e.tensor_scalar(
                                        out=sq[j][:], in0=xc[j][:, tp:tp + 512], scalar1=cw_sb[:, cwb + tp:cwb + tp + 1],
                                        scalar2=None, op0=ALU.mult), reads=[B_xc[j], B_cw], writes=[B_sq[j]])
                                    mk.op(CE, lambda e, j=j: e.tensor_tensor(out=yc[j][:], in0=yc[j][:], in1=sq[j][:],
                                                                            op=ALU.add),
                                          reads=[B_sq[j]], writes=[B_yc[j]])
                            if g >= 16:
                                mk.op(ACT, lambda e, j=j, sl=sl, so=so: e.activation(
                                    out=stg[sl][:, so:so + 512], in_=yc[j][:], func=AF.Silu),
                                    reads=[B_yc[j]], writes=[B_stg[sl]])
                            else:
                                mk.op(ACT, lambda e, j=j: e.activation(out=yc[j][:], in_=yc[j][:], func=AF.Silu),
                                      reads=[], writes=[B_yc[j]])
                                isq = g < 8
                                s_in = (float(HD) ** 0.5) if isq else 1.0
                                eps2 = EPS * HD if isq else EPS
                                mk.op(ACT, lambda e, j=j, s_in=s_in: e.activation(out=sq[j][:], in_=yc[j][:], func=AF.Square,
                                                                                  scale=s_in),
                                      reads=[B_yc[j]], writes=[B_sq[j]])
                                mk.op(PE, lambda e, j=j: e.matmul(pn[j][:], lhsT=ones, rhs=sq[j][:], start=True, stop=True),
                                      reads=[B_sq[j], B_consts], writes=[B_pn[j]])
                                mk.op(ACT, lambda e, j=j, eps2=eps2: e.activation(out=rn[j][:], in_=pn[j][:], func=AF.Sqrt,
                                                                                  bias=eps2),
                                      reads=[B_pn[j]], writes=[B_rn[j]])
                                mk.op(DVE, lambda e, j=j: e.reciprocal(out=rn[j][:], in_=rn[j][:]), writes=[B_rn[j]])
                                mk.op(POOL, lambda e, j=j, sl=sl, so=so: e.tensor_tensor(
                                    out=stg[sl][:, so:so + 512], in0=yc[j][:], in1=rn[j][:], op=ALU.mult),
                                    reads=[B_yc[j], B_rn[j]], writes=[B_stg[sl]])
                        else:
                            mk.op(ACT, lambda e, pi=pi, sl=sl, so=so: e.activation(
                                out=stgb[sl][:, so:so + 512], in_=pz[pi][:], func=AF.Copy),
                                reads=[B_pz[pi]], writes=[B_stgb[sl]])
                        sidx += 1
                        if sidx % cpst == 0:
                            t0 = (c + 1) * 512 - NST
                            if gdn:
                                mk.dma(SP, f"p1s{sl}", zg_d[g * 128:(g + 1) * 128, t0:t0 + NST], stg[sl][:],
                                       reads=[B_stg[sl]], writes=[B_zg])
                            else:
                                mk.dma(SP, f"p1sb{sl}", zqk_d[(g - 24) * 128:(g - 23) * 128, t0:t0 + NST], stgb[sl][:],
                                       reads=[B_stgb[sl]], writes=[B_zqk])
                mk.flush()
            with ExitStack() as p1c:
                wt = [sb(f"wt{i}", [128, KC, 512], BF16, stack=p1c) for i in range(2)]
                B_wt = [Buf(f"wt{i}") for i in range(2)]
                pz = [ps(f"pzt{i}", [128, 512], stack=p1c) for i in range(3)]
                B_pz = [Buf(f"pzt{i}") for i in range(3)]
                stt = [sb(f"stt{i}", [128, 4, 512], stack=p1c) for i in range(2)]
                B_stt = [Buf(f"stt{i}") for i in range(2)]
                sttb = [sb(f"sttb{i}", [128, 4, 512], BF16, stack=p1c) for i in range(2)]
                B_sttb = [Buf(f"sttb{i}") for i in range(2)]
                blocks = [(3072, 512, AF.Silu, 0, 0), (3584, 512, AF.Silu, 0, 512), (4096, 16, AF.Copy, 0, 1024),
                          (6160, 512, AF.Copy, 1, 0), (6672, 512, AF.Copy, 1, 512)]
                for q in range(4):
                    blocks.append((7184 + q * 512, 512, AF.Sigmoid, 0, 1040 + q * 512))
                pzi = 0
                sidx = 0
                ztv = ztok_d.rearrange("(n p) c -> p n c", p=128)
                vav = va_d.rearrange("(n p) c -> p n c", p=128)
                for bi, (col0, ncol, func, dest, dcol) in enumerate(blocks):
                    i = bi % 2
                    mk.dma(POOL, f"p1wt{i}", wt[i][:, :, 0:ncol], winv[:, :, col0:col0 + ncol], writes=[B_wt[i]])
                    for t in range(NT):
                        pi = pzi % 3
                        pzi += 1
                        for kc in range(KC):
                            mk.op(PE, lambda e, i=i, kc=kc, t=t, pi=pi, ncol=ncol: e.matmul(
                                pz[pi][:, 0:ncol], lhsT=hT[:, kc, t * 128:(t + 1) * 128], rhs=wt[i][:, kc, 0:ncol],
                                start=(kc == 0), stop=(kc == KC - 1)),
                                reads=[B_wt[i], B_hT], writes=[B_pz[pi]], signal=(kc == KC - 1))
                        sl = (sidx // 4) % 2
                        so = sidx % 4
                        sidx += 1
                        if dest == 0:
                            mk.op(ACT, lambda e, pi=pi, sl=sl, so=so, ncol=ncol, func=func: e.activation(
                                out=stt[sl][:, so, 0:ncol], in_=pz[pi][:, 0:ncol], func=func),
                                reads=[B_pz[pi]], writes=[B_stt[sl]])
                        else:
                            mk.op(ACT, lambda e, pi=pi, sl=sl, so=so, ncol=ncol, func=func: e.activation(
                                out=sttb[sl][:, so, 0:ncol], in_=pz[pi][:, 0:ncol], func=func),
                                reads=[B_pz[pi]], writes=[B_sttb[sl]])
                        if sidx % 4 == 0:
                            n0 = t - 3
                            if dest == 0:
                                mk.dma(SP, f"p1st{sl}", ztv[:, n0:n0 + 4, dcol:dcol + ncol], stt[sl][:, :, 0:ncol],
                                       reads=[B_stt[sl]], writes=[B_ztok])
                            else:
                                mk.dma(SP, f"p1stb{sl}", vav[:, n0:n0 + 4, dcol:dcol + ncol], sttb[sl][:, :, 0:ncol],
                                       reads=[B_sttb[sl]], writes=[B_va])
                mk.flush()
        ck(f"p1{l}")

        with ExitStack() as ph:
            p2_mixers(nc, mk, ph, l, S, NT, dict(
                consts=consts, B_consts=B_consts, ident_bf=ident_bf, B_identbf=B_identbf,
                zg_d=zg_d, zqk_d=zqk_d, ztok_d=ztok_d, va_d=va_d, m_d=m_d,
                B_zg=B_zg, B_zqk=B_zqk, B_ztok=B_ztok, B_va=B_va, B_m=B_m,
                alog_in=alog_in, dtb_in=dtb_in, gnorm_in=gnorm_in, biasT_in=biasT_in))
            mk.flush()
        ck(f"p2{l}")

        with ExitStack() as ph:
            wo = sb("wo", [128, KC, D], BF16, stack=ph)
            B_wo = Buf("wo")
            gt1 = sb("gt1", [128, D], stack=ph)
            B_gt1 = Buf("gt1")
            st_l = mk.stream("p3al")
            st_s = mk.stream("p3as")
            st_wc = mk.stream("p3aw")
            wov = wout_in[l].rearrange("(kc p) n -> p kc n", p=128)
            for hf in range(2):
                mk.dma(POOL, f"p3aw{hf}", wo[:, :, hf * 512:(hf + 1) * 512], wov[:, :, hf * 512:(hf + 1) * 512], writes=[B_wo])
            mk.dma(SP, "p3ag", gt1[:], mod_d[:, 2 * D:3 * D], reads=[B_mod], writes=[B_gt1])
            mt = [sb(f"mt{i}", [128, D], BF16, stack=ph) for i in range(2)]
            B_mt = [Buf(f"mt{i}") for i in range(2)]
            xt = [sb(f"xa{i}", [128, D], stack=ph) for i in range(2)]
            B_xt = [Buf(f"xa{i}") for i in range(2)]
            mT = [sb(f"mT{i}", [128, KC, 128], BF16, stack=ph) for i in range(2)]
            B_mT = [Buf(f"mT{i}") for i in range(2)]
            ptr = [ps(f"ptra{i}", [128, KC, 128], BF16, stack=ph) for i in range(2)]
            B_ptr = [Buf(f"ptra{i}") for i in range(2)]
            py = [ps(f"pya{i}", [128, 512], stack=ph) for i in range(4)]
            B_py = [Buf(f"pya{i}") for i in range(4)]
            tmp = [sb(f"tmpa{i}", [128, D], stack=ph) for i in range(2)]
            B_tmp = [Buf(f"tmpa{i}") for i in range(2)]
            xo = [sb(f"xoa{i}", [128, D], stack=ph) for i in range(2)]
            B_xo = [Buf(f"xoa{i}") for i in range(2)]
            for t in range(NT):
                i = t % 2
                mk.dma(SP, f"p3am{i}", mt[i][:], m_d[t * 128:(t + 1) * 128, :], reads=[B_m], writes=[B_mt[i]])
                mk.dma(SP, f"p3ax{i}", xt[i][:], x_src[t * 128:(t + 1) * 128, :], reads=[B_xs[t]] if l > 0 else [],
                       writes=[B_xt[i]])
                for kc in range(KC):
                    mk.op(PE, lambda e, i=i, kc=kc: e.transpose(ptr[i][:, kc, :], mt[i][:, kc * 128:(kc + 1) * 128], ident_bf[:]),
                          reads=[B_mt[i], B_identbf], writes=[B_ptr[i]], signal=(kc == KC - 1))
                mk.op(ACT, lambda e, i=i: e.activation(out=mT[i][:], in_=ptr[i][:], func=AF.Copy),
                      reads=[B_ptr[i]], writes=[B_mT[i]])
                for hf in range(2):
                    pi = i * 2 + hf
                    for kc in range(KC):
                        mk.op(PE, lambda e, i=i, kc=kc, hf=hf, pi=pi: e.matmul(
                            py[pi][:], lhsT=mT[i][:, kc, :], rhs=wo[:, kc, hf * 512:(hf + 1) * 512],
                            start=(kc == 0), stop=(kc == KC - 1)),
                            reads=[B_mT[i], B_wo], writes=[B_py[pi]], signal=(kc == KC - 1))
                    mk.op(DVE, lambda e, i=i, hf=hf, pi=pi: e.tensor_tensor(
                        out=tmp[i][:, hf * 512:(hf + 1) * 512], in0=py[pi][:], in1=gt1[:, hf * 512:(hf + 1) * 512],
                        op=ALU.mult), reads=[B_py[pi], B_gt1], writes=[B_tmp[i]])
                mk.op(POOL, lambda e, i=i: e.tensor_tensor(out=xo[i][:], in0=tmp[i][:], in1=xt[i][:], op=ALU.add),
                      reads=[B_tmp[i], B_xt[i]], writes=[B_xo[i]])
                mk.dma(SP, f"p3as{i}", xs_d[t * 128:(t + 1) * 128, :], xo[i][:], reads=[B_xo[i]], writes=[B_xs[t]])
            mk.flush()
        ck(f"p3a{l}")

        with ExitStack() as ph:
            last = (l == L - 1)
            w1 = sb("w1", [128, KC, DFF], BF16, stack=ph)
            w2 = sb("w2", [128, 32, D], BF16, stack=ph)
            B_w1, B_w2 = Buf("w1"), Buf("w2")
            m2 = sb("m2", [128, 3 * D], stack=ph)
            B_m2 = Buf("m2")
            st_l = mk.stream("p3bl")
            st_s = mk.stream("p3bs")
            st_wc = mk.stream("p3bw")
            w1v = w1_in[l].rearrange("(kc p) n -> p kc n", p=128)
            w2v = w2_in[l].rearrange("(fc p) n -> p fc n", p=128)
            for q in range(8):
                mk.dma(POOL, f"p3bw{q % 4}", w1[:, :, q * 512:(q + 1) * 512], w1v[:, :, q * 512:(q + 1) * 512], writes=[B_w1])
            for q in range(8):
                mk.dma(POOL, f"p3bw{q % 4}", w2[:, q * 4:(q + 1) * 4, :], w2v[:, q * 4:(q + 1) * 4, :], writes=[B_w2])
            mk.dma(SP, "p3bm", m2[:], mod_d[:, 3 * D:6 * D], reads=[B_mod], writes=[B_m2])
            if last:
                fn = sb("fnrm", [128, D], stack=ph)
                B_fn = Buf("fn")
                mk.dma(SP, "p3bf", fn[:], fnorm_in[0:1, :].partition_broadcast(128), writes=[B_fn])
                mk.op(DVE, lambda e: e.tensor_scalar(out=fn[:], in0=fn[:], scalar1=float(D) ** 0.5, scalar2=None,
                                                     op0=ALU.mult), writes=[B_fn])
            xt = [sb(f"xb{i}", [128, D], stack=ph) for i in range(2)]
            B_xt = [Buf(f"xb{i}") for i in range(2)]
            h1 = sb("h1b", [128, D], stack=ph)
            B_h1 = Buf("h1b")
            junk, B_junk = h1, B_h1
            hb = [sb(f"hbb{i}", [128, D], BF16, stack=ph) for i in range(2)]
            B_hb = [Buf(f"hbb{i}") for i in range(2)]
            hT2 = [sb(f"hT2{i}", [128, KC, 128], BF16, stack=ph) for i in range(2)]
            B_hT2 = [Buf(f"hT2{i}") for i in range(2)]
            ssq = sb("ssqb", [128, 4], stack=ph)
            B_ssq = Buf("ssqb")
            ptr = [ps(f"ptrb{i}", [128, KC, 128], BF16, stack=ph) for i in range(1)]
            B_ptr = [Buf(f"ptrb{i}") for i in range(1)]
            pa = [ps(f"pa{i}", [128, 4, 128], stack=ph) for i in range(3)]
            B_pa = [Buf(f"pa{i}") for i in range(3)]
            py = [ps(f"pyb{i}", [128, 512], stack=ph) for i in range(4)]
            B_py = [Buf(f"pyb{i}") for i in range(4)]
            rl = [sb(f"rl{i}", [128, 4, 128], stack=ph) for i in range(2)]
            B_rl = [Buf(f"rl{i}") for i in range(2)]
            aT = [sb(f"aT{i}", [128, 32, 128], BF16, stack=ph) for i in range(2)]
            B_aT = [Buf(f"aT{i}") for i in range(2)]
            tmp = [sb(f"tmpb{i}", [128, D], stack=ph) for i in range(2)]
            B_tmp = [Buf(f"tmpb{i}") for i in range(2)]
            xo, B_xo = xt, B_xt
            pai = 0
            for t in range(NT):
                i = t % 2
                mk.dma(SP, f"p3bx{i}", xt[i][:], xs_d[t * 128:(t + 1) * 128, :], reads=[B_xs[t]], writes=[B_xt[i]])
                mk.op(ACT, lambda e, i=i: e.activation(out=junk[:], in_=xt[i][:], func=AF.Square, accum_out=ssq[:, 0:1]),
                      reads=[B_xt[i]], writes=[B_junk, B_ssq])
                mk.op(ACT, lambda e: e.activation(out=ssq[:, 1:2], in_=ssq[:, 0:1], func=AF.Sqrt, bias=EPS * D),
                      reads=[B_ssq], writes=[B_ssq])
                mk.op(DVE, lambda e: e.reciprocal(out=ssq[:, 1:2], in_=ssq[:, 1:2]), reads=[B_ssq], writes=[B_ssq])
                mk.op(DVE, lambda e, i=i: e.scalar_tensor_tensor(out=h1[:], in0=xt[i][:], scalar=ssq[:, 1:2],
                                                                 in1=m2[:, D:2 * D], op0=ALU.mult, op1=ALU.mult),
                      reads=[B_xt[i], B_ssq, B_m2], writes=[B_h1])
                mk.op(POOL, lambda e, i=i: e.tensor_tensor(out=hb[i][:], in0=h1[:], in1=m2[:, 0:D], op=ALU.add),
                      reads=[B_h1, B_m2], writes=[B_hb[i]])
                for kc in range(KC):
                    mk.op(PE, lambda e, i=i, kc=kc: e.transpose(ptr[0][:, kc, :], hb[i][:, kc * 128:(kc + 1) * 128], ident_bf[:]),
                          reads=[B_hb[i], B_identbf], writes=[B_ptr[0]], signal=(kc == KC - 1))
                mk.op(ACT, lambda e, i=i: e.activation(out=hT2[i][:], in_=ptr[0][:], func=AF.Copy),
                      reads=[B_ptr[0]], writes=[B_hT2[i]])
                for fq in range(8):
                    pi = pai % 3
                    ri = pai % 2
                    pai += 1
                    for f4 in range(4):
                        fb = fq * 4 + f4
                        for kc in range(KC):
                            mk.op(PE, lambda e, i=i, kc=kc, fb=fb, f4=f4, pi=pi: e.matmul(
                                pa[pi][:, f4, :], lhsT=w1[:, kc, fb * 128:(fb + 1) * 128], rhs=hT2[i][:, kc, :],
                                start=(kc == 0), stop=(kc == KC - 1)),
                                reads=[B_w1, B_hT2[i]], writes=[B_pa[pi]], signal=(kc == KC - 1 and f4 == 3))
                    mk.op(ACT, lambda e, pi=pi, ri=ri: e.activation(out=rl[ri][:], in_=pa[pi][:], func=AF.Relu),
                          reads=[B_pa[pi]], writes=[B_rl[ri]])
                    mk.op(POOL, lambda e, i=i, ri=ri, fq=fq: e.tensor_tensor(
                        out=aT[i][:, fq * 4:(fq + 1) * 4, :], in0=rl[ri][:], in1=rl[ri][:], op=ALU.mult),
                        reads=[B_rl[ri]], writes=[B_aT[i]])
                for hf in range(2):
                    pi = i * 2 + hf
                    for fc in range(32):
                        mk.op(PE, lambda e, i=i, fc=fc, hf=hf, pi=pi: e.matmul(
                            py[pi][:], lhsT=aT[i][:, fc, :], rhs=w2[:, fc, hf * 512:(hf + 1) * 512],
                            start=(fc == 0), stop=(fc == 31)),
                            reads=[B_aT[i], B_w2], writes=[B_py[pi]], signal=(fc == 31))
                    mk.op(DVE, lambda e, i=i, hf=hf, pi=pi: e.tensor_tensor(
                        out=tmp[i][:, hf * 512:(hf + 1) * 512], in0=py[pi][:], in1=m2[:, 2 * D + hf * 512:2 * D + (hf + 1) * 512],
                        op=ALU.mult), reads=[B_py[pi], B_m2], writes=[B_tmp[i]])
                mk.op(POOL, lambda e, i=i: e.tensor_tensor(out=xo[i][:], in0=tmp[i][:], in1=xt[i][:], op=ALU.add),
                      reads=[B_tmp[i], B_xt[i]], writes=[B_xo[i]])
                if not last:
                    mk.dma(SP, f"p3bs{i}", xs_d[t * 128:(t + 1) * 128, :], xo[i][:], reads=[B_xo[i]], writes=[B_xs[t]])
                else:
                    mk.op(ACT, lambda e, i=i: e.activation(out=junk[:], in_=xo[i][:], func=AF.Square, accum_out=ssq[:, 2:3]),
                          reads=[B_xo[i]], writes=[B_junk, B_ssq])
                    mk.op(ACT, lambda e: e.activation(out=ssq[:, 3:4], in_=ssq[:, 2:3], func=AF.Sqrt, bias=EPS * D),
                          reads=[B_ssq], writes=[B_ssq])
                    mk.op(DVE, lambda e: e.reciprocal(out=ssq[:, 3:4], in_=ssq[:, 3:4]), reads=[B_ssq], writes=[B_ssq])
                    mk.op(DVE, lambda e, i=i: e.scalar_tensor_tensor(out=tmp[i][:], in0=xo[i][:], scalar=ssq[:, 3:4],
                                                                     in1=fn[:], op0=ALU.mult, op1=ALU.mult),
                          reads=[B_xo[i], B_ssq, B_fn], writes=[B_tmp[i]])
                    mk.dma(SP, f"p3bo{i}", out_d[t * 128:(t + 1) * 128, :], tmp[i][:], reads=[B_tmp[i]], writes=[B_out])
            if last:
                mk.final_wait(SP, [B_out])
            mk.flush()
            ck(f"p3b{l}")


def p2_mixers(nc, mk, ph, l, S, NT, g):
    PE, ACT, DVE, POOL, SP = mk.PE, mk.ACT, mk.DVE, mk.POOL, mk.SP
    consts, B_consts = g["consts"], g["B_consts"]
    ident_bf, B_identbf = g["ident_bf"], g["B_identbf"]
    ident = consts[:, C_ID:C_ID + 128]
    ones = consts[:, C_ONE:C_ONE + 128]
    maskA = consts[:, C_MA:C_MA + 128]
    strict = consts[:, C_ST:C_ST + 128]
    tri = consts[:, C_TRI:C_TRI + 128]
    sel0 = consts[:, C_S0:C_S0 + 128]
    sel1 = consts[:, C_S1:C_S1 + 128]

    def sb(name, shape, dt=F32):
        mk.uid += 1
        return ph.enter_context(nc.sbuf_tensor(f"{name}_u{mk.uid}", list(shape), dt))

    def ps(name, shape, dt=F32):
        mk.uid += 1
        return ph.enter_context(nc.psum_tensor(f"{name}_u{mk.uid}", list(shape), dt))

    st_l = mk.stream("p2l")
    st_s = mk.stream("p2s")
    st_k = mk.stream("p2k")
    expB = sb("expB", [128, H * 640])
    B_expB = Buf("expB")
    mk.dma(SP, st_k, expB[:], g["biasT_in"][l], writes=[B_expB])
    mk.op(ACT, lambda e: e.activation(out=expB[:], in_=expB[:], func=AF.Exp), writes=[B_expB])
    for h in range(H):
        mk.op(POOL, lambda e, h=h: e.tensor_tensor(out=expB[:, h * 640:(h + 1) * 640], in0=expB[:, h * 640:(h + 1) * 640],
                                                     in1=consts[:, C_AM:C_AM + 640], op=ALU.mult),
              reads=[B_consts], writes=[B_expB])
    hv = sb("hv", [128, 3 * H + 128])
    B_hv = Buf("hv")
    mk.dma(SP, st_k, hv[:, 0:H], g["dtb_in"][0:1, l * H:(l + 1) * H].partition_broadcast(128), writes=[B_hv])
    mk.dma(SP, st_k, hv[:, H:2 * H], g["alog_in"][0:1, l * H:(l + 1) * H].partition_broadcast(128), writes=[B_hv])
    mk.dma(SP, st_k, hv[:, 3 * H:3 * H + 128], g["gnorm_in"][0:1, l * HD:(l + 1) * HD].partition_broadcast(128),
           writes=[B_hv])
    mk.op(ACT, lambda e: e.activation(out=hv[:, 2 * H:3 * H], in_=hv[:, H:2 * H], func=AF.Exp), writes=[B_hv])
    mk.op(DVE, lambda e: e.tensor_scalar(out=hv[:, H:2 * H], in0=hv[:, 2 * H:3 * H], scalar1=-1.0, scalar2=None,
                                         op0=ALU.mult), writes=[B_hv])
    mk.op(DVE, lambda e: e.tensor_scalar(out=hv[:, 3 * H:3 * H + 128], in0=hv[:, 3 * H:3 * H + 128], scalar1=float(HD) ** 0.5,
                                         scalar2=None, op0=ALU.mult), writes=[B_hv])
    dtb = hv[:, 0:H]
    nA = hv[:, H:2 * H]
    gn = hv[:, 3 * H:3 * H + 128]

    zg = [sb(f"zg{i}", [128, 24, 128]) for i in range(2)]
    B_zgt = [Buf(f"zgt{i}") for i in range(2)]
    zq0 = sb("zq0", [128, H, 128], BF16)
    zq = [zq0, zq0]
    B_zq0 = Buf("zq0")
    B_zq = [B_zq0, B_zq0]
    kring = sb("kring", [128, 5, H, 128], BF16)
    B_kr = [Buf(f"kr{i}") for i in range(5)]
    vring = sb("vring", [128, 5, H, 132], BF16)
    B_vr = [Buf(f"vr{i}") for i in range(5)]
    zt = [sb(f"zt{i}", [128, 3088]) for i in range(2)]
    B_zt = [Buf(f"zt{i}") for i in range(2)]
    mtile = [sb(f"mtile{i}", [128, D], BF16) for i in range(2)]
    B_mtile = [Buf(f"mtile{i}") for i in range(2)]
    for s in range(5):
        mk.op(POOL, lambda e, s=s: e.memset(vring[:, s, :, 128:132], 1.0), writes=[B_vr[s]])

    gsc = [sb(f"gsc{i}", [128, 12 * H]) for i in range(2)]
    B_gsc = [Buf(f"gsc{i}") for i in range(2)]
    Gs = [sb(f"Gs{i}", [128, 3 * H]) for i in range(2)]
    B_Gs = [Buf(f"Gs{i}") for i in range(2)]
    pg = ps("pg", [128, 512])
    B_pgG = B_pgO = B_pgS = Buf("pg")
    psA = ps("psA", [128, 512])
    B_psA = Buf("psA")
    NQ = 12
    pq_t = [ps(f"pq{i}", [128, 4, 128]) for i in range(NQ // 4)]
    NBK = NQ // 4
    pq = [pq_t[i % NBK][:, (i // NBK) % 4, :] for i in range(NQ)]
    B_bank = [Buf(f"pqb{i}") for i in range(NBK)]
    B_pq = [B_bank[i % NBK] for i in range(NQ)]
    state = {"q": 0}

    def getq():
        i = state["q"] % NQ
        state["q"] += 1
        return pq[i], B_pq[i]

    qkb0 = sb("qkb0", [128, 24, 128], BF16)
    qkb = [qkb0, qkb0]
    B_qkb0 = Buf("qkb0")
    B_qkb = [B_qkb0, B_qkb0]
    NW = 1
    def hb(name, shape, dt=F32):
        t = [sb(f"{name}{i}", [128, H] + list(shape), dt) for i in range(NW)]
        b = [[Buf(f"{name}{i}_{h}") for h in range(H)] for i in range(NW)]
        return t, b
    grep_r = [sb(f"grepr{i}", [128, 128]) for i in range(2)]
    B_grep_r = [Buf(f"grepr{i}") for i in range(2)]
    eGr_r = [sb(f"eGrr{i}", [128, 128]) for i in range(2)]
    B_eGr_r = [Buf(f"eGrr{i}") for i in range(2)]
    egk, B_egk = hb("egk", [128], BF16)
    kdec, B_kdec = hb("kdec", [128], BF16)
    vb, B_vb = hb("vb", [128], BF16)
    tG, B_tG = hb("tG", [128])
    DT, B_DT = hb("DT", [128])
    attnT, B_attnT = hb("attnT", [128], BF16)
    X0, B_X0 = tG, B_tG
    Xa, B_Xa = hb("Xa", [128], BF16)
    XTa, B_XTa = hb("XTa", [128], BF16)
    Da, B_Da = hb("Da", [128], BF16)
    Db, B_Db = hb("Db", [128], BF16)
    DTa, B_DTa = hb("DTa", [128], BF16)
    DTb, B_DTb = hb("DTb", [128], BF16)
    C1m, B_C1m = hb("C1m", [128], BF16)
    C2m, B_C2m = hb("C2m", [128], BF16)
    C1T, B_C1T = hb("C1T", [128], BF16)
    C2T, B_C2T = hb("C2T", [128], BF16)
    Pa, B_Pa = hb("Pa", [128], BF16)
    Pb, B_Pb = hb("Pb", [128], BF16)
    PTa, B_PTa = hb("PTa", [128], BF16)
    PTb, B_PTb = hb("PTb", [128], BF16)
    Yb, B_Yb = hb("Yb", [128], BF16)
    Ypb, B_Ypb = hb("Ypb", [128], BF16)
    usb, B_usb = hb("usb", [128])
    wTb, B_wTb = hb("wTb", [128], BF16)
    qdT, B_qdT = hb("qdT", [128], BF16)
    vn, B_vn = hb("vn", [128], BF16)
    GW, B_GW = hb("GW", [128])
    t1, B_t1 = hb("t1", [128])
    mb, B_mb = hb("mb", [128])
    esb = [sb(f"esb{i}", [128, 640]) for i in range(2)]
    B_esb = [Buf(f"esb{i}") for i in range(2)]
    PT = [sb(f"PT{i}", [128, 640], BF16) for i in range(2)]
    B_PT = [Buf(f"PT{i}") for i in range(2)]
    sm, B_sm = hb("sm", [4])
    S32 = sb("S32", [128, H, 128])
    B_S32 = [Buf(f"S32_{h}") for h in range(H)]
    Sb = sb("Sb", [128, H, 3, 128], BF16)
    B_Sb = [[Buf(f"Sb{h}_{s}") for s in range(3)] for h in range(H)]
    mk.op(POOL, lambda e: e.memset(S32[:], 0.0), writes=B_S32)
    mk.op(POOL, lambda e: e.memset(Sb[:], 0.0), writes=[b for r in B_Sb for b in r])
    ptk = ps("ptk", [128, 8, 128], BF16)
    ptv = ps("ptv", [128, 8, 128], BF16)
    B_ptk, B_ptv = Buf("ptk"), Buf("ptv")
    ptb = ps("ptb", [128, 8, 128], BF16)
    B_ptb1 = Buf("ptb")
    B_ptb = [B_ptb1 for h in range(H)]

    zgv = g["zg_d"].rearrange("(g d) s -> d g s", d=128)
    zqv = g["zqk_d"].rearrange("(g d) s -> d g s", d=128)
    vav = g["va_d"].rearrange("s (h d) -> s h d", d=128)

    def do_tile(t):
        i = t % 2
        w = 0
        sl = t % 5
        ts = slice(t * 128, (t + 1) * 128)
        mk.dma(SP, f"p2zg{i}", zg[i][:], zgv[:, :, ts], reads=[g["B_zg"]], writes=[B_zgt[i]])
        mk.dma(SP, "p2zq", zq[i][:], zqv[:, 0:H, ts], reads=[g["B_zqk"]], writes=[B_zq[i]])
        mk.dma(SP, f"p2kr{sl}", kring[:, sl, :, :], zqv[:, H:2 * H, ts], reads=[g["B_zqk"]], writes=[B_kr[sl]])
        mk.dma(SP, f"p2vr{sl}", vring[:, sl, :, 0:128], vav[ts, :, :], reads=[g["B_va"]], writes=[B_vr[sl]])
        mk.dma(SP, f"p2zt{i}", zt[i][:], g["ztok_d"][ts, :], reads=[g["B_ztok"]], writes=[B_zt[i]])
        if DBG["p2_stage"] <= 1:
            return
        gs = gsc[i]
        a_ap = zt[i][:, 1024:1024 + H]
        b_ap = zt[i][:, 1024 + H:1024 + 2 * H]
        bet, nbet, xa, ax, ee, lg, gg, eG, dl, kd = [gs[:, k * H:(k + 1) * H] for k in range(10)]
        cdb = gs[:, 10 * H:12 * H]
        BG = B_gsc[i]
        mk.op(ACT, lambda e: e.activation(out=bet, in_=b_ap, func=AF.Sigmoid), reads=[B_zt[i]], writes=[BG])
        mk.op(DVE, lambda e: e.tensor_scalar(out=nbet, in0=bet, scalar1=-1.0, scalar2=None, op0=ALU.mult), writes=[BG])
        mk.op(DVE, lambda e: e.tensor_tensor(out=xa, in0=a_ap, in1=dtb, op=ALU.add), reads=[B_zt[i], B_hv], writes=[BG])
        mk.op(ACT, lambda e: e.activation(out=ax, in_=xa, func=AF.Abs), writes=[BG])
        mk.op(ACT, lambda e: e.activation(out=ee, in_=ax, func=AF.Exp, scale=-1.0), writes=[BG])
        mk.op(ACT, lambda e: e.activation(out=lg, in_=ee, func=AF.Ln, bias=1.0), writes=[BG])
        mk.op(DVE, lambda e: e.scalar_tensor_tensor(out=lg, in0=xa, scalar=0.0, in1=lg, op0=ALU.max, op1=ALU.add),
              writes=[BG])
        mk.op(DVE, lambda e: e.tensor_tensor(out=gg, in0=lg, in1=nA, op=ALU.mult), reads=[B_hv], writes=[BG])
        pG = pg[:, 0:3 * H]
        if DBG.get("skipG"):
            mk.op(POOL, lambda e: e.memset(Gs[i][:], 0.0), writes=[B_Gs[i]])
        else:
            mk.op(PE, lambda e: e.matmul(pG[:, 0:H], lhsT=tri, rhs=gg, start=True, stop=True),
                  reads=[BG, B_consts], writes=[B_pgG], signal=False)
            mk.op(PE, lambda e: e.matmul(pG[:, H:2 * H], lhsT=sel0, rhs=gg, start=True, stop=True),
                  reads=[BG, B_consts], writes=[B_pgG], signal=False)
            mk.op(PE, lambda e: e.matmul(pG[:, 2 * H:3 * H], lhsT=sel1, rhs=gg, start=True, stop=True),
                  reads=[BG, B_consts], writes=[B_pgG])
            mk.op(ACT, lambda e: e.activation(out=Gs[i][:], in_=pG, func=AF.Copy), reads=[B_pgG], writes=[B_Gs[i]])
        Gc = Gs[i][:, 0:H]
        mk.op(ACT, lambda e: e.activation(out=eG, in_=Gc, func=AF.Exp), reads=[B_Gs[i]], writes=[BG])
        mk.op(DVE, lambda e: e.tensor_tensor(out=dl[0:64, :], in0=Gs[i][0:64, H:2 * H], in1=Gs[i][0:64, 0:H],
                                             op=ALU.subtract), reads=[B_Gs[i]], writes=[BG])
        mk.op(DVE, lambda e: e.tensor_tensor(out=dl[64:128, :], in0=Gs[i][64:128, 2 * H:3 * H], in1=Gs[i][64:128, 0:H],
                                             op=ALU.subtract), reads=[B_Gs[i]], writes=[BG])
        mk.op(ACT, lambda e: e.activation(out=kd, in_=dl, func=AF.Exp), writes=[BG])
        mk.op(ACT, lambda e: e.activation(out=cdb, in_=Gs[i][:, H:3 * H], func=AF.Exp), reads=[B_Gs[i]], writes=[BG])
        mk.op(ACT, lambda e: e.activation(out=qkb[i][:], in_=zg[i][:], func=AF.Copy),
              reads=[B_zgt[i]], writes=[B_qkb[i]])
        if DBG["p2_stage"] <= 2:
            return

        m_lo = max(0, t - 4)
        mis = [m - (t - 4) for m in range(m_lo, t + 1)]
        mlo = mis[0]
        sc = HD ** -0.5

        def s_att(h):
            e2 = (t * H + h) % 2
            for mi in mis:
                m = t - 4 + mi
                dst = psA[:, mi * 128:(mi + 1) * 128] if mi < 4 else pg[:, 384:512]
                Bd = B_psA if mi < 4 else B_pgS
                mk.op(PE, lambda e, m=m, dst=dst: e.matmul(dst, lhsT=kring[:, m % 5, h, :], rhs=zq[i][:, h, :],
                                                            start=True, stop=True),
                      reads=[B_kr[m % 5], B_zq[i]], writes=[Bd], signal=(mi == mis[-1] or mi == 3))
            if DBG["p2_stage"] <= 2.1:
                return
            if mlo < 4:
                mk.op(ACT, lambda e: e.activation(out=esb[e2][:, mlo * 128:512], in_=psA[:, mlo * 128:512],
                                                  func=AF.Exp, scale=sc),
                      reads=[B_psA], writes=[B_esb[e2]])
            mk.op(ACT, lambda e: e.activation(out=esb[e2][:, 512:640], in_=pg[:, 384:512], func=AF.Exp, scale=sc),
                  reads=[B_pgS], writes=[B_esb[e2]])
            if DBG["p2_stage"] <= 2.2:
                return
            mk.op(POOL, lambda e: e.tensor_tensor(
                out=PT[e2][:, mlo * 128:640], in0=esb[e2][:, mlo * 128:640],
                in1=expB[:, h * 640 + mlo * 128:(h + 1) * 640], op=ALU.mult),
                reads=[B_esb[e2], B_expB], writes=[B_PT[e2]])
            if DBG["p2_stage"] <= 2.3:
                return
            for mi in mis:
                m = t - 4 + mi
                mk.op(PE, lambda e, m=m, mi=mi: e.matmul(pg[:, 128:258], lhsT=PT[e2][:, mi * 128:(mi + 1) * 128],
                                                         rhs=vring[:, m % 5, h, 0:130],
                                                         start=(mi == mis[0]), stop=(mi == mis[-1])),
                      reads=[B_PT[e2], B_vr[m % 5]], writes=[B_pgO], signal=(mi == mis[-1]))
            if DBG["p2_stage"] <= 2.4:
                return
            mk.op(DVE, lambda e: e.reciprocal(out=sm[w][:, h, 0:1], in_=pg[:, 256:257]),
                  reads=[B_pgO], writes=[B_sm[w][h]])
            gb_ap = zt[i][:, 1040 + 1024 + h * 128:1040 + 1024 + (h + 1) * 128]
            mk.op(DVE, lambda e: e.scalar_tensor_tensor(
                out=mb[w][:, h, :], in0=pg[:, 128:256], scalar=sm[w][:, h, 0:1], in1=gb_ap,
                op0=ALU.mult, op1=ALU.mult), reads=[B_pgO, B_sm[w][h], B_zt[i]], writes=[B_mb[w][h]])
        for h in range(H):
            s_att(h)
        if DBG["p2_stage"] <= 3:
            return

        def stage(fn):
            for h in range(H):
                fn(h)

        hq = {}

        def s_pre(h):
            pk, Bpk = ptk[:, h, :], B_ptk
            mk.op(PE, lambda e: e.transpose(pk, qkb[i][:, 8 + h, :], ident_bf[:]), reads=[B_qkb[i], B_identbf], writes=[Bpk])
            mk.op(ACT, lambda e: e.activation(out=egk[w][:, h, :], in_=pk, func=AF.Copy, scale=eG[:, h:h + 1]),
                  reads=[Bpk, BG], writes=[B_egk[w][h]])
            mk.op(DVE, lambda e: e.tensor_scalar(out=kdec[w][:, h, :], in0=pk, scalar1=kd[:, h:h + 1], scalar2=None,
                                                 op0=ALU.mult), reads=[Bpk, BG], writes=[B_kdec[w][h]])
            pv, Bpv = ptv[:, h, :], B_ptv
            mk.op(PE, lambda e: e.transpose(pv, qkb[i][:, 16 + h, :], ident_bf[:]), reads=[B_qkb[i], B_identbf], writes=[Bpv])
            mk.op(ACT, lambda e: e.activation(out=vb[w][:, h, :], in_=pv, func=AF.Copy), reads=[Bpv], writes=[B_vb[w][h]])
            gr, Bgr = grep_r[h % 2], B_grep_r[h % 2]
            er, Ber = eGr_r[h % 2], B_eGr_r[h % 2]
            mk.op(POOL, lambda e: e.tensor_scalar(out=gr[:], in0=ones, scalar1=gg[:, h:h + 1], scalar2=None,
                                                  op0=ALU.mult), reads=[BG, B_consts], writes=[Bgr])
            pgr, Bpgr = getq()
            mk.op(PE, lambda e: e.matmul(pgr, lhsT=gr[:], rhs=tri, start=True, stop=True),
                  reads=[Bgr, B_consts], writes=[Bpgr])
            mk.op(DVE, lambda e: e.scalar_tensor_tensor(out=tG[w][:, h, :], in0=pgr, scalar=Gs[i][:, h:h + 1], in1=maskA,
                                                        op0=ALU.subtract, op1=ALU.add),
                  reads=[Bpgr, B_Gs[i], B_consts], writes=[B_tG[w][h]])
            mk.op(ACT, lambda e: e.activation(out=DT[w][:, h, :], in_=tG[w][:, h, :], func=AF.Exp),
                  reads=[B_tG[w][h]], writes=[B_DT[w][h]])
            mk.op(ACT, lambda e: e.activation(out=er[:], in_=pgr, func=AF.Exp), reads=[Bpgr], writes=[Ber])
            mk.op(POOL, lambda e: e.tensor_tensor(out=qdT[w][:, h, :], in0=zg[i][:, h, :], in1=er[:], op=ALU.mult),
                  reads=[B_zgt[i], Ber], writes=[B_qdT[w][h]])
            pkk, Bpkk = getq()
            mk.op(PE, lambda e: e.matmul(pkk, lhsT=qkb[i][:, 8 + h, :], rhs=qkb[i][:, 8 + h, :], start=True, stop=True),
                  reads=[B_qkb[i]], writes=[Bpkk])
            pat, Bpat = getq()
            mk.op(PE, lambda e: e.matmul(pat, lhsT=qkb[i][:, 8 + h, :], rhs=qkb[i][:, h, :], start=True, stop=True),
                  reads=[B_qkb[i]], writes=[Bpat])
            mk.op(DVE, lambda e: e.tensor_tensor(out=attnT[w][:, h, :], in0=pat, in1=DT[w][:, h, :], op=ALU.mult),
                  reads=[Bpat, B_DT[w][h]], writes=[B_attnT[w][h]])
            mk.op(DVE, lambda e: e.scalar_tensor_tensor(out=X0[w][:, h, :], in0=pkk, scalar=nbet[:, h:h + 1],
                                                        in1=DT[w][:, h, :], op0=ALU.mult, op1=ALU.mult),
                  reads=[Bpkk, BG, B_DT[w][h]], writes=[B_X0[w][h]])
            def msk(dst, Bd, srcb, Bs, col):
                mk.op(POOL, lambda e: e.tensor_tensor(out=dst[w][:, h, :], in0=srcb[w][:, h, :],
                                                      in1=consts[:, col:col + 128], op=ALU.mult),
                      reads=[Bs[w][h], B_consts], writes=[Bd[w][h]])
            msk(Xa, B_Xa, X0, B_X0, C_ST)
            msk(Da, B_Da, X0, B_X0, C_MD)
            msk(C1m, B_C1m, X0, B_X0, C_MC1)
            msk(C2m, B_C2m, X0, B_X0, C_MC2)
            mk.op(PE, lambda e: e.transpose(ptb[:, h, :], Xa[w][:, h, :], ident_bf[:]),
                  reads=[B_Xa[w][h], B_identbf], writes=[B_ptb[h]])
            mk.op(ACT, lambda e: e.activation(out=XTa[w][:, h, :], in_=ptb[:, h, :], func=AF.Copy),
                  reads=[B_ptb[h]], writes=[B_XTa[w][h]])
            msk(DTa, B_DTa, XTa, B_XTa, C_MDT)
            msk(C1T, B_C1T, XTa, B_XTa, C_MC1T)
            msk(C2T, B_C2T, XTa, B_XTa, C_MC2T)
            mk.op(POOL, lambda e: e.tensor_tensor(out=Pa[w][:, h, :], in0=Da[w][:, h, :], in1=ident_bf[:], op=ALU.add),
                  reads=[B_Da[w][h], B_identbf], writes=[B_Pa[w][h]])
            mk.op(POOL, lambda e: e.tensor_tensor(out=PTa[w][:, h, :], in0=DTa[w][:, h, :], in1=ident_bf[:], op=ALU.add),
                  reads=[B_DTa[w][h], B_identbf], writes=[B_PTa[w][h]])
            hq[h] = dict(D=(Da, B_Da), DT=(DTa, B_DTa), Dn=(Db, B_Db), DTn=(DTb, B_DTb),
                         P=(Pa, B_Pa), Pn=(Pb, B_Pb), PT=(PTa, B_PTa), PTn=(PTb, B_PTb))

        stage(s_pre)
        if DBG["p2_stage"] <= 4:
            return

        if DBG.get("ginv", 1):
            def getbank():
                b = state["q"] % NBK
                state["q"] += 1
                return pq_t[b], B_bank[b]

            def grp(buf, B, hg):
                return buf[w][:, 4 * hg:4 * hg + 4, :], [B[w][4 * hg + q] for q in range(4)]

            def gmm(hg, L_, BL, R_, BR):
                pb, Bb = getbank()
                for q in range(4):
                    h = 4 * hg + q
                    mk.op(PE, lambda e, h=h, q=q: e.matmul(pb[:, q, :], lhsT=L_[w][:, h, :], rhs=R_[w][:, h, :], start=True, stop=True),
                          reads=[BL[w][h], BR[w][h]], writes=[Bb], signal=(q == 3))
                return pb, Bb

            def gevac(E, dst, Bd, hg, pb, Bb):
                o_, Bo = grp(dst, Bd, hg)
                if E is ACT:
                    mk.op(ACT, lambda e: e.activation(out=o_, in_=pb[:], func=AF.Copy), reads=[Bb], writes=Bo)
                else:
                    mk.op(DVE, lambda e: e.tensor_copy(out=o_, in_=pb[:]), reads=[Bb], writes=Bo)

            def gadd(dst, Bd, base, Bbase, hg, pb, Bb):
                o_, Bo = grp(dst, Bd, hg)
                b_, Bbs = grp(base, Bbase, hg)
                mk.op(DVE, lambda e: e.tensor_tensor(out=o_, in0=pb[:], in1=b_, op=ALU.add), reads=[Bb] + Bbs, writes=Bo)

            d = hq[0]
            for lev in range(1, 4):
                (Dm, BD), (DT_, BDT), (Dn, BDn), (DTn, BDTn) = d["D"], d["DT"], d["Dn"], d["DTn"]
                (Px, BPx), (Pnx, BPnx), (PTx, BPTx), (PTnx, BPTnx) = d["P"], d["Pn"], d["PT"], d["PTn"]
                for hg in range(2):
                    pb, Bb = gmm(hg, Dm, BD, DT_, BDT)
                    gevac(ACT, DTn, BDTn, hg, pb, Bb)
                if lev < 3:
                    for hg in range(2):
                        pb, Bb = gmm(hg, DT_, BDT, Dm, BD)
                        gevac(DVE, Dn, BDn, hg, pb, Bb)
                for hg in range(2):
                    pb, Bb = gmm(hg, DTn, BDTn, Px, BPx)
                    gadd(Pnx, BPnx, Px, BPx, hg, pb, Bb)
                for hg in range(2):
                    pb, Bb = gmm(hg, Px, BPx, DTn, BDTn)
                    gadd(PTnx, BPTnx, PTx, BPTx, hg, pb, Bb)
                d["D"], d["Dn"] = d["Dn"], d["D"]
                d["DT"], d["DTn"] = d["DTn"], d["DT"]
                d["P"], d["Pn"] = d["Pn"], d["P"]
                d["PT"], d["PTn"] = d["PTn"], d["PT"]
            (Px, BPx), (Pnx, BPnx), (PTx, BPTx), (PTnx, BPTnx) = d["P"], d["Pn"], d["PT"], d["PTn"]
            for hg in range(2):
                pb, Bb = gmm(hg, C1T, B_C1T, Px, BPx)
                gevac(ACT, Yb, B_Yb, hg, pb, Bb)
            for hg in range(2):
                pb, Bb = gmm(hg, C1m, B_C1m, PTx, BPTx)
                gevac(ACT, Ypb, B_Ypb, hg, pb, Bb)
            for hg in range(2):
                pb, Bb = gmm(hg, PTx, BPTx, Yb, B_Yb)
                gadd(Pnx, BPnx, Px, BPx, hg, pb, Bb)
            for hg in range(2):
                pb, Bb = gmm(hg, Px, BPx, Ypb, B_Ypb)
                gadd(PTnx, BPTnx, PTx, BPTx, hg, pb, Bb)
            d["P"], d["Pn"] = d["Pn"], d["P"]
            d["PT"], d["PTn"] = d["PTn"], d["PT"]
            (Px, BPx), (Pnx, BPnx), (PTx, BPTx) = d["P"], d["Pn"], d["PT"]
            for hg in range(2):
                pb, Bb = gmm(hg, C2T, B_C2T, Px, BPx)
                gevac(ACT, Yb, B_Yb, hg, pb, Bb)
            for hg in range(2):
                pb, Bb = gmm(hg, PTx, BPTx, Yb, B_Yb)
                gadd(Pnx, BPnx, Px, BPx, hg, pb, Bb)
            d["P"], d["Pn"] = d["Pn"], d["P"]
            for h in range(H):
                hq[h] = d

            if DBG["p2_stage"] <= 5:
                return

            (Px, BPx) = hq[0]["P"]
            for hg in range(2):
                pb, Bb = gmm(hg, Px, BPx, vb, B_vb)
                for q in range(4):
                    h = 4 * hg + q
                    mk.op(ACT, lambda e, h=h, q=q, pb=pb: e.activation(out=usb[w][:, h, :], in_=pb[:, q, :], func=AF.Copy,
                                                                     scale=bet[:, h:h + 1]),
                          reads=[Bb, BG], writes=[B_usb[w][h]])
            for hg in range(2):
                pb, Bb = gmm(hg, egk, B_egk, Px, BPx)
                gevac(ACT, wTb, B_wTb, hg, pb, Bb)
        else:
            def evac(E, dst, Bd, psrc, Bp):
                if E is ACT:
                    mk.op(ACT, lambda e: e.activation(out=dst, in_=psrc, func=AF.Copy), reads=[Bp], writes=[Bd])
                else:
                    mk.op(DVE, lambda e: e.tensor_copy(out=dst, in_=psrc), reads=[Bp], writes=[Bd])

            def mm(lhsT, Bl, rhs, Br):
                p_, Bp_ = getq()
                mk.op(PE, lambda e: e.matmul(p_, lhsT=lhsT, rhs=rhs, start=True, stop=True), reads=[Bl, Br], writes=[Bp_])
                return p_, Bp_

            def addto(dst, Bd, p_, Bp_, base, Bb):
                mk.op(DVE, lambda e: e.tensor_tensor(out=dst, in0=p_, in1=base, op=ALU.add), reads=[Bp_, Bb], writes=[Bd])

            for lev in range(1, 4):
                def s_sq(h, lev=lev):
                    d = hq[h]
                    (Dm, BD), (DT_, BDT), (Dn, BDn), (DTn, BDTn) = d["D"], d["DT"], d["Dn"], d["DTn"]
                    p1, Bp1 = mm(Dm[w][:, h, :], BD[w][h], DT_[w][:, h, :], BDT[w][h])
                    evac(ACT, DTn[w][:, h, :], BDTn[w][h], p1, Bp1)
                    if lev < 3:
                        p2, Bp2 = mm(DT_[w][:, h, :], BDT[w][h], Dm[w][:, h, :], BD[w][h])
                        evac(DVE, Dn[w][:, h, :], BDn[w][h], p2, Bp2)
                stage(s_sq)

                def s_p(h, lev=lev):
                    d = hq[h]
                    (DTn, BDTn), (P, BP), (Pn, BPn), (PT, BPT), (PTn, BPTn) = d["DTn"], d["P"], d["Pn"], d["PT"], d["PTn"]
                    p3, Bp3 = mm(DTn[w][:, h, :], BDTn[w][h], P[w][:, h, :], BP[w][h])
                    addto(Pn[w][:, h, :], BPn[w][h], p3, Bp3, P[w][:, h, :], BP[w][h])
                    p4, Bp4 = mm(P[w][:, h, :], BP[w][h], DTn[w][:, h, :], BDTn[w][h])
                    addto(PTn[w][:, h, :], BPTn[w][h], p4, Bp4, PT[w][:, h, :], BPT[w][h])
                    d["D"], d["Dn"] = d["Dn"], d["D"]
                    d["DT"], d["DTn"] = d["DTn"], d["DT"]
                    d["P"], d["Pn"] = d["Pn"], d["P"]
                    d["PT"], d["PTn"] = d["PTn"], d["PT"]
                stage(s_p)

            def s_m1(h):
                d = hq[h]
                (P, BP), (Pn, BPn), (PT, BPT), (PTn, BPTn) = d["P"], d["Pn"], d["PT"], d["PTn"]
                py, Bpy = mm(C1T[w][:, h, :], B_C1T[w][h], P[w][:, h, :], BP[w][h])
                evac(ACT, Yb[w][:, h, :], B_Yb[w][h], py, Bpy)
                py2, Bpy2 = mm(C1m[w][:, h, :], B_C1m[w][h], PT[w][:, h, :], BPT[w][h])
                evac(ACT, Ypb[w][:, h, :], B_Ypb[w][h], py2, Bpy2)
                pz, Bpz = mm(PT[w][:, h, :], BPT[w][h], Yb[w][:, h, :], B_Yb[w][h])
                addto(Pn[w][:, h, :], BPn[w][h], pz, Bpz, P[w][:, h, :], BP[w][h])
                pz2, Bpz2 = mm(P[w][:, h, :], BP[w][h], Ypb[w][:, h, :], B_Ypb[w][h])
                addto(PTn[w][:, h, :], BPTn[w][h], pz2, Bpz2, PT[w][:, h, :], BPT[w][h])
                d["P"], d["Pn"] = d["Pn"], d["P"]
                d["PT"], d["PTn"] = d["PTn"], d["PT"]
            stage(s_m1)

            def s_m2(h):
                d = hq[h]
                (P, BP), (Pn, BPn), (PT, BPT) = d["P"], d["Pn"], d["PT"]
                py, Bpy = mm(C2T[w][:, h, :], B_C2T[w][h], P[w][:, h, :], BP[w][h])
                evac(ACT, Yb[w][:, h, :], B_Yb[w][h], py, Bpy)
                pz, Bpz = mm(PT[w][:, h, :], BPT[w][h], Yb[w][:, h, :], B_Yb[w][h])
                addto(Pn[w][:, h, :], BPn[w][h], pz, Bpz, P[w][:, h, :], BP[w][h])
                d["P"], d["Pn"] = d["Pn"], d["P"]
            stage(s_m2)

            if DBG["p2_stage"] <= 5:
                return

            def s_uw(h):
                (P, BP) = hq[h]["P"]
                pu, Bpu = getq()
                mk.op(PE, lambda e: e.matmul(pu, lhsT=P[w][:, h, :], rhs=vb[w][:, h, :], start=True, stop=True),
                      reads=[BP[w][h], B_vb[w][h]], writes=[Bpu])
                mk.op(ACT, lambda e: e.activation(out=usb[w][:, h, :], in_=pu, func=AF.Copy, scale=bet[:, h:h + 1]),
                      reads=[Bpu, BG], writes=[B_usb[w][h]])
                pw, Bpw = getq()
                mk.op(PE, lambda e: e.matmul(pw, lhsT=egk[w][:, h, :], rhs=P[w][:, h, :], start=True, stop=True),
                      reads=[BP[w][h], B_egk[w][h]], writes=[Bpw])
                mk.op(ACT, lambda e: e.activation(out=wTb[w][:, h, :], in_=pw, func=AF.Copy), reads=[Bpw], writes=[B_wTb[w][h]])
            stage(s_uw)
        if DBG["p2_stage"] <= 6:
            return

        pws = {}
        for c in range(2):
            n = 2 * t + c
            cur, nxt = n % 3, (n + 1) % 3
            rs = slice(c * 64, (c + 1) * 64)

            def s_ws(h, c=c, cur=cur, rs=rs):
                if c == 0:
                    pws[h] = getq()
                pw_, Bpw_ = pws[h]
                mk.op(PE, lambda e: e.matmul(pw_[rs, :], lhsT=wTb[w][:, h, rs], rhs=Sb[:, h, cur, :], start=True, stop=True),
                      reads=[B_wTb[w][h], B_Sb[h][cur]], writes=[Bpw_])
                mk.op(DVE, lambda e: e.scalar_tensor_tensor(out=vn[w][rs, h, :], in0=pw_[rs, :], scalar=nbet[rs, h:h + 1],
                                                            in1=usb[w][rs, h, :], op0=ALU.mult, op1=ALU.add),
                      reads=[Bpw_, BG, B_usb[w][h]], writes=[B_vn[w][h]])
            stage(s_ws)

            def s_ds(h, c=c, nxt=nxt, rs=rs):
                pd, Bpd = getq()
                mk.op(PE, lambda e: e.matmul(pd, lhsT=kdec[w][rs, h, :], rhs=vn[w][rs, h, :], start=True, stop=True),
                      reads=[B_kdec[w][h], B_vn[w][h]], writes=[Bpd])
                mk.op(DVE, lambda e: e.scalar_tensor_tensor(out=S32[:, h, :], in0=S32[:, h, :],
                                                            scalar=cdb[:, c * H + h:c * H + h + 1], in1=pd,
                                                            op0=ALU.mult, op1=ALU.add),
                      reads=[Bpd, BG], writes=[B_S32[h]])
                mk.op(ACT, lambda e: e.activation(out=Sb[:, h, nxt, :], in_=S32[:, h, :], func=AF.Copy),
                      reads=[B_S32[h]], writes=[B_Sb[h][nxt]])
            stage(s_ds)
        if DBG["p2_stage"] <= 7:
            return

        def s_out(h):
            po, Bpo = getq()
            s0, s1 = (2 * t) % 3, (2 * t + 1) % 3
            mk.op(PE, lambda e: e.matmul(po[0:64, :], lhsT=qdT[w][:, h, 0:64], rhs=Sb[:, h, s0, :], start=True, stop=False,
                                         skip_group_check=True),
                  reads=[B_qdT[w][h], B_Sb[h][s0]], writes=[Bpo], signal=False)
            mk.op(PE, lambda e: e.matmul(po[64:128, :], lhsT=qdT[w][:, h, 64:128], rhs=Sb[:, h, s1, :], start=True, stop=False,
                                         skip_group_check=True),
                  reads=[B_qdT[w][h], B_Sb[h][s1]], writes=[Bpo], signal=False)
            mk.op(PE, lambda e: e.matmul(po, lhsT=attnT[w][:, h, :], rhs=vn[w][:, h, :], start=False, stop=True,
                                         skip_group_check=True),
                  reads=[B_attnT[w][h], B_vn[w][h]], writes=[Bpo])
            mk.op(ACT, lambda e: e.activation(out=t1[w][:, h, :], in_=po, func=AF.Square, accum_out=sm[w][:, h, 1:2]),
                  reads=[Bpo], writes=[B_t1[w][h], B_sm[w][h]])
            mk.op(ACT, lambda e: e.activation(out=sm[w][:, h, 2:3], in_=sm[w][:, h, 1:2], func=AF.Sqrt, bias=EPS * HD),
                  writes=[B_sm[w][h]])
            mk.op(DVE, lambda e: e.reciprocal(out=sm[w][:, h, 2:3], in_=sm[w][:, h, 2:3]), writes=[B_sm[w][h]])
            ga_ap = zt[i][:, 1040 + h * 128:1040 + (h + 1) * 128]
            mk.op(POOL, lambda e: e.tensor_tensor(out=GW[w][:, h, :], in0=zt[i][:, h * 128:(h + 1) * 128], in1=gn, op=ALU.mult),
                  reads=[B_zt[i], B_hv], writes=[B_GW[w][h]])
            mk.op(POOL, lambda e: e.tensor_tensor(out=GW[w][:, h, :], in0=GW[w][:, h, :], in1=ga_ap, op=ALU.mult),
                  reads=[B_zt[i]], writes=[B_GW[w][h]])
            mk.op(DVE, lambda e: e.scalar_tensor_tensor(out=t1[w][:, h, :], in0=po, scalar=sm[w][:, h, 2:3], in1=GW[w][:, h, :],
                                                        op0=ALU.mult, op1=ALU.mult),
                  reads=[Bpo, B_sm[w][h], B_GW[w][h]], writes=[B_t1[w][h]])
            mk.op(POOL, lambda e: e.tensor_tensor(out=mtile[i][:, h * 128:(h + 1) * 128], in0=t1[w][:, h, :], in1=mb[w][:, h, :],
                                                  op=ALU.add),
                  reads=[B_t1[w][h], B_mb[w][h]], writes=[B_mtile[i]])
        stage(s_out)
        mk.dma(SP, f"p2m{i}", g["m_d"][ts, :], mtile[i][:], reads=[B_mtile[i]], writes=[g["B_m"]])

    for t in range(NT):
        do_tile(t)


def make_consts():
    c = np.zeros((128, C_N), np.float32)
    p = np.arange(128)[:, None]
    f = np.arange(128)[None, :]
    same = (p // 64) == (f // 64)
    c[:, C_ID:C_ID + 128] = np.eye(128)
    c[:, C_ONE:C_ONE + 128] = 1.0
    c[:, C_MA:C_MA + 128] = np.where(same & (f >= p), 0.0, NEG)
    c[:, C_ST:C_ST + 128] = (same & (f > p))
    c[:, C_TRI:C_TRI + 128] = (same & (p <= f))
    kl = np.arange(128)[:, None, None]
    mi = np.arange(5)[None, :, None]
    ql = np.arange(128)[None, None, :]
    dch = 2 * (mi - 4) + kl // 64 - ql // 64
    c[:, C_AM:C_AM + 640] = ((dch >= -8) & (dch <= 0)).reshape(128, 640)
    st = same & (f > p)
    bd16 = (p // 16) == (f // 16)
    bd32 = (p // 32) == (f // 32)
    md, mc1, mc2 = st & bd16, st & bd32 & ~bd16, st & ~bd32
    c[:, C_MD:C_MD + 128] = md
    c[:, C_MC1:C_MC1 + 128] = mc1
    c[:, C_MC2:C_MC2 + 128] = mc2
    c[:, C_MDT:C_MDT + 128] = md.T
    c[:, C_MC1T:C_MC1T + 128] = mc1.T
    c[:, C_MC2T:C_MC2T + 128] = mc2.T
    c[:, C_S0:C_S0 + 128] = (p < 64)
    c[:, C_S1:C_S1 + 128] = (p >= 64)
    return c


def layout_inputs(x, c, w_ada, b_ada, norm_mix, norm_mlp, w_in, conv_w, a_log, dt_bias,
                  gdn_norm, rel_bias, w_out, w_ff_in, w_ff_out, final_norm):
    f = lambda a: np.ascontiguousarray(np.asarray(a, dtype=np.float32))
    L = w_ada.shape[0]
    kl = np.arange(128)[:, None, None]
    mi = np.arange(5)[None, :, None]
    ql = np.arange(128)[None, None, :]
    idx = np.clip(ql + 128 * (4 - mi) - kl, -256, 256) + 256
    rb = np.asarray(rel_bias, np.float32)
    biasT = rb[:, :, idx]
    biasT = f(biasT.transpose(0, 2, 1, 3, 4).reshape(L, 128, H * 640))
    cw = np.asarray(conv_w, np.float32).reshape(L, 4, 24, 128).transpose(3, 0, 2, 1).reshape(128, L * 96)
    shared = dict(
        w_ada=f(w_ada), b_ada=f(b_ada), norm_mix=f(norm_mix), norm_mlp=f(norm_mlp), w_in=f(w_in),
        cw=f(cw), a_log=f(np.asarray(a_log).reshape(1, -1)), dt_bias=f(np.asarray(dt_bias).reshape(1, -1)),
        gdn_norm=f(np.asarray(gdn_norm).reshape(1, -1)), biasT=biasT, w_out=f(w_out), w_ff_in=f(w_ff_in),
        w_ff_out=f(w_ff_out), final_norm=f(np.asarray(final_norm).reshape(1, -1)), consts=make_consts())
    x = np.asarray(x, np.float32)
    c = np.asarray(c, np.float32)
    per = []
    for b in range(x.shape[0]):
        d = dict(shared)
        d["x"] = f(x[b])
        d["cT"] = f(c[b].reshape(KC, 128).T)
        per.append(d)
    return per


def kernel(**inputs):
    x = np.asarray(inputs["x"])
    B, S, _ = x.shape
    L = np.asarray(inputs["w_ada"]).shape[0]
    per = layout_inputs(**inputs)
    nc, mk = build_program(S, L)
    n = B
    in_maps = [per[j] for j in range(n)]
    res = run_bass_kernel_spmd(nc, in_maps, core_ids=list(range(n)))
    out = np.stack([np.asarray(res.results[b]["out"], np.float32) for b in range(B)], axis=0)
    return out
```

```python
import numpy as np
from contextlib import ExitStack
import concourse.bass as bass
import concourse.mybir as mybir
from concourse.bass_utils import run_bass_kernel_spmd

F32 = mybir.dt.float32
BF16 = mybir.dt.bfloat16
AF = mybir.ActivationFunctionType
ALU = mybir.AluOpType
AX = mybir.AxisListType

D = 1024
H = 8
HD = 128
KC = 8
DFF = 4096
INW = 9232
EPS = 1e-6
LIM = 24000
NEG = -30000.0

C_ID, C_ONE, C_MA, C_ST, C_TRI, C_AM = 0, 128, 256, 384, 512, 640
C_S0, C_S1 = 1280, 1408
C_MD, C_MC1, C_MC2, C_MDT, C_MC1T, C_MC2T = 1536, 1664, 1792, 1920, 2048, 2176
C_N = 2304


class Buf:
    __slots__ = ("name", "w", "r")

    def __init__(self, name):
        self.name = name
        self.w = None
        self.r = {}


class Src:
    def __init__(self, mk, name, unit):
        self.mk = mk
        self.name = name
        self.unit = unit
        self.sems = []
        self.count = 0
        self.finals = []

    def _roll(self):
        if not self.sems or self.count + self.unit > LIM:
            if self.sems:
                self.finals.append(self.count)
            self.sems.append(self.mk.new_sem())
            self.count = 0

    def next_event(self):
        self._roll()
        self.count += self.unit
        return (self, len(self.sems) - 1, self.count)

    def peek_event(self):
        self._roll()
        return (self, len(self.sems) - 1, self.count + self.unit)


class Eng(Src):
    def __init__(self, mk, name, key, same_sync):
        super().__init__(mk, name, 1)
        self.key = key
        self.same_sync = same_sync
        self.thunks = []
        self.waited = {}
        self.is_eng = True


class MK:
    def __init__(self, nc, es):
        self.nc = nc
        self.es = es
        self.nsem = 0
        self.PE = Eng(self, "pe", "tensor", False)
        self.ACT = Eng(self, "act", "scalar", True)
        self.DVE = Eng(self, "dve", "vector", True)
        self.POOL = Eng(self, "pool", "gpsimd", True)
        self.SP = Eng(self, "sp", "sync", False)
        self.engs = [self.PE, self.ACT, self.DVE, self.POOL, self.SP]
        self.nops = 0
        self.uid = 0
        self.streams = {}

    def new_sem(self):
        self.nsem += 1
        return self.es.enter_context(self.nc.semaphore(f"s{self.nsem}"))

    def stream(self, name):
        if name in self.streams:
            return self.streams[name]
        s = Src(self, name, 16)
        s.is_eng = False
        s.last = None
        self.streams[name] = s
        return s

    def _wait(self, E, ev):
        src, ep, val = ev
        if src is E and not E.same_sync:
            return
        key = (id(src), ep)
        if E.waited.get(key, 0) >= val:
            return
        if src.is_eng:
            for (sid, e2), v in E.waited.items():
                if sid == id(src) and e2 > ep:
                    return
        E.waited[key] = val
        sem = src.sems[ep]
        E.thunks.append(lambda eng, sem=sem, val=val: eng.wait_ge(sem, val))

    def _deps(self, E, reads, writes):
        for b in reads:
            if b.w is not None:
                self._wait(E, b.w)
        for b in writes:
            if b.w is not None:
                self._wait(E, b.w)
            for ev in b.r.values():
                self._wait(E, ev)

    def op(self, E, fn, reads=(), writes=(), signal=True):
        self.nops += 1
        self._deps(E, reads, writes)
        if signal:
            ev = E.next_event()
            sem = ev[0].sems[ev[1]]
            E.thunks.append(lambda eng, fn=fn, sem=sem: fn(eng).then_inc(sem, 1))
        else:
            assert E is self.PE
            ev = E.peek_event()
            E.thunks.append(lambda eng, fn=fn: fn(eng))
        for b in writes:
            b.w = ev
            b.r = {}
        for b in reads:
            b.r[id(E)] = ev

    def dma(self, Q, st, out, in_, reads=(), writes=(), **kw):
        self.nops += 1
        if isinstance(st, str):
            st = self.stream(st)
        self._deps(Q, reads, writes)
        if st.last is not None:
            self._wait(Q, st.last)
        ev = st.next_event()
        st.last = ev
        sem = ev[0].sems[ev[1]]
        Q.thunks.append(lambda eng, out=out, in_=in_, sem=sem, kw=kw:
                        eng.dma_start(out=out, in_=in_, **kw).then_inc(sem, 16))
        for b in writes:
            b.w = ev
            b.r = {}
        for b in reads:
            b.r[id(st)] = ev

    def final_wait(self, E, bufs):
        for b in bufs:
            if b.w is not None:
                self._wait(E, b.w)

    def barrier(self):
        evs = []
        for E in self.engs:
            if E.sems and E.count > 0:
                evs.append((E, len(E.sems) - 1, E.count))
        for s in self.streams.values():
            if s.last is not None:
                evs.append(s.last)
        for E in self.engs:
            for ev in evs:
                if ev[0] is not E:
                    self._wait(E, ev)

    def flush(self):
        self.barrier()
        nc = self.nc
        with nc.Block() as block:
            for E in self.engs:
                th = E.thunks
                E.thunks = []
                if not th:
                    continue

                def mkfn(th):
                    def f(eng):
                        for t in th:
                            t(eng)
                    return f
                getattr(block, E.key)(mkfn(th))


class _Stop(Exception):
    pass


DBG = {"p2_stage": 99}


def build_program(S, L, debug=False, stop=None):
    assert S % 512 == 0
    NT = S // 128
    nc = bass.Bass("TRN2", target_bir_lowering=False)
    es = ExitStack()
    mk = MK(nc, es)
    PE, ACT, DVE, POOL, SP = mk.PE, mk.ACT, mk.DVE, mk.POOL, mk.SP

    def din(name, shape, dt=F32):
        return nc.dram_tensor(name, list(shape), dt, kind="ExternalInput").ap()

    def dscr(name, shape, dt=F32):
        kind = "ExternalOutput" if debug else "Internal"
        return nc.dram_tensor(name, list(shape), dt, kind=kind).ap()

    x_in = din("x", [S, D])
    cT_in = din("cT", [128, KC])
    wada_in = din("w_ada", [L, D, 6 * D])
    bada_in = din("b_ada", [L, 6 * D])
    nmix_in = din("norm_mix", [L, D])
    nmlp_in = din("norm_mlp", [L, D])
    win_in = din("w_in", [L, D, INW])
    cw_in = din("cw", [128, L * 96])
    alog_in = din("a_log", [1, L * H])
    dtb_in = din("dt_bias", [1, L * H])
    gnorm_in = din("gdn_norm", [1, L * HD])
    biasT_in = din("biasT", [L, 128, H * 640])
    wout_in = din("w_out", [L, D, D])
    w1_in = din("w_ff_in", [L, D, DFF])
    w2_in = din("w_ff_out", [L, DFF, D])
    fnorm_in = din("final_norm", [1, D])
    consts_in = din("consts", [128, C_N])
    out_d = nc.dram_tensor("out", [S, D], F32, kind="ExternalOutput").ap()

    xs_d = dscr("xs", [S, D])
    zg_d = dscr("zg", [3 * D, S])
    zqk_d = dscr("zqk", [2 * D, S], BF16)
    ztok_d = dscr("ztok", [S, 3088])
    va_d = dscr("va", [S, D], BF16)
    m_d = dscr("m", [S, D], BF16)
    mod_d = dscr("modd", [128, 6 * D])

    B_xs = [Buf(f"xs{t}") for t in range(NT)]
    B_zg, B_zqk, B_ztok, B_va, B_m, B_mod = (Buf("zg"), Buf("zqk"), Buf("ztok"), Buf("va"),
                                               Buf("m"), Buf("mod"))
    B_out = Buf("out")

    def sb(name, shape, dt=F32, stack=None):
        mk.uid += 1
        return (stack or es).enter_context(nc.sbuf_tensor(f"{name}_u{mk.uid}", list(shape), dt))

    def ps(name, shape, dt=F32, stack=None):
        mk.uid += 1
        return (stack or es).enter_context(nc.psum_tensor(f"{name}_u{mk.uid}", list(shape), dt))

    consts = sb("consts", [128, C_N])
    B_consts = Buf("consts")
    ident_bf = sb("ident_bf", [128, 128], BF16)
    B_identbf = Buf("identbf")
    cw_sb = sb("cw", [128, L * 96])
    B_cw = Buf("cw")
    crep = sb("crep", [128, KC, 128])
    B_crep = Buf("crep")
    st_c = mk.stream("const")
    mk.dma(SP, st_c, consts[:], consts_in[:, :], writes=[B_consts])
    mk.dma(SP, st_c, cw_sb[:], cw_in[:, :], writes=[B_cw])
    mk.op(ACT, lambda e: e.activation(out=ident_bf[:], in_=consts[:, C_ID:C_ID + 128], func=AF.Copy),
          reads=[B_consts], writes=[B_identbf])
    ident = consts[:, C_ID:C_ID + 128]
    ones = consts[:, C_ONE:C_ONE + 128]
    maskA = consts[:, C_MA:C_MA + 128]
    strict = consts[:, C_ST:C_ST + 128]
    tri = consts[:, C_TRI:C_TRI + 128]

    with ExitStack() as ps0:
        ct = sb("ct", [128, KC], stack=ps0)
        cs = sb("cs", [128, KC], stack=ps0)
        B_ct, B_cs = Buf("ct"), Buf("cs")
        mk.dma(SP, st_c, ct[:], cT_in[:, :], writes=[B_ct])
        mk.op(ACT, lambda e: e.activation(out=cs[:], in_=ct[:], func=AF.Silu), reads=[B_ct], writes=[B_cs])
        for kc in range(KC):
            mk.op(DVE, lambda e, kc=kc: e.tensor_scalar(out=crep[:, kc, :], in0=ones, scalar1=cs[:, kc:kc + 1],
                                                         scalar2=None, op0=ALU.mult),
                  reads=[B_cs, B_consts], writes=[B_crep])
        mk.flush()

    def ck(name):
        if stop == name:
            mk.final_wait(SP, B_xs + [B_zg, B_zqk, B_ztok, B_va, B_m, B_mod])
            mk.flush()
            raise _Stop()

    try:
        _layers(locals())
    except _Stop:
        pass
    es.close()
    return nc, mk


def _layers(G):
    (nc, mk, es, S, L, NT, PE, ACT, DVE, POOL, SP, sb, ps, ck) = [G[k] for k in
        ("nc", "mk", "es", "S", "L", "NT", "PE", "ACT", "DVE", "POOL", "SP", "sb", "ps", "ck")]
    (x_in, wada_in, bada_in, nmix_in, nmlp_in, win_in, alog_in, dtb_in, gnorm_in, biasT_in, wout_in, w1_in, w2_in,
     fnorm_in, out_d, xs_d, zg_d, zqk_d, ztok_d, va_d, m_d, mod_d) = [G[k] for k in
        ("x_in", "wada_in", "bada_in", "nmix_in", "nmlp_in", "win_in", "alog_in", "dtb_in", "gnorm_in", "biasT_in",
         "wout_in", "w1_in", "w2_in", "fnorm_in", "out_d", "xs_d", "zg_d", "zqk_d", "ztok_d", "va_d", "m_d", "mod_d")]
    (B_xs, B_zg, B_zqk, B_ztok, B_va, B_m, B_mod, B_out, consts, B_consts, ident_bf, B_identbf, cw_sb, B_cw, crep,
     B_crep, ident, ones) = [G[k] for k in
        ("B_xs", "B_zg", "B_zqk", "B_ztok", "B_va", "B_m", "B_mod", "B_out", "consts", "B_consts", "ident_bf",
         "B_identbf", "cw_sb", "B_cw", "crep", "B_crep", "ident", "ones")]
    for l in range(L):
        x_src = x_in if l == 0 else xs_d

        with ExitStack() as ph:
            wa = [sb(f"wa{i}", [128, KC, 512], stack=ph) for i in range(2)]
            B_wa = [Buf(f"wa{i}") for i in range(2)]
            modb = sb("modb", [128, 6 * D], stack=ph)
            B_modb = Buf("modb")
            brow = sb("brow", [128, 6 * D], stack=ph)
            B_brow = Buf("brow")
            nrow = sb("nrow", [128, 2 * D], stack=ph)
            B_nrow = Buf("nrow")
            pm = [ps(f"pm{i}", [128, 512], stack=ph) for i in range(2)]
            B_pm = [Buf(f"pm{i}") for i in range(2)]
            st_w = mk.stream("wada")
            st_m = mk.stream("modst")
            mk.dma(SP, st_m, brow[:], bada_in[l:l + 1, :].partition_broadcast(128), writes=[B_brow])
            mk.dma(SP, st_m, nrow[:, 0:D], nmix_in[l:l + 1, :].partition_broadcast(128), writes=[B_nrow])
            mk.dma(SP, st_m, nrow[:, D:2 * D], nmlp_in[l:l + 1, :].partition_broadcast(128), writes=[B_nrow])
            wav = wada_in[l].rearrange("(kc p) n -> p kc n", p=128)
            for nb in range(12):
                i = nb % 2
                mk.dma(SP, f"wada{i}", wa[i][:], wav[:, :, nb * 512:(nb + 1) * 512], writes=[B_wa[i]])
                for kc in range(KC):
                    mk.op(PE, lambda e, i=i, kc=kc: e.matmul(pm[i][:], lhsT=crep[:, kc, :], rhs=wa[i][:, kc, :],
                                                             start=(kc == 0), stop=(kc == KC - 1)),
                          reads=[B_crep, B_wa[i]], writes=[B_pm[i]], signal=(kc == KC - 1))
                mk.op(DVE, lambda e, i=i, nb=nb: e.tensor_tensor(out=modb[:, nb * 512:(nb + 1) * 512], in0=pm[i][:],
                                                                   in1=brow[:, nb * 512:(nb + 1) * 512], op=ALU.add),
                      reads=[B_pm[i], B_brow], writes=[B_modb])
            mk.op(DVE, lambda e: e.tensor_scalar(out=nrow[:], in0=nrow[:], scalar1=float(D) ** 0.5, scalar2=None,
                                                 op0=ALU.mult), writes=[B_nrow])
            mk.op(DVE, lambda e: e.scalar_tensor_tensor(out=modb[:, D:2 * D], in0=modb[:, D:2 * D], scalar=1.0,
                                                         in1=nrow[:, 0:D], op0=ALU.add, op1=ALU.mult),
                  reads=[B_nrow], writes=[B_modb])
            mk.op(DVE, lambda e: e.scalar_tensor_tensor(out=modb[:, 4 * D:5 * D], in0=modb[:, 4 * D:5 * D], scalar=1.0,
                                                         in1=nrow[:, D:2 * D], op0=ALU.add, op1=ALU.mult),
                  reads=[B_nrow], writes=[B_modb])
            mk.dma(SP, st_m, mod_d[:, :], modb[:], reads=[B_modb], writes=[B_mod])
            mk.flush()
        ck(f"mod{l}")

        with ExitStack() as ph:
            hT = sb("hT", [128, KC, S], BF16, stack=ph)
            B_hT = Buf("hT")
            m1 = sb("m1", [128, 2 * D], stack=ph)
            B_m1 = Buf("m1")
            st_l = mk.stream("p1l")
            st_s = mk.stream("p1s")
            st_wc = mk.stream("p1w")
            mk.dma(SP, "p1m", m1[:], mod_d[:, 0:2 * D], reads=[B_mod], writes=[B_m1])
            with ExitStack() as p1a:
                xt = [sb(f"xt{i}", [128, D], stack=p1a) for i in range(2)]
                B_xt = [Buf(f"xt{i}") for i in range(2)]
                junk = sb("junk", [128, D], stack=p1a)
                B_junk = Buf("junk")
                h1 = sb("h1", [128, D], stack=p1a)
                B_h1 = Buf("h1")
                hb = [sb(f"hb{i}", [128, D], BF16, stack=p1a) for i in range(2)]
                B_hb = [Buf(f"hb{i}") for i in range(2)]
                ssq = sb("ssq", [128, 2], stack=p1a)
                B_ssq = Buf("ssq")
                ptr = [ps(f"ptr{i}", [128, KC, 128], BF16, stack=p1a) for i in range(2)]
                B_ptr = [Buf(f"ptr{i}") for i in range(2)]
                for t in range(NT):
                    i = t % 2
                    mk.dma(SP, f"p1x{i}", xt[i][:], x_src[t * 128:(t + 1) * 128, :], reads=[B_xs[t]] if l > 0 else [],
                           writes=[B_xt[i]])
                    mk.op(ACT, lambda e, i=i: e.activation(out=junk[:], in_=xt[i][:], func=AF.Square,
                                                           accum_out=ssq[:, 0:1]),
                          reads=[B_xt[i]], writes=[B_junk, B_ssq])
                    mk.op(ACT, lambda e: e.activation(out=ssq[:, 1:2], in_=ssq[:, 0:1], func=AF.Sqrt, bias=EPS * D),
                          reads=[B_ssq], writes=[B_ssq])
                    mk.op(DVE, lambda e: e.reciprocal(out=ssq[:, 1:2], in_=ssq[:, 1:2]), reads=[B_ssq], writes=[B_ssq])
                    mk.op(DVE, lambda e, i=i: e.scalar_tensor_tensor(out=h1[:], in0=xt[i][:], scalar=ssq[:, 1:2],
                                                                     in1=m1[:, D:2 * D], op0=ALU.mult, op1=ALU.mult),
                          reads=[B_xt[i], B_ssq, B_m1], writes=[B_h1])
                    mk.op(POOL, lambda e, i=i: e.tensor_tensor(out=hb[i][:], in0=h1[:], in1=m1[:, 0:D], op=ALU.add),
                          reads=[B_h1, B_m1], writes=[B_hb[i]])
                    for kc in range(KC):
                        mk.op(PE, lambda e, i=i, kc=kc: e.transpose(ptr[i][:, kc, :], hb[i][:, kc * 128:(kc + 1) * 128],
                                                                    ident_bf[:]),
                              reads=[B_hb[i], B_identbf], writes=[B_ptr[i]], signal=(kc == KC - 1))
                    mk.op(ACT, lambda e, i=i, t=t: e.activation(out=hT[:, :, t * 128:(t + 1) * 128], in_=ptr[i][:],
                                                                func=AF.Copy),
                          reads=[B_ptr[i]], writes=[B_hT])
                mk.flush()
            winv = win_in[l].rearrange("(kc p) n -> p kc n", p=128)
            with ExitStack() as p1b:
                wb = [sb(f"wb{i}", [128, KC, 512], BF16, stack=p1b) for i in range(2)]
                B_wb = [Buf(f"wb{i}") for i in range(2)]
                pz = [ps(f"pz{i}", [128, 512], stack=p1b) for i in range(3)]
                B_pz = [Buf(f"pz{i}") for i in range(3)]
                pn = [ps(f"pn{i}", [128, 512], stack=p1b) for i in range(2)]
                B_pn = [Buf(f"pn{i}") for i in range(2)]
                xc = [sb(f"xc{i}", [128, 515], stack=p1b) for i in range(2)]
                B_xc = [Buf(f"xc{i}") for i in range(2)]
                yc = [sb(f"yc{i}", [128, 512], stack=p1b) for i in range(2)]
                B_yc = [Buf(f"yc{i}") for i in range(2)]
                sq = [sb(f"sq{i}", [128, 512], stack=p1b) for i in range(2)]
                B_sq = [Buf(f"sq{i}") for i in range(2)]
                rn = [sb(f"rn{i}", [128, 512], stack=p1b) for i in range(2)]
                B_rn = [Buf(f"rn{i}") for i in range(2)]
                NST = min(S, 2048)
                stg = [sb(f"stg{i}", [128, NST], stack=p1b) for i in range(2)]
                B_stg = [Buf(f"stg{i}") for i in range(2)]
                stgb = [sb(f"stgb{i}", [128, NST], BF16, stack=p1b) for i in range(2)]
                B_stgb = [Buf(f"stgb{i}") for i in range(2)]
                nchunk = S // 512
                cpst = NST // 512
                sidx = 0
                wi = 0
                pzi = 0
                pending = []
                deferred = None
                for g in range(24 + 16):
                    gdn = g < 24
                    col0 = g * 128 if gdn else 4112 + (g - 24) * 128
                    if g % 4 == 0:
                        wi += 1
                        i = wi % 2
                        mk.dma(POOL, f"p1w{i}", wb[i][:], winv[:, :, col0:col0 + 512], writes=[B_wb[i]])
                    wo4 = (g % 4) * 128
                    for c in range(nchunk):
                        pi = pzi % 3
                        pzi += 1
                        for kc in range(KC):
                            mk.op(PE, lambda e, i=i, kc=kc, c=c, pi=pi, wo4=wo4: e.matmul(
                                pz[pi][:], lhsT=wb[i][:, kc, wo4:wo4 + 128], rhs=hT[:, kc, c * 512:(c + 1) * 512],
                                start=(kc == 0), stop=(kc == KC - 1)),
                                reads=[B_wb[i], B_hT], writes=[B_pz[pi]], signal=(kc == KC - 1))
                        sl = (sidx // cpst) % 2
                        so = (sidx % cpst) * 512
                        if gdn:
                            j = c % 2
                            if c == 0:
                                mk.op(POOL, lambda e, j=j: e.memset(xc[j][:, 0:3], 0.0), writes=[B_xc[j]])
                            mk.op(ACT, lambda e, j=j, pi=pi: e.activation(out=xc[j][:, 3:515], in_=pz[pi][:], func=AF.Copy),
                                  reads=[B_pz[pi]], writes=[B_xc[j]])
                            if c + 1 < nchunk:
                                mk.op(POOL, lambda e, j=j: e.tensor_copy(out=xc[1 - j][:, 0:3], in_=xc[j][:, 512:515]),
                                      reads=[B_xc[j]], writes=[B_xc[1 - j]])
                            CE = DVE if (c % 2 == 0) else POOL
                            cwb = l * 96 + g * 4
                            mk.op(CE, lambda e, j=j, cwb=cwb: e.tensor_scalar(
                                out=yc[j][:], in0=xc[j][:, 0:512], scalar1=cw_sb[:, cwb:cwb + 1], scalar2=None,
                                op0=ALU.mult), reads=[B_xc[j], B_cw], writes=[B_yc[j]])
                            for tp in range(1, 4):
                                if CE is DVE:
                                    mk.op(CE, lambda e, j=j, cwb=cwb, tp=tp: e.scalar_tensor_tensor(
                                        out=yc[j][:], in0=xc[j][:, tp:tp + 512], scalar=cw_sb[:, cwb + tp:cwb + tp + 1],
                                        in1=yc[j][:], op0=ALU.mult, op1=ALU.add),
                                        reads=[B_xc[j], B_cw], writes=[B_yc[j]])
                                else:
                                    mk.op(CE, lambda e, j=j, cwb=cwb, tp=tp: e.tensor_scalar(
                                        out=sq[j][:], in0=xc[j][:, tp:tp + 512], scalar1=cw_sb[:, cwb + tp:cwb + tp + 1],
                                        scalar2=None, op0=ALU.mult), reads=[B_xc[j], B_cw], writes=[B_sq[j]])
                                    mk.op(CE, lambda e, j=j: e.tensor_tensor(out=yc[j][:], in0=yc[j][:], in1=sq[j][:],
                                                                            op=ALU.add),
                                          reads=[B_sq[j]], writes=[B_yc[j]])
                            if g >= 16:
                                mk.op(ACT, lambda e, j=j, sl=sl, so=so: e.activation(
                                    out=stg[sl][:, so:so + 512], in_=yc[j][:], func=AF.Silu),
                                    reads=[B_yc[j]], writes=[B_stg[sl]])
                            else:
                                mk.op(ACT, lambda e, j=j: e.activation(out=yc[j][:], in_=yc[j][:], func=AF.Silu),
                                      reads=[], writes=[B_yc[j]])
                                isq = g < 8
                                s_in = (float(HD) ** 0.5) if isq else 1.0
                                eps2 = EPS * HD if isq else EPS
                                mk.op(ACT, lambda e, j=j, s_in=s_in: e.activation(out=sq[j][:], in_=yc[j][:], func=AF.Square,
                                                                                  scale=s_in),
                                      reads=[B_yc[j]], writes=[B_sq[j]])

                                def partB(j=j, sl=sl, so=so, eps2=eps2):
                                    mk.op(PE, lambda e: e.matmul(pn[j][:], lhsT=ones, rhs=sq[j][:], start=True, stop=True),
                                          reads=[B_sq[j], B_consts], writes=[B_pn[j]])
                                    mk.op(ACT, lambda e: e.activation(out=rn[j][:], in_=pn[j][:], func=AF.Sqrt, bias=eps2),
                                          reads=[B_pn[j]], writes=[B_rn[j]])
                                    mk.op(DVE, lambda e: e.reciprocal(out=rn[j][:], in_=rn[j][:]), writes=[B_rn[j]])
                                    mk.op(POOL, lambda e: e.tensor_tensor(
                                        out=stg[sl][:, so:so + 512], in0=yc[j][:], in1=rn[j][:], op=ALU.mult),
                                        reads=[B_yc[j], B_rn[j]], writes=[B_stg[sl]])
                                deferred = partB
                        else:
                            mk.op(ACT, lambda e, pi=pi, sl=sl, so=so: e.activation(
                                out=stgb[sl][:, so:so + 512], in_=pz[pi][:], func=AF.Copy),
                                reads=[B_pz[pi]], writes=[B_stgb[sl]])
                        sidx += 1
                        store = None
                        if sidx % cpst == 0:
                            t0 = (c + 1) * 512 - NST
                            if gdn:
                                store = lambda sl=sl, t0=t0, g=g: mk.dma(SP, f"p1s{sl}", zg_d[g * 128:(g + 1) * 128, t0:t0 + NST],
                                                                        stg[sl][:], reads=[B_stg[sl]], writes=[B_zg])
                            else:
                                store = lambda sl=sl, t0=t0, g=g: mk.dma(SP, f"p1sb{sl}", zqk_d[(g - 24) * 128:(g - 23) * 128, t0:t0 + NST],
                                                                        stgb[sl][:], reads=[B_stgb[sl]], writes=[B_zqk])
                        for fn_ in pending:
                            fn_()
                        pending = [f_ for f_ in (deferred, store) if f_ is not None]
                        deferred = None
                for fn_ in pending:
                    fn_()
                pending = []
                mk.flush()
            with ExitStack() as p1c:
                wt = [sb(f"wt{i}", [128, KC, 512], BF16, stack=p1c) for i in range(2)]
                B_wt = [Buf(f"wt{i}") for i in range(2)]
                pz = [ps(f"pzt{i}", [128, 512], stack=p1c) for i in range(3)]
                B_pz = [Buf(f"pzt{i}") for i in range(3)]
                stt = [sb(f"stt{i}", [128, 4, 512], stack=p1c) for i in range(2)]
                B_stt = [Buf(f"stt{i}") for i in range(2)]
                sttb = [sb(f"sttb{i}", [128, 4, 512], BF16, stack=p1c) for i in range(2)]
                B_sttb = [Buf(f"sttb{i}") for i in range(2)]
                blocks = [(3072, 512, AF.Silu, 0, 0), (3584, 512, AF.Silu, 0, 512), (4096, 16, AF.Copy, 0, 1024),
                          (6160, 512, AF.Copy, 1, 0), (6672, 512, AF.Copy, 1, 512)]
                for q in range(4):
                    blocks.append((7184 + q * 512, 512, AF.Sigmoid, 0, 1040 + q * 512))
                pzi = 0
                sidx = 0
                ztv = ztok_d.rearrange("(n p) c -> p n c", p=128)
                vav = va_d.rearrange("(n p) c -> p n c", p=128)
                for bi, (col0, ncol, func, dest, dcol) in enumerate(blocks):
                    i = bi % 2
                    mk.dma(POOL, f"p1wt{i}", wt[i][:, :, 0:ncol], winv[:, :, col0:col0 + ncol], writes=[B_wt[i]])
                    for t in range(NT):
                        pi = pzi % 3
                        pzi += 1
                        for kc in range(KC):
                            mk.op(PE, lambda e, i=i, kc=kc, t=t, pi=pi, ncol=ncol: e.matmul(
                                pz[pi][:, 0:ncol], lhsT=hT[:, kc, t * 128:(t + 1) * 128], rhs=wt[i][:, kc, 0:ncol],
                                start=(kc == 0), stop=(kc == KC - 1)),
                                reads=[B_wt[i], B_hT], writes=[B_pz[pi]], signal=(kc == KC - 1))
                        sl = (sidx // 4) % 2
                        so = sidx % 4
                        sidx += 1
                        if dest == 0:
                            mk.op(ACT, lambda e, pi=pi, sl=sl, so=so, ncol=ncol, func=func: e.activation(
                                out=stt[sl][:, so, 0:ncol], in_=pz[pi][:, 0:ncol], func=func),
                                reads=[B_pz[pi]], writes=[B_stt[sl]])
                        else:
                            mk.op(ACT, lambda e, pi=pi, sl=sl, so=so, ncol=ncol, func=func: e.activation(
                                out=sttb[sl][:, so, 0:ncol], in_=pz[pi][:, 0:ncol], func=func),
                                reads=[B_pz[pi]], writes=[B_sttb[sl]])
                        if sidx % 4 == 0:
                            n0 = t - 3
                            if dest == 0:
                                mk.dma(SP, f"p1st{sl}", ztv[:, n0:n0 + 4, dcol:dcol + ncol], stt[sl][:, :, 0:ncol],
                                       reads=[B_stt[sl]], writes=[B_ztok])
                            else:
                                mk.dma(SP, f"p1stb{sl}", vav[:, n0:n0 + 4, dcol:dcol + ncol], sttb[sl][:, :, 0:ncol],
                                       reads=[B_sttb[sl]], writes=[B_va])
                mk.flush()
        ck(f"p1{l}")

        with ExitStack() as ph:
            p2_mixers(nc, mk, ph, l, S, NT, dict(
                consts=consts, B_consts=B_consts, ident_bf=ident_bf, B_identbf=B_identbf,
                zg_d=zg_d, zqk_d=zqk_d, ztok_d=ztok_d, va_d=va_d, m_d=m_d,
                B_zg=B_zg, B_zqk=B_zqk, B_ztok=B_ztok, B_va=B_va, B_m=B_m,
                alog_in=alog_in, dtb_in=dtb_in, gnorm_in=gnorm_in, biasT_in=biasT_in))
            mk.flush()
        ck(f"p2{l}")

        with ExitStack() as ph:
            wo = sb("wo", [128, KC, D], BF16, stack=ph)
            B_wo = Buf("wo")
            gt1 = sb("gt1", [128, D], stack=ph)
            B_gt1 = Buf("gt1")
            st_l = mk.stream("p3al")
            st_s = mk.stream("p3as")
            st_wc = mk.stream("p3aw")
            wov = wout_in[l].rearrange("(kc p) n -> p kc n", p=128)
            for hf in range(2):
                mk.dma(POOL, f"p3aw{hf}", wo[:, :, hf * 512:(hf + 1) * 512], wov[:, :, hf * 512:(hf + 1) * 512], writes=[B_wo])
            mk.dma(SP, "p3ag", gt1[:], mod_d[:, 2 * D:3 * D], reads=[B_mod], writes=[B_gt1])
            mt = [sb(f"mt{i}", [128, D], BF16, stack=ph) for i in range(2)]
            B_mt = [Buf(f"mt{i}") for i in range(2)]
            xt = [sb(f"xa{i}", [128, D], stack=ph) for i in range(2)]
            B_xt = [Buf(f"xa{i}") for i in range(2)]
            mT = [sb(f"mT{i}", [128, KC, 128], BF16, stack=ph) for i in range(2)]
            B_mT = [Buf(f"mT{i}") for i in range(2)]
            ptr = [ps(f"ptra{i}", [128, KC, 128], BF16, stack=ph) for i in range(2)]
            B_ptr = [Buf(f"ptra{i}") for i in range(2)]
            py = [ps(f"pya{i}", [128, 512], stack=ph) for i in range(4)]
            B_py = [Buf(f"pya{i}") for i in range(4)]
            tmp = [sb(f"tmpa{i}", [128, D], stack=ph) for i in range(2)]
            B_tmp = [Buf(f"tmpa{i}") for i in range(2)]
            xo = [sb(f"xoa{i}", [128, D], stack=ph) for i in range(2)]
            B_xo = [Buf(f"xoa{i}") for i in range(2)]
            for t in range(NT):
                i = t % 2
                mk.dma(SP, f"p3am{i}", mt[i][:], m_d[t * 128:(t + 1) * 128, :], reads=[B_m], writes=[B_mt[i]])
                mk.dma(SP, f"p3ax{i}", xt[i][:], x_src[t * 128:(t + 1) * 128, :], reads=[B_xs[t]] if l > 0 else [],
                       writes=[B_xt[i]])
                for kc in range(KC):
                    mk.op(PE, lambda e, i=i, kc=kc: e.transpose(ptr[i][:, kc, :], mt[i][:, kc * 128:(kc + 1) * 128], ident_bf[:]),
                          reads=[B_mt[i], B_identbf], writes=[B_ptr[i]], signal=(kc == KC - 1))
                mk.op(ACT, lambda e, i=i: e.activation(out=mT[i][:], in_=ptr[i][:], func=AF.Copy),
                      reads=[B_ptr[i]], writes=[B_mT[i]])
                for hf in range(2):
                    pi = i * 2 + hf
                    for kc in range(KC):
                        mk.op(PE, lambda e, i=i, kc=kc, hf=hf, pi=pi: e.matmul(
                            py[pi][:], lhsT=mT[i][:, kc, :], rhs=wo[:, kc, hf * 512:(hf + 1) * 512],
                            start=(kc == 0), stop=(kc == KC - 1)),
                            reads=[B_mT[i], B_wo], writes=[B_py[pi]], signal=(kc == KC - 1))
                    mk.op(DVE, lambda e, i=i, hf=hf, pi=pi: e.tensor_tensor(
                        out=tmp[i][:, hf * 512:(hf + 1) * 512], in0=py[pi][:], in1=gt1[:, hf * 512:(hf + 1) * 512],
                        op=ALU.mult), reads=[B_py[pi], B_gt1], writes=[B_tmp[i]])
                mk.op(POOL, lambda e, i=i: e.tensor_tensor(out=xo[i][:], in0=tmp[i][:], in1=xt[i][:], op=ALU.add),
                      reads=[B_tmp[i], B_xt[i]], writes=[B_xo[i]])
                mk.dma(SP, f"p3as{i}", xs_d[t * 128:(t + 1) * 128, :], xo[i][:], reads=[B_xo[i]], writes=[B_xs[t]])
            mk.flush()
        ck(f"p3a{l}")

        with ExitStack() as ph:
            last = (l == L - 1)
            w1 = sb("w1", [128, KC, DFF], BF16, stack=ph)
            w2 = sb("w2", [128, 32, D], BF16, stack=ph)
            B_w1, B_w2 = Buf("w1"), Buf("w2")
            m2 = sb("m2", [128, 3 * D], stack=ph)
            B_m2 = Buf("m2")
            st_l = mk.stream("p3bl")
            st_s = mk.stream("p3bs")
            st_wc = mk.stream("p3bw")
            w1v = w1_in[l].rearrange("(kc p) n -> p kc n", p=128)
            w2v = w2_in[l].rearrange("(fc p) n -> p fc n", p=128)
            for q in range(8):
                mk.dma(POOL, f"p3bw{q % 4}", w1[:, :, q * 512:(q + 1) * 512], w1v[:, :, q * 512:(q + 1) * 512], writes=[B_w1])
            for q in range(8):
                mk.dma(POOL, f"p3bw{q % 4}", w2[:, q * 4:(q + 1) * 4, :], w2v[:, q * 4:(q + 1) * 4, :], writes=[B_w2])
            mk.dma(SP, "p3bm", m2[:], mod_d[:, 3 * D:6 * D], reads=[B_mod], writes=[B_m2])
            if last:
                fn = sb("fnrm", [128, D], stack=ph)
                B_fn = Buf("fn")
                mk.dma(SP, "p3bf", fn[:], fnorm_in[0:1, :].partition_broadcast(128), writes=[B_fn])
                mk.op(DVE, lambda e: e.tensor_scalar(out=fn[:], in0=fn[:], scalar1=float(D) ** 0.5, scalar2=None,
                                                     op0=ALU.mult), writes=[B_fn])
            xt = [sb(f"xb{i}", [128, D], stack=ph) for i in range(2)]
            B_xt = [Buf(f"xb{i}") for i in range(2)]
            h1 = sb("h1b", [128, D], stack=ph)
            B_h1 = Buf("h1b")
            junk, B_junk = h1, B_h1
            hb = [sb(f"hbb{i}", [128, D], BF16, stack=ph) for i in range(2)]
            B_hb = [Buf(f"hbb{i}") for i in range(2)]
            hT2 = [sb(f"hT2{i}", [128, KC, 128], BF16, stack=ph) for i in range(2)]
            B_hT2 = [Buf(f"hT2{i}") for i in range(2)]
            ssq = sb("ssqb", [128, 4], stack=ph)
            B_ssq = Buf("ssqb")
            ptr = [ps(f"ptrb{i}", [128, KC, 128], BF16, stack=ph) for i in range(1)]
            B_ptr = [Buf(f"ptrb{i}") for i in range(1)]
            pa = [ps(f"pa{i}", [128, 4, 128], stack=ph) for i in range(3)]
            B_pa = [Buf(f"pa{i}") for i in range(3)]
            py = [ps(f"pyb{i}", [128, 512], stack=ph) for i in range(4)]
            B_py = [Buf(f"pyb{i}") for i in range(4)]
            rl = [sb(f"rl{i}", [128, 4, 128], stack=ph) for i in range(2)]
            B_rl = [Buf(f"rl{i}") for i in range(2)]
            aT = [sb(f"aT{i}", [128, 32, 128], BF16, stack=ph) for i in range(2)]
            B_aT = [Buf(f"aT{i}") for i in range(2)]
            tmp = [sb(f"tmpb{i}", [128, D], stack=ph) for i in range(2)]
            B_tmp = [Buf(f"tmpb{i}") for i in range(2)]
            xo, B_xo = xt, B_xt
            pai = 0
            for t in range(NT):
                i = t % 2
                mk.dma(SP, f"p3bx{i}", xt[i][:], xs_d[t * 128:(t + 1) * 128, :], reads=[B_xs[t]], writes=[B_xt[i]])
                mk.op(ACT, lambda e, i=i: e.activation(out=junk[:], in_=xt[i][:], func=AF.Square, accum_out=ssq[:, 0:1]),
                      reads=[B_xt[i]], writes=[B_junk, B_ssq])
                mk.op(ACT, lambda e: e.activation(out=ssq[:, 1:2], in_=ssq[:, 0:1], func=AF.Sqrt, bias=EPS * D),
                      reads=[B_ssq], writes=[B_ssq])
                mk.op(DVE, lambda e: e.reciprocal(out=ssq[:, 1:2], in_=ssq[:, 1:2]), reads=[B_ssq], writes=[B_ssq])
                mk.op(DVE, lambda e, i=i: e.scalar_tensor_tensor(out=h1[:], in0=xt[i][:], scalar=ssq[:, 1:2],
                                                                 in1=m2[:, D:2 * D], op0=ALU.mult, op1=ALU.mult),
                      reads=[B_xt[i], B_ssq, B_m2], writes=[B_h1])
                mk.op(POOL, lambda e, i=i: e.tensor_tensor(out=hb[i][:], in0=h1[:], in1=m2[:, 0:D], op=ALU.add),
                      reads=[B_h1, B_m2], writes=[B_hb[i]])
                for kc in range(KC):
                    mk.op(PE, lambda e, i=i, kc=kc: e.transpose(ptr[0][:, kc, :], hb[i][:, kc * 128:(kc + 1) * 128], ident_bf[:]),
                          reads=[B_hb[i], B_identbf], writes=[B_ptr[0]], signal=(kc == KC - 1))
                mk.op(ACT, lambda e, i=i: e.activation(out=hT2[i][:], in_=ptr[0][:], func=AF.Copy),
                      reads=[B_ptr[0]], writes=[B_hT2[i]])
                for fq in range(8):
                    pi = pai % 3
                    ri = pai % 2
                    pai += 1
                    for f4 in range(4):
                        fb = fq * 4 + f4
                        for kc in range(KC):
                            mk.op(PE, lambda e, i=i, kc=kc, fb=fb, f4=f4, pi=pi: e.matmul(
                                pa[pi][:, f4, :], lhsT=w1[:, kc, fb * 128:(fb + 1) * 128], rhs=hT2[i][:, kc, :],
                                start=(kc == 0), stop=(kc == KC - 1)),
                                reads=[B_w1, B_hT2[i]], writes=[B_pa[pi]], signal=(kc == KC - 1 and f4 == 3))
                    mk.op(ACT, lambda e, pi=pi, ri=ri: e.activation(out=rl[ri][:], in_=pa[pi][:], func=AF.Relu),
                          reads=[B_pa[pi]], writes=[B_rl[ri]])
                    mk.op(POOL, lambda e, i=i, ri=ri, fq=fq: e.tensor_tensor(
                        out=aT[i][:, fq * 4:(fq + 1) * 4, :], in0=rl[ri][:], in1=rl[ri][:], op=ALU.mult),
                        reads=[B_rl[ri]], writes=[B_aT[i]])
                for hf in range(2):
                    pi = i * 2 + hf
                    for fc in range(32):
                        mk.op(PE, lambda e, i=i, fc=fc, hf=hf, pi=pi: e.matmul(
                            py[pi][:], lhsT=aT[i][:, fc, :], rhs=w2[:, fc, hf * 512:(hf + 1) * 512],
                            start=(fc == 0), stop=(fc == 31)),
                            reads=[B_aT[i], B_w2], writes=[B_py[pi]], signal=(fc == 31))
                    mk.op(DVE, lambda e, i=i, hf=hf, pi=pi: e.tensor_tensor(
                        out=tmp[i][:, hf * 512:(hf + 1) * 512], in0=py[pi][:], in1=m2[:, 2 * D + hf * 512:2 * D + (hf + 1) * 512],
                        op=ALU.mult), reads=[B_py[pi], B_m2], writes=[B_tmp[i]])
                mk.op(POOL, lambda e, i=i: e.tensor_tensor(out=xo[i][:], in0=tmp[i][:], in1=xt[i][:], op=ALU.add),
                      reads=[B_tmp[i], B_xt[i]], writes=[B_xo[i]])
                if not last:
                    mk.dma(SP, f"p3bs{i}", xs_d[t * 128:(t + 1) * 128, :], xo[i][:], reads=[B_xo[i]], writes=[B_xs[t]])
                else:
                    mk.op(ACT, lambda e, i=i: e.activation(out=junk[:], in_=xo[i][:], func=AF.Square, accum_out=ssq[:, 2:3]),
                          reads=[B_xo[i]], writes=[B_junk, B_ssq])
                    mk.op(ACT, lambda e: e.activation(out=ssq[:, 3:4], in_=ssq[:, 2:3], func=AF.Sqrt, bias=EPS * D),
                          reads=[B_ssq], writes=[B_ssq])
                    mk.op(DVE, lambda e: e.reciprocal(out=ssq[:, 3:4], in_=ssq[:, 3:4]), reads=[B_ssq], writes=[B_ssq])
                    mk.op(DVE, lambda e, i=i: e.scalar_tensor_tensor(out=tmp[i][:], in0=xo[i][:], scalar=ssq[:, 3:4],
                                                                     in1=fn[:], op0=ALU.mult, op1=ALU.mult),
                          reads=[B_xo[i], B_ssq, B_fn], writes=[B_tmp[i]])
                    mk.dma(SP, f"p3bo{i}", out_d[t * 128:(t + 1) * 128, :], tmp[i][:], reads=[B_tmp[i]], writes=[B_out])
            if last:
                mk.final_wait(SP, [B_out])
            mk.flush()
            ck(f"p3b{l}")


def p2_mixers(nc, mk, ph, l, S, NT, g):
    PE, ACT, DVE, POOL, SP = mk.PE, mk.ACT, mk.DVE, mk.POOL, mk.SP
    consts, B_consts = g["consts"], g["B_consts"]
    ident_bf, B_identbf = g["ident_bf"], g["B_identbf"]
    ident = consts[:, C_ID:C_ID + 128]
    ones = consts[:, C_ONE:C_ONE + 128]
    maskA = consts[:, C_MA:C_MA + 128]
    strict = consts[:, C_ST:C_ST + 128]
    tri = consts[:, C_TRI:C_TRI + 128]
    sel0 = consts[:, C_S0:C_S0 + 128]
    sel1 = consts[:, C_S1:C_S1 + 128]

    def sb(name, shape, dt=F32):
        mk.uid += 1
        return ph.enter_context(nc.sbuf_tensor(f"{name}_u{mk.uid}", list(shape), dt))

    def ps(name, shape, dt=F32):
        mk.uid += 1
        return ph.enter_context(nc.psum_tensor(f"{name}_u{mk.uid}", list(shape), dt))

    st_l = mk.stream("p2l")
    st_s = mk.stream("p2s")
    st_k = mk.stream("p2k")
    expB = sb("expB", [128, H * 640])
    B_expB = Buf("expB")
    mk.dma(SP, st_k, expB[:], g["biasT_in"][l], writes=[B_expB])
    mk.op(ACT, lambda e: e.activation(out=expB[:], in_=expB[:], func=AF.Exp), writes=[B_expB])
    for h in range(H):
        mk.op(POOL, lambda e, h=h: e.tensor_tensor(out=expB[:, h * 640:(h + 1) * 640], in0=expB[:, h * 640:(h + 1) * 640],
                                                     in1=consts[:, C_AM:C_AM + 640], op=ALU.mult),
              reads=[B_consts], writes=[B_expB])
    hv = sb("hv", [128, 3 * H + 128])
    B_hv = Buf("hv")
    mk.dma(SP, st_k, hv[:, 0:H], g["dtb_in"][0:1, l * H:(l + 1) * H].partition_broadcast(128), writes=[B_hv])
    mk.dma(SP, st_k, hv[:, H:2 * H], g["alog_in"][0:1, l * H:(l + 1) * H].partition_broadcast(128), writes=[B_hv])
    mk.dma(SP, st_k, hv[:, 3 * H:3 * H + 128], g["gnorm_in"][0:1, l * HD:(l + 1) * HD].partition_broadcast(128),
           writes=[B_hv])
    mk.op(ACT, lambda e: e.activation(out=hv[:, 2 * H:3 * H], in_=hv[:, H:2 * H], func=AF.Exp), writes=[B_hv])
    mk.op(DVE, lambda e: e.tensor_scalar(out=hv[:, H:2 * H], in0=hv[:, 2 * H:3 * H], scalar1=-1.0, scalar2=None,
                                         op0=ALU.mult), writes=[B_hv])
    mk.op(DVE, lambda e: e.tensor_scalar(out=hv[:, 3 * H:3 * H + 128], in0=hv[:, 3 * H:3 * H + 128], scalar1=float(HD) ** 0.5,
                                         scalar2=None, op0=ALU.mult), writes=[B_hv])
    dtb = hv[:, 0:H]
    nA = hv[:, H:2 * H]
    gn = hv[:, 3 * H:3 * H + 128]

    zg = [sb(f"zg{i}", [128, 24, 128]) for i in range(2)]
    B_zgt = [Buf(f"zgt{i}") for i in range(2)]
    zq0 = sb("zq0", [128, H, 128], BF16)
    zq = [zq0, zq0]
    B_zq0 = Buf("zq0")
    B_zq = [B_zq0, B_zq0]
    kring = sb("kring", [128, 5, H, 128], BF16)
    B_kr = [Buf(f"kr{i}") for i in range(5)]
    vring = sb("vring", [128, 5, H, 132], BF16)
    B_vr = [Buf(f"vr{i}") for i in range(5)]
    zt = [sb(f"zt{i}", [128, 3088]) for i in range(2)]
    B_zt = [Buf(f"zt{i}") for i in range(2)]
    mtile = [sb(f"mtile{i}", [128, D], BF16) for i in range(2)]
    B_mtile = [Buf(f"mtile{i}") for i in range(2)]
    for s in range(5):
        mk.op(POOL, lambda e, s=s: e.memset(vring[:, s, :, 128:132], 1.0), writes=[B_vr[s]])

    gsc = [sb(f"gsc{i}", [128, 12 * H]) for i in range(2)]
    B_gsc = [Buf(f"gsc{i}") for i in range(2)]
    Gs = [sb(f"Gs{i}", [128, 3 * H]) for i in range(2)]
    B_Gs = [Buf(f"Gs{i}") for i in range(2)]
    pg = ps("pg", [128, 512])
    B_pgG = B_pgO = B_pgS = Buf("pg")
    psA = ps("psA", [128, 512])
    B_psA = Buf("psA")
    NQ = 12
    pq_t = [ps(f"pq{i}", [128, 4, 128]) for i in range(NQ // 4)]
    NBK = NQ // 4
    pq = [pq_t[i % NBK][:, (i // NBK) % 4, :] for i in range(NQ)]
    B_bank = [Buf(f"pqb{i}") for i in range(NBK)]
    B_pq = [B_bank[i % NBK] for i in range(NQ)]
    state = {"q": 0}

    def getq():
        i = state["q"] % NQ
        state["q"] += 1
        return pq[i], B_pq[i]

    qkb0 = sb("qkb0", [128, 24, 128], BF16)
    qkb = [qkb0, qkb0]
    B_qkb0 = Buf("qkb0")
    B_qkb = [B_qkb0, B_qkb0]
    NW = 1
    def hb(name, shape, dt=F32):
        t = [sb(f"{name}{i}", [128, H] + list(shape), dt) for i in range(NW)]
        b = [[Buf(f"{name}{i}_{h}") for h in range(H)] for i in range(NW)]
        return t, b
    grep_r = [sb(f"grepr{i}", [128, 128]) for i in range(2)]
    B_grep_r = [Buf(f"grepr{i}") for i in range(2)]
    eGr_r = [sb(f"eGrr{i}", [128, 128]) for i in range(2)]
    B_eGr_r = [Buf(f"eGrr{i}") for i in range(2)]
    egk, B_egk = hb("egk", [128], BF16)
    kdec, B_kdec = hb("kdec", [128], BF16)
    vb, B_vb = hb("vb", [128], BF16)
    tG, B_tG = hb("tG", [128])
    DT, B_DT = hb("DT", [128])
    attnT, B_attnT = hb("attnT", [128], BF16)
    X0, B_X0 = tG, B_tG
    Xa, B_Xa = hb("Xa", [128], BF16)
    XTa, B_XTa = hb("XTa", [128], BF16)
    Da, B_Da = hb("Da", [128], BF16)
    Db, B_Db = hb("Db", [128], BF16)
    DTa, B_DTa = hb("DTa", [128], BF16)
    DTb, B_DTb = hb("DTb", [128], BF16)
    C1m, B_C1m = hb("C1m", [128], BF16)
    C2m, B_C2m = hb("C2m", [128], BF16)
    C1T, B_C1T = hb("C1T", [128], BF16)
    C2T, B_C2T = hb("C2T", [128], BF16)
    Pa, B_Pa = hb("Pa", [128], BF16)
    Pb, B_Pb = hb("Pb", [128], BF16)
    PTa, B_PTa = hb("PTa", [128], BF16)
    PTb, B_PTb = hb("PTb", [128], BF16)
    Yb, B_Yb = hb("Yb", [128], BF16)
    Ypb, B_Ypb = hb("Ypb", [128], BF16)
    usb, B_usb = hb("usb", [128])
    wTb, B_wTb = hb("wTb", [128], BF16)
    qdT, B_qdT = hb("qdT", [128], BF16)
    vn, B_vn = hb("vn", [128], BF16)
    GW, B_GW = hb("GW", [128])
    t1, B_t1 = hb("t1", [128])
    mb, B_mb = hb("mb", [128])
    esb = [sb(f"esb{i}", [128, 640]) for i in range(2)]
    B_esb = [Buf(f"esb{i}") for i in range(2)]
    PT = [sb(f"PT{i}", [128, 640], BF16) for i in range(2)]
    B_PT = [Buf(f"PT{i}") for i in range(2)]
    sm, B_sm = hb("sm", [4])
    S32 = sb("S32", [128, H, 128])
    B_S32 = [Buf(f"S32_{h}") for h in range(H)]
    Sb = sb("Sb", [128, H, 3, 128], BF16)
    B_Sb = [[Buf(f"Sb{h}_{s}") for s in range(3)] for h in range(H)]
    mk.op(POOL, lambda e: e.memset(S32[:], 0.0), writes=B_S32)
    mk.op(POOL, lambda e: e.memset(Sb[:], 0.0), writes=[b for r in B_Sb for b in r])
    ptk = ps("ptk", [128, 8, 128], BF16)
    ptv = ps("ptv", [128, 8, 128], BF16)
    B_ptk, B_ptv = Buf("ptk"), Buf("ptv")
    ptb = ps("ptb", [128, 8, 128], BF16)
    B_ptb1 = Buf("ptb")
    B_ptb = [B_ptb1 for h in range(H)]

    zgv = g["zg_d"].rearrange("(g d) s -> d g s", d=128)
    zqv = g["zqk_d"].rearrange("(g d) s -> d g s", d=128)
    vav = g["va_d"].rearrange("s (h d) -> s h d", d=128)

    def do_tile(t):
        i = t % 2
        w = 0
        sl = t % 5
        ts = slice(t * 128, (t + 1) * 128)
        mk.dma(SP, f"p2zg{i}", zg[i][:], zgv[:, :, ts], reads=[g["B_zg"]], writes=[B_zgt[i]])
        mk.dma(SP, "p2zq", zq[i][:], zqv[:, 0:H, ts], reads=[g["B_zqk"]], writes=[B_zq[i]])
        mk.dma(SP, f"p2kr{sl}", kring[:, sl, :, :], zqv[:, H:2 * H, ts], reads=[g["B_zqk"]], writes=[B_kr[sl]])
        mk.dma(SP, f"p2vr{sl}", vring[:, sl, :, 0:128], vav[ts, :, :], reads=[g["B_va"]], writes=[B_vr[sl]])
        mk.dma(SP, f"p2zt{i}", zt[i][:], g["ztok_d"][ts, :], reads=[g["B_ztok"]], writes=[B_zt[i]])
        if DBG["p2_stage"] <= 1:
            return
        gs = gsc[i]
        a_ap = zt[i][:, 1024:1024 + H]
        b_ap = zt[i][:, 1024 + H:1024 + 2 * H]
        bet, nbet, xa, ax, ee, lg, gg, eG, dl, kd = [gs[:, k * H:(k + 1) * H] for k in range(10)]
        cdb = gs[:, 10 * H:12 * H]
        BG = B_gsc[i]
        mk.op(ACT, lambda e: e.activation(out=bet, in_=b_ap, func=AF.Sigmoid), reads=[B_zt[i]], writes=[BG])
        mk.op(DVE, lambda e: e.tensor_scalar(out=nbet, in0=bet, scalar1=-1.0, scalar2=None, op0=ALU.mult), writes=[BG])
        mk.op(DVE, lambda e: e.tensor_tensor(out=xa, in0=a_ap, in1=dtb, op=ALU.add), reads=[B_zt[i], B_hv], writes=[BG])
        mk.op(ACT, lambda e: e.activation(out=ax, in_=xa, func=AF.Abs), writes=[BG])
        mk.op(ACT, lambda e: e.activation(out=ee, in_=ax, func=AF.Exp, scale=-1.0), writes=[BG])
        mk.op(ACT, lambda e: e.activation(out=lg, in_=ee, func=AF.Ln, bias=1.0), writes=[BG])
        mk.op(DVE, lambda e: e.scalar_tensor_tensor(out=lg, in0=xa, scalar=0.0, in1=lg, op0=ALU.max, op1=ALU.add),
              writes=[BG])
        mk.op(DVE, lambda e: e.tensor_tensor(out=gg, in0=lg, in1=nA, op=ALU.mult), reads=[B_hv], writes=[BG])
        pG = pg[:, 0:3 * H]
        if DBG.get("skipG"):
            mk.op(POOL, lambda e: e.memset(Gs[i][:], 0.0), writes=[B_Gs[i]])
        else:
            mk.op(PE, lambda e: e.matmul(pG[:, 0:H], lhsT=tri, rhs=gg, start=True, stop=True),
                  reads=[BG, B_consts], writes=[B_pgG], signal=False)
            mk.op(PE, lambda e: e.matmul(pG[:, H:2 * H], lhsT=sel0, rhs=gg, start=True, stop=True),
                  reads=[BG, B_consts], writes=[B_pgG], signal=False)
            mk.op(PE, lambda e: e.matmul(pG[:, 2 * H:3 * H], lhsT=sel1, rhs=gg, start=True, stop=True),
                  reads=[BG, B_consts], writes=[B_pgG])
            mk.op(ACT, lambda e: e.activation(out=Gs[i][:], in_=pG, func=AF.Copy), reads=[B_pgG], writes=[B_Gs[i]])
        Gc = Gs[i][:, 0:H]
        mk.op(ACT, lambda e: e.activation(out=eG, in_=Gc, func=AF.Exp), reads=[B_Gs[i]], writes=[BG])
        mk.op(DVE, lambda e: e.tensor_tensor(out=dl[0:64, :], in0=Gs[i][0:64, H:2 * H], in1=Gs[i][0:64, 0:H],
                                             op=ALU.subtract), reads=[B_Gs[i]], writes=[BG])
        mk.op(DVE, lambda e: e.tensor_tensor(out=dl[64:128, :], in0=Gs[i][64:128, 2 * H:3 * H], in1=Gs[i][64:128, 0:H],
                                             op=ALU.subtract), reads=[B_Gs[i]], writes=[BG])
        mk.op(ACT, lambda e: e.activation(out=kd, in_=dl, func=AF.Exp), writes=[BG])
        mk.op(ACT, lambda e: e.activation(out=cdb, in_=Gs[i][:, H:3 * H], func=AF.Exp), reads=[B_Gs[i]], writes=[BG])
        mk.op(ACT, lambda e: e.activation(out=qkb[i][:], in_=zg[i][:], func=AF.Copy),
              reads=[B_zgt[i]], writes=[B_qkb[i]])
        if DBG["p2_stage"] <= 2:
            return

        m_lo = max(0, t - 4)
        mis = [m - (t - 4) for m in range(m_lo, t + 1)]
        mlo = mis[0]
        sc = HD ** -0.5

        def s_att(h):
            e2 = (t * H + h) % 2
            for mi in mis:
                m = t - 4 + mi
                dst = psA[:, mi * 128:(mi + 1) * 128] if mi < 4 else pg[:, 384:512]
                Bd = B_psA if mi < 4 else B_pgS
                mk.op(PE, lambda e, m=m, dst=dst: e.matmul(dst, lhsT=kring[:, m % 5, h, :], rhs=zq[i][:, h, :],
                                                            start=True, stop=True),
                      reads=[B_kr[m % 5], B_zq[i]], writes=[Bd], signal=(mi == mis[-1] or mi == 3))
            if DBG["p2_stage"] <= 2.1:
                return
            if mlo < 4:
                mk.op(ACT, lambda e: e.activation(out=esb[e2][:, mlo * 128:512], in_=psA[:, mlo * 128:512],
                                                  func=AF.Exp, scale=sc),
                      reads=[B_psA], writes=[B_esb[e2]])
            mk.op(ACT, lambda e: e.activation(out=esb[e2][:, 512:640], in_=pg[:, 384:512], func=AF.Exp, scale=sc),
                  reads=[B_pgS], writes=[B_esb[e2]])
            if DBG["p2_stage"] <= 2.2:
                return
            mk.op(POOL, lambda e: e.tensor_tensor(
                out=PT[e2][:, mlo * 128:640], in0=esb[e2][:, mlo * 128:640],
                in1=expB[:, h * 640 + mlo * 128:(h + 1) * 640], op=ALU.mult),
                reads=[B_esb[e2], B_expB], writes=[B_PT[e2]])
            if DBG["p2_stage"] <= 2.3:
                return
            for mi in mis:
                m = t - 4 + mi
                mk.op(PE, lambda e, m=m, mi=mi: e.matmul(pg[:, 128:258], lhsT=PT[e2][:, mi * 128:(mi + 1) * 128],
                                                         rhs=vring[:, m % 5, h, 0:130],
                                                         start=(mi == mis[0]), stop=(mi == mis[-1])),
                      reads=[B_PT[e2], B_vr[m % 5]], writes=[B_pgO], signal=(mi == mis[-1]))
            if DBG["p2_stage"] <= 2.4:
                return
            mk.op(DVE, lambda e: e.reciprocal(out=sm[w][:, h, 0:1], in_=pg[:, 256:257]),
                  reads=[B_pgO], writes=[B_sm[w][h]])
            gb_ap = zt[i][:, 1040 + 1024 + h * 128:1040 + 1024 + (h + 1) * 128]
            mk.op(DVE, lambda e: e.scalar_tensor_tensor(
                out=mb[w][:, h, :], in0=pg[:, 128:256], scalar=sm[w][:, h, 0:1], in1=gb_ap,
                op0=ALU.mult, op1=ALU.mult), reads=[B_pgO, B_sm[w][h], B_zt[i]], writes=[B_mb[w][h]])
        for h in range(H):
            s_att(h)
        if DBG["p2_stage"] <= 3:
            return

        def stage(fn):
            for h in range(H):
                fn(h)

        hq = {}

        def s_pre(h):
            pk, Bpk = ptk[:, h, :], B_ptk
            mk.op(PE, lambda e: e.transpose(pk, qkb[i][:, 8 + h, :], ident_bf[:]), reads=[B_qkb[i], B_identbf], writes=[Bpk])
            mk.op(ACT, lambda e: e.activation(out=egk[w][:, h, :], in_=pk, func=AF.Copy, scale=eG[:, h:h + 1]),
                  reads=[Bpk, BG], writes=[B_egk[w][h]])
            mk.op(DVE, lambda e: e.tensor_scalar(out=kdec[w][:, h, :], in0=pk, scalar1=kd[:, h:h + 1], scalar2=None,
                                                 op0=ALU.mult), reads=[Bpk, BG], writes=[B_kdec[w][h]])
            pv, Bpv = ptv[:, h, :], B_ptv
            mk.op(PE, lambda e: e.transpose(pv, qkb[i][:, 16 + h, :], ident_bf[:]), reads=[B_qkb[i], B_identbf], writes=[Bpv])
            mk.op(ACT, lambda e: e.activation(out=vb[w][:, h, :], in_=pv, func=AF.Copy), reads=[Bpv], writes=[B_vb[w][h]])
            gr, Bgr = grep_r[h % 2], B_grep_r[h % 2]
            er, Ber = eGr_r[h % 2], B_eGr_r[h % 2]
            mk.op(POOL, lambda e: e.tensor_scalar(out=gr[:], in0=ones, scalar1=gg[:, h:h + 1], scalar2=None,
                                                  op0=ALU.mult), reads=[BG, B_consts], writes=[Bgr])
            pgr, Bpgr = getq()
            mk.op(PE, lambda e: e.matmul(pgr, lhsT=gr[:], rhs=tri, start=True, stop=True),
                  reads=[Bgr, B_consts], writes=[Bpgr])
            mk.op(DVE, lambda e: e.scalar_tensor_tensor(out=tG[w][:, h, :], in0=pgr, scalar=Gs[i][:, h:h + 1], in1=maskA,
                                                        op0=ALU.subtract, op1=ALU.add),
                  reads=[Bpgr, B_Gs[i], B_consts], writes=[B_tG[w][h]])
            mk.op(ACT, lambda e: e.activation(out=DT[w][:, h, :], in_=tG[w][:, h, :], func=AF.Exp),
                  reads=[B_tG[w][h]], writes=[B_DT[w][h]])
            mk.op(ACT, lambda e: e.activation(out=er[:], in_=pgr, func=AF.Exp), reads=[Bpgr], writes=[Ber])
            mk.op(POOL, lambda e: e.tensor_tensor(out=qdT[w][:, h, :], in0=zg[i][:, h, :], in1=er[:], op=ALU.mult),
                  reads=[B_zgt[i], Ber], writes=[B_qdT[w][h]])
            pkk, Bpkk = getq()
            mk.op(PE, lambda e: e.matmul(pkk, lhsT=qkb[i][:, 8 + h, :], rhs=qkb[i][:, 8 + h, :], start=True, stop=True),
                  reads=[B_qkb[i]], writes=[Bpkk])
            pat, Bpat = getq()
            mk.op(PE, lambda e: e.matmul(pat, lhsT=qkb[i][:, 8 + h, :], rhs=qkb[i][:, h, :], start=True, stop=True),
                  reads=[B_qkb[i]], writes=[Bpat])
            mk.op(DVE, lambda e: e.tensor_tensor(out=attnT[w][:, h, :], in0=pat, in1=DT[w][:, h, :], op=ALU.mult),
                  reads=[Bpat, B_DT[w][h]], writes=[B_attnT[w][h]])
            mk.op(DVE, lambda e: e.scalar_tensor_tensor(out=X0[w][:, h, :], in0=pkk, scalar=nbet[:, h:h + 1],
                                                        in1=DT[w][:, h, :], op0=ALU.mult, op1=ALU.mult),
                  reads=[Bpkk, BG, B_DT[w][h]], writes=[B_X0[w][h]])
            def msk(dst, Bd, srcb, Bs, col):
                mk.op(POOL, lambda e: e.tensor_tensor(out=dst[w][:, h, :], in0=srcb[w][:, h, :],
                                                      in1=consts[:, col:col + 128], op=ALU.mult),
                      reads=[Bs[w][h], B_consts], writes=[Bd[w][h]])
            msk(Xa, B_Xa, X0, B_X0, C_ST)
            msk(Da, B_Da, X0, B_X0, C_MD)
            msk(C1m, B_C1m, X0, B_X0, C_MC1)
            msk(C2m, B_C2m, X0, B_X0, C_MC2)
            mk.op(PE, lambda e: e.transpose(ptb[:, h, :], Xa[w][:, h, :], ident_bf[:]),
                  reads=[B_Xa[w][h], B_identbf], writes=[B_ptb[h]])
            mk.op(ACT, lambda e: e.activation(out=XTa[w][:, h, :], in_=ptb[:, h, :], func=AF.Copy),
                  reads=[B_ptb[h]], writes=[B_XTa[w][h]])
            msk(DTa, B_DTa, XTa, B_XTa, C_MDT)
            msk(C1T, B_C1T, XTa, B_XTa, C_MC1T)
            msk(C2T, B_C2T, XTa, B_XTa, C_MC2T)
            mk.op(POOL, lambda e: e.tensor_tensor(out=Pa[w][:, h, :], in0=Da[w][:, h, :], in1=ident_bf[:], op=ALU.add),
                  reads=[B_Da[w][h], B_identbf], writes=[B_Pa[w][h]])
            mk.op(POOL, lambda e: e.tensor_tensor(out=PTa[w][:, h, :], in0=DTa[w][:, h, :], in1=ident_bf[:], op=ALU.add),
                  reads=[B_DTa[w][h], B_identbf], writes=[B_PTa[w][h]])
            hq[h] = dict(D=(Da, B_Da), DT=(DTa, B_DTa), Dn=(Db, B_Db), DTn=(DTb, B_DTb),
                         P=(Pa, B_Pa), Pn=(Pb, B_Pb), PT=(PTa, B_PTa), PTn=(PTb, B_PTb))

        stage(s_pre)
        if DBG["p2_stage"] <= 4:
            return

        if DBG.get("ginv", 1):
            def getbank():
                b = state["q"] % NBK
                state["q"] += 1
                return pq_t[b], B_bank[b]

            def grp(buf, B, hg):
                return buf[w][:, 4 * hg:4 * hg + 4, :], [B[w][4 * hg + q] for q in range(4)]

            def gmm(hg, L_, BL, R_, BR):
                pb, Bb = getbank()
                for q in range(4):
                    h = 4 * hg + q
                    mk.op(PE, lambda e, h=h, q=q: e.matmul(pb[:, q, :], lhsT=L_[w][:, h, :], rhs=R_[w][:, h, :], start=True, stop=True),
                          reads=[BL[w][h], BR[w][h]], writes=[Bb], signal=(q == 3))
                return pb, Bb

            def gevac(E, dst, Bd, hg, pb, Bb):
                o_, Bo = grp(dst, Bd, hg)
                if E is ACT:
                    mk.op(ACT, lambda e: e.activation(out=o_, in_=pb[:], func=AF.Copy), reads=[Bb], writes=Bo)
                else:
                    mk.op(DVE, lambda e: e.tensor_copy(out=o_, in_=pb[:]), reads=[Bb], writes=Bo)

            def gadd(dst, Bd, base, Bbase, hg, pb, Bb):
                o_, Bo = grp(dst, Bd, hg)
                b_, Bbs = grp(base, Bbase, hg)
                mk.op(DVE, lambda e: e.tensor_tensor(out=o_, in0=pb[:], in1=b_, op=ALU.add), reads=[Bb] + Bbs, writes=Bo)

            d = hq[0]
            for lev in range(1, 4):
                (Dm, BD), (DT_, BDT), (Dn, BDn), (DTn, BDTn) = d["D"], d["DT"], d["Dn"], d["DTn"]
                (Px, BPx), (Pnx, BPnx), (PTx, BPTx), (PTnx, BPTnx) = d["P"], d["Pn"], d["PT"], d["PTn"]
                for hg in range(2):
                    pb, Bb = gmm(hg, Dm, BD, DT_, BDT)
                    gevac(ACT, DTn, BDTn, hg, pb, Bb)
                if lev < 3:
                    for hg in range(2):
                        pb, Bb = gmm(hg, DT_, BDT, Dm, BD)
                        gevac(DVE, Dn, BDn, hg, pb, Bb)
                for hg in range(2):
                    pb, Bb = gmm(hg, DTn, BDTn, Px, BPx)
                    gadd(Pnx, BPnx, Px, BPx, hg, pb, Bb)
                for hg in range(2):
                    pb, Bb = gmm(hg, Px, BPx, DTn, BDTn)
                    gadd(PTnx, BPTnx, PTx, BPTx, hg, pb, Bb)
                d["D"], d["Dn"] = d["Dn"], d["D"]
                d["DT"], d["DTn"] = d["DTn"], d["DT"]
                d["P"], d["Pn"] = d["Pn"], d["P"]
                d["PT"], d["PTn"] = d["PTn"], d["PT"]
            (Px, BPx), (Pnx, BPnx), (PTx, BPTx), (PTnx, BPTnx) = d["P"], d["Pn"], d["PT"], d["PTn"]
            for hg in range(2):
                pb, Bb = gmm(hg, C1T, B_C1T, Px, BPx)
                gevac(ACT, Yb, B_Yb, hg, pb, Bb)
            for hg in range(2):
                pb, Bb = gmm(hg, C1m, B_C1m, PTx, BPTx)
                gevac(ACT, Ypb, B_Ypb, hg, pb, Bb)
            for hg in range(2):
                pb, Bb = gmm(hg, PTx, BPTx, Yb, B_Yb)
                gadd(Pnx, BPnx, Px, BPx, hg, pb, Bb)
            for hg in range(2):
                pb, Bb = gmm(hg, Px, BPx, Ypb, B_Ypb)
                gadd(PTnx, BPTnx, PTx, BPTx, hg, pb, Bb)
            d["P"], d["Pn"] = d["Pn"], d["P"]
            d["PT"], d["PTn"] = d["PTn"], d["PT"]
            (Px, BPx), (Pnx, BPnx), (PTx, BPTx) = d["P"], d["Pn"], d["PT"]
            for hg in range(2):
                pb, Bb = gmm(hg, C2T, B_C2T, Px, BPx)
                gevac(ACT, Yb, B_Yb, hg, pb, Bb)
            for hg in range(2):
                pb, Bb = gmm(hg, PTx, BPTx, Yb, B_Yb)
                gadd(Pnx, BPnx, Px, BPx, hg, pb, Bb)
            d["P"], d["Pn"] = d["Pn"], d["P"]
            for h in range(H):
                hq[h] = d

            if DBG["p2_stage"] <= 5:
                return

            (Px, BPx) = hq[0]["P"]
            for hg in range(2):
                pb, Bb = gmm(hg, Px, BPx, vb, B_vb)
                for q in range(4):
                    h = 4 * hg + q
                    mk.op(ACT, lambda e, h=h, q=q, pb=pb: e.activation(out=usb[w][:, h, :], in_=pb[:, q, :], func=AF.Copy,
                                                                     scale=bet[:, h:h + 1]),
                          reads=[Bb, BG], writes=[B_usb[w][h]])
            for hg in range(2):
                pb, Bb = gmm(hg, egk, B_egk, Px, BPx)
                gevac(ACT, wTb, B_wTb, hg, pb, Bb)
        else:
            def evac(E, dst, Bd, psrc, Bp):
                if E is ACT:
                    mk.op(ACT, lambda e: e.activation(out=dst, in_=psrc, func=AF.Copy), reads=[Bp], writes=[Bd])
                else:
                    mk.op(DVE, lambda e: e.tensor_copy(out=dst, in_=psrc), reads=[Bp], writes=[Bd])

            def mm(lhsT, Bl, rhs, Br):
                p_, Bp_ = getq()
                mk.op(PE, lambda e: e.matmul(p_, lhsT=lhsT, rhs=rhs, start=True, stop=True), reads=[Bl, Br], writes=[Bp_])
                return p_, Bp_

            def addto(dst, Bd, p_, Bp_, base, Bb):
                mk.op(DVE, lambda e: e.tensor_tensor(out=dst, in0=p_, in1=base, op=ALU.add), reads=[Bp_, Bb], writes=[Bd])

            for lev in range(1, 4):
                def s_sq(h, lev=lev):
                    d = hq[h]
                    (Dm, BD), (DT_, BDT), (Dn, BDn), (DTn, BDTn) = d["D"], d["DT"], d["Dn"], d["DTn"]
                    p1, Bp1 = mm(Dm[w][:, h, :], BD[w][h], DT_[w][:, h, :], BDT[w][h])
                    evac(ACT, DTn[w][:, h, :], BDTn[w][h], p1, Bp1)
                    if lev < 3:
                        p2, Bp2 = mm(DT_[w][:, h, :], BDT[w][h], Dm[w][:, h, :], BD[w][h])
                        evac(DVE, Dn[w][:, h, :], BDn[w][h], p2, Bp2)
                stage(s_sq)

                def s_p(h, lev=lev):
                    d = hq[h]
                    (DTn, BDTn), (P, BP), (Pn, BPn), (PT, BPT), (PTn, BPTn) = d["DTn"], d["P"], d["Pn"], d["PT"], d["PTn"]
                    p3, Bp3 = mm(DTn[w][:, h, :], BDTn[w][h], P[w][:, h, :], BP[w][h])
                    addto(Pn[w][:, h, :], BPn[w][h], p3, Bp3, P[w][:, h, :], BP[w][h])
                    p4, Bp4 = mm(P[w][:, h, :], BP[w][h], DTn[w][:, h, :], BDTn[w][h])
                    addto(PTn[w][:, h, :], BPTn[w][h], p4, Bp4, PT[w][:, h, :], BPT[w][h])
                    d["D"], d["Dn"] = d["Dn"], d["D"]
                    d["DT"], d["DTn"] = d["DTn"], d["DT"]
                    d["P"], d["Pn"] = d["Pn"], d["P"]
                    d["PT"], d["PTn"] = d["PTn"], d["PT"]
                stage(s_p)

            def s_m1(h):
                d = hq[h]
                (P, BP), (Pn, BPn), (PT, BPT), (PTn, BPTn) = d["P"], d["Pn"], d["PT"], d["PTn"]
                py, Bpy = mm(C1T[w][:, h, :], B_C1T[w][h], P[w][:, h, :], BP[w][h])
                evac(ACT, Yb[w][:, h, :], B_Yb[w][h], py, Bpy)
                py2, Bpy2 = mm(C1m[w][:, h, :], B_C1m[w][h], PT[w][:, h, :], BPT[w][h])
                evac(ACT, Ypb[w][:, h, :], B_Ypb[w][h], py2, Bpy2)
                pz, Bpz = mm(PT[w][:, h, :], BPT[w][h], Yb[w][:, h, :], B_Yb[w][h])
                addto(Pn[w][:, h, :], BPn[w][h], pz, Bpz, P[w][:, h, :], BP[w][h])
                pz2, Bpz2 = mm(P[w][:, h, :], BP[w][h], Ypb[w][:, h, :], B_Ypb[w][h])
                addto(PTn[w][:, h, :], BPTn[w][h], pz2, Bpz2, PT[w][:, h, :], BPT[w][h])
                d["P"], d["Pn"] = d["Pn"], d["P"]
                d["PT"], d["PTn"] = d["PTn"], d["PT"]
            stage(s_m1)

            def s_m2(h):
                d = hq[h]
                (P, BP), (Pn, BPn), (PT, BPT) = d["P"], d["Pn"], d["PT"]
                py, Bpy = mm(C2T[w][:, h, :], B_C2T[w][h], P[w][:, h, :], BP[w][h])
                evac(ACT, Yb[w][:, h, :], B_Yb[w][h], py, Bpy)
                pz, Bpz = mm(PT[w][:, h, :], BPT[w][h], Yb[w][:, h, :], B_Yb[w][h])
                addto(Pn[w][:, h, :], BPn[w][h], pz, Bpz, P[w][:, h, :], BP[w][h])
                d["P"], d["Pn"] = d["Pn"], d["P"]
            stage(s_m2)

            if DBG["p2_stage"] <= 5:
                return

            def s_uw(h):
                (P, BP) = hq[h]["P"]
                pu, Bpu = getq()
                mk.op(PE, lambda e: e.matmul(pu, lhsT=P[w][:, h, :], rhs=vb[w][:, h, :], start=True, stop=True),
                      reads=[BP[w][h], B_vb[w][h]], writes=[Bpu])
                mk.op(ACT, lambda e: e.activation(out=usb[w][:, h, :], in_=pu, func=AF.Copy, scale=bet[:, h:h + 1]),
                      reads=[Bpu, BG], writes=[B_usb[w][h]])
                pw, Bpw = getq()
                mk.op(PE, lambda e: e.matmul(pw, lhsT=egk[w][:, h, :], rhs=P[w][:, h, :], start=True, stop=True),
                      reads=[BP[w][h], B_egk[w][h]], writes=[Bpw])
                mk.op(ACT, lambda e: e.activation(out=wTb[w][:, h, :], in_=pw, func=AF.Copy), reads=[Bpw], writes=[B_wTb[w][h]])
            stage(s_uw)
        if DBG["p2_stage"] <= 6:
            return

        pws = {}
        for c in range(2):
            n = 2 * t + c
            cur, nxt = n % 3, (n + 1) % 3
            rs = slice(c * 64, (c + 1) * 64)

            def s_ws(h, c=c, cur=cur, rs=rs):
                if c == 0:
                    pws[h] = getq()
                pw_, Bpw_ = pws[h]
                mk.op(PE, lambda e: e.matmul(pw_[rs, :], lhsT=wTb[w][:, h, rs], rhs=Sb[:, h, cur, :], start=True, stop=True),
                      reads=[B_wTb[w][h], B_Sb[h][cur]], writes=[Bpw_])
                mk.op(DVE, lambda e: e.scalar_tensor_tensor(out=vn[w][rs, h, :], in0=pw_[rs, :], scalar=nbet[rs, h:h + 1],
                                                            in1=usb[w][rs, h, :], op0=ALU.mult, op1=ALU.add),
                      reads=[Bpw_, BG, B_usb[w][h]], writes=[B_vn[w][h]])
            stage(s_ws)

            def s_ds(h, c=c, nxt=nxt, rs=rs):
                pd, Bpd = getq()
                mk.op(PE, lambda e: e.matmul(pd, lhsT=kdec[w][rs, h, :], rhs=vn[w][rs, h, :], start=True, stop=True),
                      reads=[B_kdec[w][h], B_vn[w][h]], writes=[Bpd])
                mk.op(DVE, lambda e: e.scalar_tensor_tensor(out=S32[:, h, :], in0=S32[:, h, :],
                                                            scalar=cdb[:, c * H + h:c * H + h + 1], in1=pd,
                                                            op0=ALU.mult, op1=ALU.add),
                      reads=[Bpd, BG], writes=[B_S32[h]])
                mk.op(ACT, lambda e: e.activation(out=Sb[:, h, nxt, :], in_=S32[:, h, :], func=AF.Copy),
                      reads=[B_S32[h]], writes=[B_Sb[h][nxt]])
            stage(s_ds)
        if DBG["p2_stage"] <= 7:
            return

        def s_out(h):
            po, Bpo = getq()
            s0, s1 = (2 * t) % 3, (2 * t + 1) % 3
            mk.op(PE, lambda e: e.matmul(po[0:64, :], lhsT=qdT[w][:, h, 0:64], rhs=Sb[:, h, s0, :], start=True, stop=False,
                                         skip_group_check=True),
                  reads=[B_qdT[w][h], B_Sb[h][s0]], writes=[Bpo], signal=False)
            mk.op(PE, lambda e: e.matmul(po[64:128, :], lhsT=qdT[w][:, h, 64:128], rhs=Sb[:, h, s1, :], start=True, stop=False,
                                         skip_group_check=True),
                  reads=[B_qdT[w][h], B_Sb[h][s1]], writes=[Bpo], signal=False)
            mk.op(PE, lambda e: e.matmul(po, lhsT=attnT[w][:, h, :], rhs=vn[w][:, h, :], start=False, stop=True,
                                         skip_group_check=True),
                  reads=[B_attnT[w][h], B_vn[w][h]], writes=[Bpo])
            mk.op(ACT, lambda e: e.activation(out=t1[w][:, h, :], in_=po, func=AF.Square, accum_out=sm[w][:, h, 1:2]),
                  reads=[Bpo], writes=[B_t1[w][h], B_sm[w][h]])
            mk.op(ACT, lambda e: e.activation(out=sm[w][:, h, 2:3], in_=sm[w][:, h, 1:2], func=AF.Sqrt, bias=EPS * HD),
                  writes=[B_sm[w][h]])
            mk.op(DVE, lambda e: e.reciprocal(out=sm[w][:, h, 2:3], in_=sm[w][:, h, 2:3]), writes=[B_sm[w][h]])
            ga_ap = zt[i][:, 1040 + h * 128:1040 + (h + 1) * 128]
            mk.op(POOL, lambda e: e.tensor_tensor(out=GW[w][:, h, :], in0=zt[i][:, h * 128:(h + 1) * 128], in1=gn, op=ALU.mult),
                  reads=[B_zt[i], B_hv], writes=[B_GW[w][h]])
            mk.op(POOL, lambda e: e.tensor_tensor(out=GW[w][:, h, :], in0=GW[w][:, h, :], in1=ga_ap, op=ALU.mult),
                  reads=[B_zt[i]], writes=[B_GW[w][h]])
            mk.op(DVE, lambda e: e.scalar_tensor_tensor(out=t1[w][:, h, :], in0=po, scalar=sm[w][:, h, 2:3], in1=GW[w][:, h, :],
                                                        op0=ALU.mult, op1=ALU.mult),
                  reads=[Bpo, B_sm[w][h], B_GW[w][h]], writes=[B_t1[w][h]])
            mk.op(POOL, lambda e: e.tensor_tensor(out=mtile[i][:, h * 128:(h + 1) * 128], in0=t1[w][:, h, :], in1=mb[w][:, h, :],
                                                  op=ALU.add),
                  reads=[B_t1[w][h], B_mb[w][h]], writes=[B_mtile[i]])
        stage(s_out)
        mk.dma(SP, f"p2m{i}", g["m_d"][ts, :], mtile[i][:], reads=[B_mtile[i]], writes=[g["B_m"]])

    for t in range(NT):
        do_tile(t)


def make_consts():
    c = np.zeros((128, C_N), np.float32)
    p = np.arange(128)[:, None]
    f = np.arange(128)[None, :]
    same = (p // 64) == (f // 64)
    c[:, C_ID:C_ID + 128] = np.eye(128)
    c[:, C_ONE:C_ONE + 128] = 1.0
    c[:, C_MA:C_MA + 128] = np.where(same & (f >= p), 0.0, NEG)
    c[:, C_ST:C_ST + 128] = (same & (f > p))
    c[:, C_TRI:C_TRI + 128] = (same & (p <= f))
    kl = np.arange(128)[:, None, None]
    mi = np.arange(5)[None, :, None]
    ql = np.arange(128)[None, None, :]
    dch = 2 * (mi - 4) + kl // 64 - ql // 64
    c[:, C_AM:C_AM + 640] = ((dch >= -8) & (dch <= 0)).reshape(128, 640)
    st = same & (f > p)
    bd16 = (p // 16) == (f // 16)
    bd32 = (p // 32) == (f // 32)
    md, mc1, mc2 = st & bd16, st & bd32 & ~bd16, st & ~bd32
    c[:, C_MD:C_MD + 128] = md
    c[:, C_MC1:C_MC1 + 128] = mc1
    c[:, C_MC2:C_MC2 + 128] = mc2
    c[:, C_MDT:C_MDT + 128] = md.T
    c[:, C_MC1T:C_MC1T + 128] = mc1.T
    c[:, C_MC2T:C_MC2T + 128] = mc2.T
    c[:, C_S0:C_S0 + 128] = (p < 64)
    c[:, C_S1:C_S1 + 128] = (p >= 64)
    return c


def layout_inputs(x, c, w_ada, b_ada, norm_mix, norm_mlp, w_in, conv_w, a_log, dt_bias,
                  gdn_norm, rel_bias, w_out, w_ff_in, w_ff_out, final_norm):
    f = lambda a: np.ascontiguousarray(np.asarray(a, dtype=np.float32))
    L = w_ada.shape[0]
    kl = np.arange(128)[:, None, None]
    mi = np.arange(5)[None, :, None]
    ql = np.arange(128)[None, None, :]
    idx = np.clip(ql + 128 * (4 - mi) - kl, -256, 256) + 256
    rb = np.asarray(rel_bias, np.float32)
    biasT = rb[:, :, idx]
    biasT = f(biasT.transpose(0, 2, 1, 3, 4).reshape(L, 128, H * 640))
    cw = np.asarray(conv_w, np.float32).reshape(L, 4, 24, 128).transpose(3, 0, 2, 1).reshape(128, L * 96)
    shared = dict(
        w_ada=f(w_ada), b_ada=f(b_ada), norm_mix=f(norm_mix), norm_mlp=f(norm_mlp), w_in=f(w_in),
        cw=f(cw), a_log=f(np.asarray(a_log).reshape(1, -1)), dt_bias=f(np.asarray(dt_bias).reshape(1, -1)),
        gdn_norm=f(np.asarray(gdn_norm).reshape(1, -1)), biasT=biasT, w_out=f(w_out), w_ff_in=f(w_ff_in),
        w_ff_out=f(w_ff_out), final_norm=f(np.asarray(final_norm).reshape(1, -1)), consts=make_consts())
    x = np.asarray(x, np.float32)
    c = np.asarray(c, np.float32)
    per = []
    for b in range(x.shape[0]):
        d = dict(shared)
        d["x"] = f(x[b])
        d["cT"] = f(c[b].reshape(KC, 128).T)
        per.append(d)
    return per


def kernel(**inputs):
    x = np.asarray(inputs["x"])
    B, S, _ = x.shape
    L = np.asarray(inputs["w_ada"]).shape[0]
    per = layout_inputs(**inputs)
    nc, mk = build_program(S, L)
    n = B
    in_maps = [per[j] for j in range(n)]
    res = run_bass_kernel_spmd(nc, in_maps, core_ids=list(range(n)))
    out = np.stack([np.asarray(res.results[b]["out"], np.float32) for b in range(B)], axis=0)
    return out
```

```python
import numpy as np
from contextlib import ExitStack
import concourse.bass as bass
import concourse.mybir as mybir
from concourse.bass_utils import run_bass_kernel_spmd

F32 = mybir.dt.float32
BF16 = mybir.dt.bfloat16
AF = mybir.ActivationFunctionType
ALU = mybir.AluOpType
AX = mybir.AxisListType

D = 1024
H = 8
HD = 128
KC = 8
DFF = 4096
INW = 9232
EPS = 1e-6
LIM = 24000
NEG = -30000.0

C_ID, C_ONE, C_MA, C_ST, C_TRI, C_AM = 0, 128, 256, 384, 512, 640
C_S0, C_S1 = 1280, 1408
C_MD, C_MC1, C_MC2, C_MDT, C_MC1T, C_MC2T = 1536, 1664, 1792, 1920, 2048, 2176
C_N = 2304


class Buf:
    __slots__ = ("name", "w", "r")

    def __init__(self, name):
        self.name = name
        self.w = None
        self.r = {}


class Src:
    def __init__(self, mk, name, unit):
        self.mk = mk
        self.name = name
        self.unit = unit
        self.sems = []
        self.count = 0
        self.finals = []

    def _roll(self):
        if not self.sems or self.count + self.unit > LIM:
            if self.sems:
                self.finals.append(self.count)
            self.sems.append(self.mk.new_sem())
            self.count = 0

    def next_event(self):
        self._roll()
        self.count += self.unit
        return (self, len(self.sems) - 1, self.count)

    def peek_event(self):
        self._roll()
        return (self, len(self.sems) - 1, self.count + self.unit)


class Eng(Src):
    def __init__(self, mk, name, key, same_sync):
        super().__init__(mk, name, 1)
        self.key = key
        self.same_sync = same_sync
        self.thunks = []
        self.waited = {}
        self.is_eng = True


class MK:
    def __init__(self, nc, es):
        self.nc = nc
        self.es = es
        self.nsem = 0
        self.PE = Eng(self, "pe", "tensor", False)
        self.ACT = Eng(self, "act", "scalar", True)
        self.DVE = Eng(self, "dve", "vector", True)
        self.POOL = Eng(self, "pool", "gpsimd", True)
        self.SP = Eng(self, "sp", "sync", False)
        self.engs = [self.PE, self.ACT, self.DVE, self.POOL, self.SP]
        self.nops = 0
        self.uid = 0
        self.streams = {}

    def new_sem(self):
        self.nsem += 1
        return self.es.enter_context(self.nc.semaphore(f"s{self.nsem}"))

    def stream(self, name):
        if name in self.streams:
            return self.streams[name]
        s = Src(self, name, 16)
        s.is_eng = False
        s.last = None
        self.streams[name] = s
        return s

    def _wait(self, E, ev):
        src, ep, val = ev
        if src is E and not E.same_sync:
            return
        key = (id(src), ep)
        if E.waited.get(key, 0) >= val:
            return
        if src.is_eng:
            for (sid, e2), v in E.waited.items():
                if sid == id(src) and e2 > ep:
                    return
        E.waited[key] = val
        sem = src.sems[ep]
        E.thunks.append(lambda eng, sem=sem, val=val: eng.wait_ge(sem, val))

    def _deps(self, E, reads, writes):
        for b in reads:
            if b.w is not None:
                self._wait(E, b.w)
        for b in writes:
            if b.w is not None:
                self._wait(E, b.w)
            for ev in b.r.values():
                self._wait(E, ev)

    def op(self, E, fn, reads=(), writes=(), signal=True):
        self.nops += 1
        self._deps(E, reads, writes)
        if signal:
            ev = E.next_event()
            sem = ev[0].sems[ev[1]]
            E.thunks.append(lambda eng, fn=fn, sem=sem: fn(eng).then_inc(sem, 1))
        else:
            assert E is self.PE
            ev = E.peek_event()
            E.thunks.append(lambda eng, fn=fn: fn(eng))
        for b in writes:
            b.w = ev
            b.r = {}
        for b in reads:
            b.r[id(E)] = ev

    def dma(self, Q, st, out, in_, reads=(), writes=(), **kw):
        self.nops += 1
        if isinstance(st, str):
            st = self.stream(st)
        self._deps(Q, reads, writes)
        if st.last is not None:
            self._wait(Q, st.last)
        ev = st.next_event()
        st.last = ev
        sem = ev[0].sems[ev[1]]
        Q.thunks.append(lambda eng, out=out, in_=in_, sem=sem, kw=kw:
                        eng.dma_start(out=out, in_=in_, **kw).then_inc(sem, 16))
        for b in writes:
            b.w = ev
            b.r = {}
        for b in reads:
            b.r[id(st)] = ev

    def final_wait(self, E, bufs):
        for b in bufs:
            if b.w is not None:
                self._wait(E, b.w)

    def barrier(self):
        evs = []
        for E in self.engs:
            if E.sems and E.count > 0:
                evs.append((E, len(E.sems) - 1, E.count))
        for s in self.streams.values():
            if s.last is not None:
                evs.append(s.last)
        for E in self.engs:
            for ev in evs:
                if ev[0] is not E:
                    self._wait(E, ev)

    def flush(self):
        self.barrier()
        nc = self.nc
        with nc.Block() as block:
            for E in self.engs:
                th = E.thunks
                E.thunks = []
                if not th:
                    continue

                def mkfn(th):
                    def f(eng):
                        for t in th:
                            t(eng)
                    return f
                getattr(block, E.key)(mkfn(th))


class _Stop(Exception):
    pass


DBG = {"p2_stage": 99}


def build_program(S, L, debug=False, stop=None):
    assert S % 512 == 0
    NT = S // 128
    nc = bass.Bass("TRN2", target_bir_lowering=False)
    es = ExitStack()
    mk = MK(nc, es)
    PE, ACT, DVE, POOL, SP = mk.PE, mk.ACT, mk.DVE, mk.POOL, mk.SP

    def din(name, shape, dt=F32):
        return nc.dram_tensor(name, list(shape), dt, kind="ExternalInput").ap()

    def dscr(name, shape, dt=F32):
        kind = "ExternalOutput" if debug else "Internal"
        return nc.dram_tensor(name, list(shape), dt, kind=kind).ap()

    x_in = din("x", [S, D])
    cT_in = din("cT", [128, KC])
    wada_in = din("w_ada", [L, D, 6 * D])
    bada_in = din("b_ada", [L, 6 * D])
    nmix_in = din("norm_mix", [L, D])
    nmlp_in = din("norm_mlp", [L, D])
    win_in = din("w_in", [L, D, INW])
    cw_in = din("cw", [128, L * 96])
    alog_in = din("a_log", [1, L * H])
    dtb_in = din("dt_bias", [1, L * H])
    gnorm_in = din("gdn_norm", [1, L * HD])
    biasT_in = din("biasT", [L, 128, H * 640])
    wout_in = din("w_out", [L, D, D])
    w1_in = din("w_ff_in", [L, D, DFF])
    w2_in = din("w_ff_out", [L, DFF, D])
    fnorm_in = din("final_norm", [1, D])
    consts_in = din("consts", [128, C_N])
    out_d = nc.dram_tensor("out", [S, D], F32, kind="ExternalOutput").ap()

    xs_d = dscr("xs", [S, D])
    zg_d = dscr("zg", [3 * D, S])
    zqk_d = dscr("zqk", [2 * D, S], BF16)
    ztok_d = dscr("ztok", [S, 3088])
    va_d = dscr("va", [S, D], BF16)
    m_d = dscr("m", [S, D], BF16)
    mod_d = dscr("modd", [128, 6 * D])

    B_xs = [Buf(f"xs{t}") for t in range(NT)]
    B_zg, B_zqk, B_ztok, B_va, B_m, B_mod = (Buf("zg"), Buf("zqk"), Buf("ztok"), Buf("va"),
                                               Buf("m"), Buf("mod"))
    B_out = Buf("out")

    def sb(name, shape, dt=F32, stack=None):
        mk.uid += 1
        return (stack or es).enter_context(nc.sbuf_tensor(f"{name}_u{mk.uid}", list(shape), dt))

    def ps(name, shape, dt=F32, stack=None):
        mk.uid += 1
        return (stack or es).enter_context(nc.psum_tensor(f"{name}_u{mk.uid}", list(shape), dt))

    consts = sb("consts", [128, C_N])
    B_consts = Buf("consts")
    ident_bf = sb("ident_bf", [128, 128], BF16)
    B_identbf = Buf("identbf")
    cw_sb = sb("cw", [128, L * 96])
    B_cw = Buf("cw")
    crep = sb("crep", [128, KC, 128])
    B_crep = Buf("crep")
    st_c = mk.stream("const")
    mk.dma(SP, st_c, consts[:], consts_in[:, :], writes=[B_consts])
    mk.dma(SP, st_c, cw_sb[:], cw_in[:, :], writes=[B_cw])
    mk.op(ACT, lambda e: e.activation(out=ident_bf[:], in_=consts[:, C_ID:C_ID + 128], func=AF.Copy),
          reads=[B_consts], writes=[B_identbf])
    ident = consts[:, C_ID:C_ID + 128]
    ones = consts[:, C_ONE:C_ONE + 128]
    maskA = consts[:, C_MA:C_MA + 128]
    strict = consts[:, C_ST:C_ST + 128]
    tri = consts[:, C_TRI:C_TRI + 128]

    with ExitStack() as ps0:
        ct = sb("ct", [128, KC], stack=ps0)
        cs = sb("cs", [128, KC], stack=ps0)
        B_ct, B_cs = Buf("ct"), Buf("cs")
        mk.dma(SP, st_c, ct[:], cT_in[:, :], writes=[B_ct])
        mk.op(ACT, lambda e: e.activation(out=cs[:], in_=ct[:], func=AF.Silu), reads=[B_ct], writes=[B_cs])
        for kc in range(KC):
            mk.op(DVE, lambda e, kc=kc: e.tensor_scalar(out=crep[:, kc, :], in0=ones, scalar1=cs[:, kc:kc + 1],
                                                         scalar2=None, op0=ALU.mult),
                  reads=[B_cs, B_consts], writes=[B_crep])
        mk.flush()

    def ck(name):
        if stop == name:
            mk.final_wait(SP, B_xs + [B_zg, B_zqk, B_ztok, B_va, B_m, B_mod])
            mk.flush()
            raise _Stop()

    try:
        _layers(locals())
    except _Stop:
        pass
    es.close()
    return nc, mk


def _layers(G):
    (nc, mk, es, S, L, NT, PE, ACT, DVE, POOL, SP, sb, ps, ck) = [G[k] for k in
        ("nc", "mk", "es", "S", "L", "NT", "PE", "ACT", "DVE", "POOL", "SP", "sb", "ps", "ck")]
    (x_in, wada_in, bada_in, nmix_in, nmlp_in, win_in, alog_in, dtb_in, gnorm_in, biasT_in, wout_in, w1_in, w2_in,
     fnorm_in, out_d, xs_d, zg_d, zqk_d, ztok_d, va_d, m_d, mod_d) = [G[k] for k in
        ("x_in", "wada_in", "bada_in", "nmix_in", "nmlp_in", "win_in", "alog_in", "dtb_in", "gnorm_in", "biasT_in",
         "wout_in", "w1_in", "w2_in", "fnorm_in", "out_d", "xs_d", "zg_d", "zqk_d", "ztok_d", "va_d", "m_d", "mod_d")]
    (B_xs, B_zg, B_zqk, B_ztok, B_va, B_m, B_mod, B_out, consts, B_consts, ident_bf, B_identbf, cw_sb, B_cw, crep,
     B_crep, ident, ones) = [G[k] for k in
        ("B_xs", "B_zg", "B_zqk", "B_ztok", "B_va", "B_m", "B_mod", "B_out", "consts", "B_consts", "ident_bf",
         "B_identbf", "cw_sb", "B_cw", "crep", "B_crep", "ident", "ones")]
    for l in range(L):
        x_src = x_in if l == 0 else xs_d

        with ExitStack() as ph:
            wa = [sb(f"wa{i}", [128, KC, 512], stack=ph) for i in range(2)]
            B_wa = [Buf(f"wa{i}") for i in range(2)]
            modb = sb("modb", [128, 6 * D], stack=ph)
            B_modb = Buf("modb")
            brow = sb("brow", [128, 6 * D], stack=ph)
            B_brow = Buf("brow")
            nrow = sb("nrow", [128, 2 * D], stack=ph)
            B_nrow = Buf("nrow")
            pm = [ps(f"pm{i}", [128, 512], stack=ph) for i in range(2)]
            B_pm = [Buf(f"pm{i}") for i in range(2)]
            st_w = mk.stream("wada")
            st_m = mk.stream("modst")
            mk.dma(SP, st_m, brow[:], bada_in[l:l + 1, :].partition_broadcast(128), writes=[B_brow])
            mk.dma(SP, st_m, nrow[:, 0:D], nmix_in[l:l + 1, :].partition_broadcast(128), writes=[B_nrow])
            mk.dma(SP, st_m, nrow[:, D:2 * D], nmlp_in[l:l + 1, :].partition_broadcast(128), writes=[B_nrow])
            wav = wada_in[l].rearrange("(kc p) n -> p kc n", p=128)
            for nb in range(12):
                i = nb % 2
                mk.dma(SP, f"wada{i}", wa[i][:], wav[:, :, nb * 512:(nb + 1) * 512], writes=[B_wa[i]])
                for kc in range(KC):
                    mk.op(PE, lambda e, i=i, kc=kc: e.matmul(pm[i][:], lhsT=crep[:, kc, :], rhs=wa[i][:, kc, :],
                                                             start=(kc == 0), stop=(kc == KC - 1)),
                          reads=[B_crep, B_wa[i]], writes=[B_pm[i]], signal=(kc == KC - 1))
                mk.op(DVE, lambda e, i=i, nb=nb: e.tensor_tensor(out=modb[:, nb * 512:(nb + 1) * 512], in0=pm[i][:],
                                                                   in1=brow[:, nb * 512:(nb + 1) * 512], op=ALU.add),
                      reads=[B_pm[i], B_brow], writes=[B_modb])
            mk.op(DVE, lambda e: e.tensor_scalar(out=nrow[:], in0=nrow[:], scalar1=float(D) ** 0.5, scalar2=None,
                                                 op0=ALU.mult), writes=[B_nrow])
            mk.op(DVE, lambda e: e.scalar_tensor_tensor(out=modb[:, D:2 * D], in0=modb[:, D:2 * D], scalar=1.0,
                                                         in1=nrow[:, 0:D], op0=ALU.add, op1=ALU.mult),
                  reads=[B_nrow], writes=[B_modb])
            mk.op(DVE, lambda e: e.scalar_tensor_tensor(out=modb[:, 4 * D:5 * D], in0=modb[:, 4 * D:5 * D], scalar=1.0,
                                                         in1=nrow[:, D:2 * D], op0=ALU.add, op1=ALU.mult),
                  reads=[B_nrow], writes=[B_modb])
            mk.dma(SP, st_m, mod_d[:, :], modb[:], reads=[B_modb], writes=[B_mod])
            mk.flush()
        ck(f"mod{l}")

        with ExitStack() as ph:
            hT = sb("hT", [128, KC, S], BF16, stack=ph)
            B_hT = Buf("hT")
            m1 = sb("m1", [128, 2 * D], stack=ph)
            B_m1 = Buf("m1")
            st_l = mk.stream("p1l")
            st_s = mk.stream("p1s")
            st_wc = mk.stream("p1w")
            mk.dma(SP, "p1m", m1[:], mod_d[:, 0:2 * D], reads=[B_mod], writes=[B_m1])
            with ExitStack() as p1a:
                xt = [sb(f"xt{i}", [128, D], stack=p1a) for i in range(2)]
                B_xt = [Buf(f"xt{i}") for i in range(2)]
                junk = sb("junk", [128, D], stack=p1a)
                B_junk = Buf("junk")
                h1 = sb("h1", [128, D], stack=p1a)
                B_h1 = Buf("h1")
                hb = [sb(f"hb{i}", [128, D], BF16, stack=p1a) for i in range(2)]
                B_hb = [Buf(f"hb{i}") for i in range(2)]
                ssq = sb("ssq", [128, 2], stack=p1a)
                B_ssq = Buf("ssq")
                ptr = [ps(f"ptr{i}", [128, KC, 128], BF16, stack=p1a) for i in range(2)]
                B_ptr = [Buf(f"ptr{i}") for i in range(2)]
                for t in range(NT):
                    i = t % 2
                    mk.dma(SP, f"p1x{i}", xt[i][:], x_src[t * 128:(t + 1) * 128, :], reads=[B_xs[t]] if l > 0 else [],
                           writes=[B_xt[i]])
                    mk.op(ACT, lambda e, i=i: e.activation(out=junk[:], in_=xt[i][:], func=AF.Square,
                                                           accum_out=ssq[:, 0:1]),
                          reads=[B_xt[i]], writes=[B_junk, B_ssq])
                    mk.op(ACT, lambda e: e.activation(out=ssq[:, 1:2], in_=ssq[:, 0:1], func=AF.Sqrt, bias=EPS * D),
                          reads=[B_ssq], writes=[B_ssq])
                    mk.op(DVE, lambda e: e.reciprocal(out=ssq[:, 1:2], in_=ssq[:, 1:2]), reads=[B_ssq], writes=[B_ssq])
                    mk.op(DVE, lambda e, i=i: e.scalar_tensor_tensor(out=h1[:], in0=xt[i][:], scalar=ssq[:, 1:2],
                                                                     in1=m1[:, D:2 * D], op0=ALU.mult, op1=ALU.mult),
                          reads=[B_xt[i], B_ssq, B_m1], writes=[B_h1])
                    mk.op(POOL, lambda e, i=i: e.tensor_tensor(out=hb[i][:], in0=h1[:], in1=m1[:, 0:D], op=ALU.add),
                          reads=[B_h1, B_m1], writes=[B_hb[i]])
                    for kc in range(KC):
                        mk.op(PE, lambda e, i=i, kc=kc: e.transpose(ptr[i][:, kc, :], hb[i][:, kc * 128:(kc + 1) * 128],
                                                                    ident_bf[:]),
                              reads=[B_hb[i], B_identbf], writes=[B_ptr[i]], signal=(kc == KC - 1))
                    mk.op(ACT, lambda e, i=i, t=t: e.activation(out=hT[:, :, t * 128:(t + 1) * 128], in_=ptr[i][:],
                                                                func=AF.Copy),
                          reads=[B_ptr[i]], writes=[B_hT])
                mk.flush()
            winv = win_in[l].rearrange("(kc p) n -> p kc n", p=128)
            with ExitStack() as p1b:
                wb = [sb(f"wb{i}", [128, KC, 512], BF16, stack=p1b) for i in range(2)]
                B_wb = [Buf(f"wb{i}") for i in range(2)]
                pz = [ps(f"pz{i}", [128, 512], stack=p1b) for i in range(3)]
                B_pz = [Buf(f"pz{i}") for i in range(3)]
                pn = [ps(f"pn{i}", [128, 512], stack=p1b) for i in range(2)]
                B_pn = [Buf(f"pn{i}") for i in range(2)]
                xc = [sb(f"xc{i}", [128, 515], stack=p1b) for i in range(2)]
                B_xc = [Buf(f"xc{i}") for i in range(2)]
                yc = [sb(f"yc{i}", [128, 512], stack=p1b) for i in range(2)]
                B_yc = [Buf(f"yc{i}") for i in range(2)]
                sq = [sb(f"sq{i}", [128, 512], stack=p1b) for i in range(2)]
                B_sq = [Buf(f"sq{i}") for i in range(2)]
                rn = [sb(f"rn{i}", [128, 512], stack=p1b) for i in range(2)]
                B_rn = [Buf(f"rn{i}") for i in range(2)]
                NST = min(S, 2048)
                stg = [sb(f"stg{i}", [128, NST], stack=p1b) for i in range(2)]
                B_stg = [Buf(f"stg{i}") for i in range(2)]
                stgb = [sb(f"stgb{i}", [128, NST], BF16, stack=p1b) for i in range(2)]
                B_stgb = [Buf(f"stgb{i}") for i in range(2)]
                nchunk = S // 512
                cpst = NST // 512
                sidx = 0
                wi = 0
                pzi = 0
                pending = []
                deferred = None
                for g in range(24 + 16):
                    gdn = g < 24
                    col0 = g * 128 if gdn else 4112 + (g - 24) * 128
                    if g % 4 == 0:
                        wi += 1
                        i = wi % 2
                        mk.dma(POOL, f"p1w{i}", wb[i][:], winv[:, :, col0:col0 + 512], writes=[B_wb[i]])
                    wo4 = (g % 4) * 128
                    for c in range(nchunk):
                        pi = pzi % 3
                        pzi += 1
                        for kc in range(KC):
                            mk.op(PE, lambda e, i=i, kc=kc, c=c, pi=pi, wo4=wo4: e.matmul(
                                pz[pi][:], lhsT=wb[i][:, kc, wo4:wo4 + 128], rhs=hT[:, kc, c * 512:(c + 1) * 512],
                                start=(kc == 0), stop=(kc == KC - 1)),
                                reads=[B_wb[i], B_hT], writes=[B_pz[pi]], signal=(kc == KC - 1))
                        sl = (sidx // cpst) % 2
                        so = (sidx % cpst) * 512
                        if gdn:
                            j = c % 2
                            if c == 0:
                                mk.op(POOL, lambda e, j=j: e.memset(xc[j][:, 0:3], 0.0), writes=[B_xc[j]])
                            mk.op(ACT, lambda e, j=j, pi=pi: e.activation(out=xc[j][:, 3:515], in_=pz[pi][:], func=AF.Copy),
                                  reads=[B_pz[pi]], writes=[B_xc[j]])
                            if c + 1 < nchunk:
                                mk.op(POOL, lambda e, j=j: e.tensor_copy(out=xc[1 - j][:, 0:3], in_=xc[j][:, 512:515]),
                                      reads=[B_xc[j]], writes=[B_xc[1 - j]])
                            CE = DVE
                            cwb = l * 96 + g * 4
                            mk.op(CE, lambda e, j=j, cwb=cwb: e.tensor_scalar(
                                out=yc[j][:], in0=xc[j][:, 0:512], scalar1=cw_sb[:, cwb:cwb + 1], scalar2=None,
                                op0=ALU.mult), reads=[B_xc[j], B_cw], writes=[B_yc[j]])
                            for tp in range(1, 4):
                                if CE is DVE:
                                    mk.op(CE, lambda e, j=j, cwb=cwb, tp=tp: e.scalar_tensor_tensor(
                                        out=yc[j][:], in0=xc[j][:, tp:tp + 512], scalar=cw_sb[:, cwb + tp:cwb + tp + 1],
                                        in1=yc[j][:], op0=ALU.mult, op1=ALU.add),
                                        reads=[B_xc[j], B_cw], writes=[B_yc[j]])
                                else:
                                    mk.op(CE, lambda e, j=j, cwb=cwb, tp=tp: e.tensor_scalar(
                                        out=sq[j][:], in0=xc[j][:, tp:tp + 512], scalar1=cw_sb[:, cwb + tp:cwb + tp + 1],
                                        scalar2=None, op0=ALU.mult), reads=[B_xc[j], B_cw], writes=[B_sq[j]])
                                    mk.op(CE, lambda e, j=j: e.tensor_tensor(out=yc[j][:], in0=yc[j][:], in1=sq[j][:],
                                                                            op=ALU.add),
                                          reads=[B_sq[j]], writes=[B_yc[j]])
                            if g >= 16:
                                mk.op(ACT, lambda e, j=j, sl=sl, so=so: e.activation(
                                    out=stg[sl][:, so:so + 512], in_=yc[j][:], func=AF.Silu),
                                    reads=[B_yc[j]], writes=[B_stg[sl]])
                            else:
                                mk.op(ACT, lambda e, j=j: e.activation(out=yc[j][:], in_=yc[j][:], func=AF.Silu),
                                      reads=[], writes=[B_yc[j]])
                                isq = g < 8
                                s_in = (float(HD) ** 0.5) if isq else 1.0
                                eps2 = EPS * HD if isq else EPS
                                mk.op(ACT, lambda e, j=j, s_in=s_in: e.activation(out=sq[j][:], in_=yc[j][:], func=AF.Square,
                                                                                  scale=s_in),
                                      reads=[B_yc[j]], writes=[B_sq[j]])

                                def partB(j=j, sl=sl, so=so, eps2=eps2):
                                    mk.op(PE, lambda e: e.matmul(pn[j][:], lhsT=ones, rhs=sq[j][:], start=True, stop=True),
                                          reads=[B_sq[j], B_consts], writes=[B_pn[j]])
                                    mk.op(ACT, lambda e: e.activation(out=rn[j][:], in_=pn[j][:], func=AF.Sqrt, bias=eps2),
                                          reads=[B_pn[j]], writes=[B_rn[j]])
                                    mk.op(DVE, lambda e: e.reciprocal(out=rn[j][:], in_=rn[j][:]), writes=[B_rn[j]])
                                    mk.op(DVE, lambda e: e.tensor_tensor(
                                        out=stg[sl][:, so:so + 512], in0=yc[j][:], in1=rn[j][:], op=ALU.mult),
                                        reads=[B_yc[j], B_rn[j]], writes=[B_stg[sl]])
                                deferred = partB
                        else:
                            mk.op(ACT, lambda e, pi=pi, sl=sl, so=so: e.activation(
                                out=stgb[sl][:, so:so + 512], in_=pz[pi][:], func=AF.Copy),
                                reads=[B_pz[pi]], writes=[B_stgb[sl]])
                        sidx += 1
                        store = None
                        if sidx % cpst == 0:
                            t0 = (c + 1) * 512 - NST
                            if gdn:
                                store = lambda sl=sl, t0=t0, g=g: mk.dma(SP, f"p1s{sl}", zg_d[g * 128:(g + 1) * 128, t0:t0 + NST],
                                                                        stg[sl][:], reads=[B_stg[sl]], writes=[B_zg])
                            else:
                                store = lambda sl=sl, t0=t0, g=g: mk.dma(SP, f"p1sb{sl}", zqk_d[(g - 24) * 128:(g - 23) * 128, t0:t0 + NST],
                                                                        stgb[sl][:], reads=[B_stgb[sl]], writes=[B_zqk])
                        for fn_ in pending:
                            fn_()
                        pending = [f_ for f_ in (deferred, store) if f_ is not None]
                        deferred = None
                    if nchunk % 2 == 1:
                        for fn_ in pending:
                            fn_()
                        pending = []
                for fn_ in pending:
                    fn_()
                pending = []
                mk.flush()
            with ExitStack() as p1c:
                wt = [sb(f"wt{i}", [128, KC, 512], BF16, stack=p1c) for i in range(2)]
                B_wt = [Buf(f"wt{i}") for i in range(2)]
                pz = [ps(f"pzt{i}", [128, 512], stack=p1c) for i in range(3)]
                B_pz = [Buf(f"pzt{i}") for i in range(3)]
                stt = [sb(f"stt{i}", [128, 4, 512], stack=p1c) for i in range(2)]
                B_stt = [Buf(f"stt{i}") for i in range(2)]
                sttb = [sb(f"sttb{i}", [128, 4, 512], BF16, stack=p1c) for i in range(2)]
                B_sttb = [Buf(f"sttb{i}") for i in range(2)]
                blocks = [(3072, 512, AF.Silu, 0, 0), (3584, 512, AF.Silu, 0, 512), (4096, 16, AF.Copy, 0, 1024),
                          (6160, 512, AF.Copy, 1, 0), (6672, 512, AF.Copy, 1, 512)]
                for q in range(4):
                    blocks.append((7184 + q * 512, 512, AF.Sigmoid, 0, 1040 + q * 512))
                pzi = 0
                sidx = 0
                ztv = ztok_d.rearrange("(n p) c -> p n c", p=128)
                vav = va_d.rearrange("(n p) c -> p n c", p=128)
                for bi, (col0, ncol, func, dest, dcol) in enumerate(blocks):
                    i = bi % 2
                    mk.dma(POOL, f"p1wt{i}", wt[i][:, :, 0:ncol], winv[:, :, col0:col0 + ncol], writes=[B_wt[i]])
                    for t in range(NT):
                        pi = pzi % 3
                        pzi += 1
                        for kc in range(KC):
                            mk.op(PE, lambda e, i=i, kc=kc, t=t, pi=pi, ncol=ncol: e.matmul(
                                pz[pi][:, 0:ncol], lhsT=hT[:, kc, t * 128:(t + 1) * 128], rhs=wt[i][:, kc, 0:ncol],
                                start=(kc == 0), stop=(kc == KC - 1)),
                                reads=[B_wt[i], B_hT], writes=[B_pz[pi]], signal=(kc == KC - 1))
                        sl = (sidx // 4) % 2
                        so = sidx % 4
                        sidx += 1
                        if dest == 0:
                            mk.op(ACT, lambda e, pi=pi, sl=sl, so=so, ncol=ncol, func=func: e.activation(
                                out=stt[sl][:, so, 0:ncol], in_=pz[pi][:, 0:ncol], func=func),
                                reads=[B_pz[pi]], writes=[B_stt[sl]])
                        else:
                            mk.op(ACT, lambda e, pi=pi, sl=sl, so=so, ncol=ncol, func=func: e.activation(
                                out=sttb[sl][:, so, 0:ncol], in_=pz[pi][:, 0:ncol], func=func),
                                reads=[B_pz[pi]], writes=[B_sttb[sl]])
                        if sidx % 4 == 0:
                            n0 = t - 3
                            if dest == 0:
                                mk.dma(SP, f"p1st{sl}", ztv[:, n0:n0 + 4, dcol:dcol + ncol], stt[sl][:, :, 0:ncol],
                                       reads=[B_stt[sl]], writes=[B_ztok])
                            else:
                                mk.dma(SP, f"p1stb{sl}", vav[:, n0:n0 + 4, dcol:dcol + ncol], sttb[sl][:, :, 0:ncol],
                                       reads=[B_sttb[sl]], writes=[B_va])
                mk.flush()
        ck(f"p1{l}")

        with ExitStack() as ph:
            p2_mixers(nc, mk, ph, l, S, NT, dict(
                consts=consts, B_consts=B_consts, ident_bf=ident_bf, B_identbf=B_identbf,
                zg_d=zg_d, zqk_d=zqk_d, ztok_d=ztok_d, va_d=va_d, m_d=m_d,
                B_zg=B_zg, B_zqk=B_zqk, B_ztok=B_ztok, B_va=B_va, B_m=B_m,
                alog_in=alog_in, dtb_in=dtb_in, gnorm_in=gnorm_in, biasT_in=biasT_in))
            mk.flush()
        ck(f"p2{l}")

        with ExitStack() as ph:
            wo = sb("wo", [128, KC, D], BF16, stack=ph)
            B_wo = Buf("wo")
            gt1 = sb("gt1", [128, D], stack=ph)
            B_gt1 = Buf("gt1")
            st_l = mk.stream("p3al")
            st_s = mk.stream("p3as")
            st_wc = mk.stream("p3aw")
            wov = wout_in[l].rearrange("(kc p) n -> p kc n", p=128)
            for hf in range(2):
                mk.dma(POOL, f"p3aw{hf}", wo[:, :, hf * 512:(hf + 1) * 512], wov[:, :, hf * 512:(hf + 1) * 512], writes=[B_wo])
            mk.dma(SP, "p3ag", gt1[:], mod_d[:, 2 * D:3 * D], reads=[B_mod], writes=[B_gt1])
            mt = [sb(f"mt{i}", [128, D], BF16, stack=ph) for i in range(2)]
            B_mt = [Buf(f"mt{i}") for i in range(2)]
            xt = [sb(f"xa{i}", [128, D], stack=ph) for i in range(2)]
            B_xt = [Buf(f"xa{i}") for i in range(2)]
            mT = [sb(f"mT{i}", [128, KC, 128], BF16, stack=ph) for i in range(2)]
            B_mT = [Buf(f"mT{i}") for i in range(2)]
            ptr = [ps(f"ptra{i}", [128, KC, 128], BF16, stack=ph) for i in range(2)]
            B_ptr = [Buf(f"ptra{i}") for i in range(2)]
            py = [ps(f"pya{i}", [128, 512], stack=ph) for i in range(4)]
            B_py = [Buf(f"pya{i}") for i in range(4)]
            tmp = [sb(f"tmpa{i}", [128, D], stack=ph) for i in range(2)]
            B_tmp = [Buf(f"tmpa{i}") for i in range(2)]
            xo = [sb(f"xoa{i}", [128, D], stack=ph) for i in range(2)]
            B_xo = [Buf(f"xoa{i}") for i in range(2)]
            for t in range(NT):
                i = t % 2
                mk.dma(SP, f"p3am{i}", mt[i][:], m_d[t * 128:(t + 1) * 128, :], reads=[B_m], writes=[B_mt[i]])
                mk.dma(SP, f"p3ax{i}", xt[i][:], x_src[t * 128:(t + 1) * 128, :], reads=[B_xs[t]] if l > 0 else [],
                       writes=[B_xt[i]])
                for kc in range(KC):
                    mk.op(PE, lambda e, i=i, kc=kc: e.transpose(ptr[i][:, kc, :], mt[i][:, kc * 128:(kc + 1) * 128], ident_bf[:]),
                          reads=[B_mt[i], B_identbf], writes=[B_ptr[i]], signal=(kc == KC - 1))
                mk.op(ACT, lambda e, i=i: e.activation(out=mT[i][:], in_=ptr[i][:], func=AF.Copy),
                      reads=[B_ptr[i]], writes=[B_mT[i]])
                for hf in range(2):
                    pi = i * 2 + hf
                    for kc in range(KC):
                        mk.op(PE, lambda e, i=i, kc=kc, hf=hf, pi=pi: e.matmul(
                            py[pi][:], lhsT=mT[i][:, kc, :], rhs=wo[:, kc, hf * 512:(hf + 1) * 512],
                            start=(kc == 0), stop=(kc == KC - 1)),
                            reads=[B_mT[i], B_wo], writes=[B_py[pi]], signal=(kc == KC - 1))
                    mk.op(DVE, lambda e, i=i, hf=hf, pi=pi: e.tensor_tensor(
                        out=tmp[i][:, hf * 512:(hf + 1) * 512], in0=py[pi][:], in1=gt1[:, hf * 512:(hf + 1) * 512],
                        op=ALU.mult), reads=[B_py[pi], B_gt1], writes=[B_tmp[i]])
                mk.op(POOL, lambda e, i=i: e.tensor_tensor(out=xo[i][:], in0=tmp[i][:], in1=xt[i][:], op=ALU.add),
                      reads=[B_tmp[i], B_xt[i]], writes=[B_xo[i]])
                mk.dma(SP, f"p3as{i}", xs_d[t * 128:(t + 1) * 128, :], xo[i][:], reads=[B_xo[i]], writes=[B_xs[t]])
            mk.flush()
        ck(f"p3a{l}")

        with ExitStack() as ph:
            last = (l == L - 1)
            w1 = sb("w1", [128, KC, DFF], BF16, stack=ph)
            w2 = sb("w2", [128, 32, D], BF16, stack=ph)
            B_w1, B_w2 = Buf("w1"), Buf("w2")
            m2 = sb("m2", [128, 3 * D], stack=ph)
            B_m2 = Buf("m2")
            st_l = mk.stream("p3bl")
            st_s = mk.stream("p3bs")
            st_wc = mk.stream("p3bw")
            w1v = w1_in[l].rearrange("(kc p) n -> p kc n", p=128)
            w2v = w2_in[l].rearrange("(fc p) n -> p fc n", p=128)
            for q in range(8):
                mk.dma(POOL, f"p3bw{q % 4}", w1[:, :, q * 512:(q + 1) * 512], w1v[:, :, q * 512:(q + 1) * 512], writes=[B_w1])
            for q in range(8):
                mk.dma(POOL, f"p3bw{q % 4}", w2[:, q * 4:(q + 1) * 4, :], w2v[:, q * 4:(q + 1) * 4, :], writes=[B_w2])
            mk.dma(SP, "p3bm", m2[:], mod_d[:, 3 * D:6 * D], reads=[B_mod], writes=[B_m2])
            if last:
                fn = sb("fnrm", [128, D], stack=ph)
                B_fn = Buf("fn")
                mk.dma(SP, "p3bf", fn[:], fnorm_in[0:1, :].partition_broadcast(128), writes=[B_fn])
                mk.op(DVE, lambda e: e.tensor_scalar(out=fn[:], in0=fn[:], scalar1=float(D) ** 0.5, scalar2=None,
                                                     op0=ALU.mult), writes=[B_fn])
            xt = [sb(f"xb{i}", [128, D], stack=ph) for i in range(2)]
            B_xt = [Buf(f"xb{i}") for i in range(2)]
            h1 = sb("h1b", [128, D], stack=ph)
            B_h1 = Buf("h1b")
            junk, B_junk = h1, B_h1
            hb = [sb(f"hbb{i}", [128, D], BF16, stack=ph) for i in range(2)]
            B_hb = [Buf(f"hbb{i}") for i in range(2)]
            hT2 = [sb(f"hT2{i}", [128, KC, 128], BF16, stack=ph) for i in range(2)]
            B_hT2 = [Buf(f"hT2{i}") for i in range(2)]
            ssq = sb("ssqb", [128, 4], stack=ph)
            B_ssq = Buf("ssqb")
            ptr = [ps(f"ptrb{i}", [128, KC, 128], BF16, stack=ph) for i in range(1)]
            B_ptr = [Buf(f"ptrb{i}") for i in range(1)]
            pa = [ps(f"pa{i}", [128, 4, 128], stack=ph) for i in range(3)]
            B_pa = [Buf(f"pa{i}") for i in range(3)]
            py = [ps(f"pyb{i}", [128, 512], stack=ph) for i in range(4)]
            B_py = [Buf(f"pyb{i}") for i in range(4)]
            rl = [sb(f"rl{i}", [128, 4, 128], stack=ph) for i in range(2)]
            B_rl = [Buf(f"rl{i}") for i in range(2)]
            aT = [sb(f"aT{i}", [128, 32, 128], BF16, stack=ph) for i in range(2)]
            B_aT = [Buf(f"aT{i}") for i in range(2)]
            tmp = [sb(f"tmpb{i}", [128, D], stack=ph) for i in range(2)]
            B_tmp = [Buf(f"tmpb{i}") for i in range(2)]
            xo, B_xo = xt, B_xt
            pai = 0
            for t in range(NT):
                i = t % 2
                mk.dma(SP, f"p3bx{i}", xt[i][:], xs_d[t * 128:(t + 1) * 128, :], reads=[B_xs[t]], writes=[B_xt[i]])
                mk.op(ACT, lambda e, i=i: e.activation(out=junk[:], in_=xt[i][:], func=AF.Square, accum_out=ssq[:, 0:1]),
                      reads=[B_xt[i]], writes=[B_junk, B_ssq])
                mk.op(ACT, lambda e: e.activation(out=ssq[:, 1:2], in_=ssq[:, 0:1], func=AF.Sqrt, bias=EPS * D),
                      reads=[B_ssq], writes=[B_ssq])
                mk.op(DVE, lambda e: e.reciprocal(out=ssq[:, 1:2], in_=ssq[:, 1:2]), reads=[B_ssq], writes=[B_ssq])
                mk.op(DVE, lambda e, i=i: e.scalar_tensor_tensor(out=h1[:], in0=xt[i][:], scalar=ssq[:, 1:2],
                                                                 in1=m2[:, D:2 * D], op0=ALU.mult, op1=ALU.mult),
                      reads=[B_xt[i], B_ssq, B_m2], writes=[B_h1])
                mk.op(POOL, lambda e, i=i: e.tensor_tensor(out=hb[i][:], in0=h1[:], in1=m2[:, 0:D], op=ALU.add),
                      reads=[B_h1, B_m2], writes=[B_hb[i]])
                for kc in range(KC):
                    mk.op(PE, lambda e, i=i, kc=kc: e.transpose(ptr[0][:, kc, :], hb[i][:, kc * 128:(kc + 1) * 128], ident_bf[:]),
                          reads=[B_hb[i], B_identbf], writes=[B_ptr[0]], signal=(kc == KC - 1))
                mk.op(ACT, lambda e, i=i: e.activation(out=hT2[i][:], in_=ptr[0][:], func=AF.Copy),
                      reads=[B_ptr[0]], writes=[B_hT2[i]])
                for fq in range(8):
                    pi = pai % 3
                    ri = pai % 2
                    pai += 1
                    for f4 in range(4):
                        fb = fq * 4 + f4
                        for kc in range(KC):
                            mk.op(PE, lambda e, i=i, kc=kc, fb=fb, f4=f4, pi=pi: e.matmul(
                                pa[pi][:, f4, :], lhsT=w1[:, kc, fb * 128:(fb + 1) * 128], rhs=hT2[i][:, kc, :],
                                start=(kc == 0), stop=(kc == KC - 1)),
                                reads=[B_w1, B_hT2[i]], writes=[B_pa[pi]], signal=(kc == KC - 1 and f4 == 3))
                    mk.op(ACT, lambda e, pi=pi, ri=ri: e.activation(out=rl[ri][:], in_=pa[pi][:], func=AF.Relu),
                          reads=[B_pa[pi]], writes=[B_rl[ri]])
                    mk.op(ACT, lambda e, i=i, ri=ri, fq=fq: e.activation(
                        out=aT[i][:, fq * 4:(fq + 1) * 4, :], in_=rl[ri][:], func=AF.Square),
                        reads=[B_rl[ri]], writes=[B_aT[i]])
                for hf in range(2):
                    pi = i * 2 + hf
                    for fc in range(32):
                        mk.op(PE, lambda e, i=i, fc=fc, hf=hf, pi=pi: e.matmul(
                            py[pi][:], lhsT=aT[i][:, fc, :], rhs=w2[:, fc, hf * 512:(hf + 1) * 512],
                            start=(fc == 0), stop=(fc == 31)),
                            reads=[B_aT[i], B_w2], writes=[B_py[pi]], signal=(fc == 31))
                    mk.op(DVE, lambda e, i=i, hf=hf, pi=pi: e.tensor_tensor(
                        out=tmp[i][:, hf * 512:(hf + 1) * 512], in0=py[pi][:], in1=m2[:, 2 * D + hf * 512:2 * D + (hf + 1) * 512],
                        op=ALU.mult), reads=[B_py[pi], B_m2], writes=[B_tmp[i]])
                mk.op(POOL, lambda e, i=i: e.tensor_tensor(out=xo[i][:], in0=tmp[i][:], in1=xt[i][:], op=ALU.add),
                      reads=[B_tmp[i], B_xt[i]], writes=[B_xo[i]])
                if not last:
                    mk.dma(SP, f"p3bs{i}", xs_d[t * 128:(t + 1) * 128, :], xo[i][:], reads=[B_xo[i]], writes=[B_xs[t]])
                else:
                    mk.op(ACT, lambda e, i=i: e.activation(out=junk[:], in_=xo[i][:], func=AF.Square, accum_out=ssq[:, 2:3]),
                          reads=[B_xo[i]], writes=[B_junk, B_ssq])
                    mk.op(ACT, lambda e: e.activation(out=ssq[:, 3:4], in_=ssq[:, 2:3], func=AF.Sqrt, bias=EPS * D),
                          reads=[B_ssq], writes=[B_ssq])
                    mk.op(DVE, lambda e: e.reciprocal(out=ssq[:, 3:4], in_=ssq[:, 3:4]), reads=[B_ssq], writes=[B_ssq])
                    mk.op(DVE, lambda e, i=i: e.scalar_tensor_tensor(out=tmp[i][:], in0=xo[i][:], scalar=ssq[:, 3:4],
                                                                     in1=fn[:], op0=ALU.mult, op1=ALU.mult),
                          reads=[B_xo[i], B_ssq, B_fn], writes=[B_tmp[i]])
                    mk.dma(SP, f"p3bo{i}", out_d[t * 128:(t + 1) * 128, :], tmp[i][:], reads=[B_tmp[i]], writes=[B_out])
            if last:
                mk.final_wait(SP, [B_out])
            mk.flush()
            ck(f"p3b{l}")


def p2_mixers(nc, mk, ph, l, S, NT, g):
    PE, ACT, DVE, POOL, SP = mk.PE, mk.ACT, mk.DVE, mk.POOL, mk.SP
    consts, B_consts = g["consts"], g["B_consts"]
    ident_bf, B_identbf = g["ident_bf"], g["B_identbf"]
    ident = consts[:, C_ID:C_ID + 128]
    ones = consts[:, C_ONE:C_ONE + 128]
    maskA = consts[:, C_MA:C_MA + 128]
    strict = consts[:, C_ST:C_ST + 128]
    tri = consts[:, C_TRI:C_TRI + 128]
    sel0 = consts[:, C_S0:C_S0 + 128]
    sel1 = consts[:, C_S1:C_S1 + 128]

    def sb(name, shape, dt=F32):
        mk.uid += 1
        return ph.enter_context(nc.sbuf_tensor(f"{name}_u{mk.uid}", list(shape), dt))

    def ps(name, shape, dt=F32):
        mk.uid += 1
        return ph.enter_context(nc.psum_tensor(f"{name}_u{mk.uid}", list(shape), dt))

    st_l = mk.stream("p2l")
    st_s = mk.stream("p2s")
    st_k = mk.stream("p2k")
    expB = sb("expB", [128, H * 640])
    B_expB = Buf("expB")
    mk.dma(SP, st_k, expB[:], g["biasT_in"][l], writes=[B_expB])
    mk.op(ACT, lambda e: e.activation(out=expB[:], in_=expB[:], func=AF.Exp), writes=[B_expB])
    for h in range(H):
        mk.op(POOL, lambda e, h=h: e.tensor_tensor(out=expB[:, h * 640:(h + 1) * 640], in0=expB[:, h * 640:(h + 1) * 640],
                                                     in1=consts[:, C_AM:C_AM + 640], op=ALU.mult),
              reads=[B_consts], writes=[B_expB])
    hv = sb("hv", [128, 3 * H + 128])
    B_hv = Buf("hv")
    mk.dma(SP, st_k, hv[:, 0:H], g["dtb_in"][0:1, l * H:(l + 1) * H].partition_broadcast(128), writes=[B_hv])
    mk.dma(SP, st_k, hv[:, H:2 * H], g["alog_in"][0:1, l * H:(l + 1) * H].partition_broadcast(128), writes=[B_hv])
    mk.dma(SP, st_k, hv[:, 3 * H:3 * H + 128], g["gnorm_in"][0:1, l * HD:(l + 1) * HD].partition_broadcast(128),
           writes=[B_hv])
    mk.op(ACT, lambda e: e.activation(out=hv[:, 2 * H:3 * H], in_=hv[:, H:2 * H], func=AF.Exp), writes=[B_hv])
    mk.op(DVE, lambda e: e.tensor_scalar(out=hv[:, H:2 * H], in0=hv[:, 2 * H:3 * H], scalar1=-1.0, scalar2=None,
                                         op0=ALU.mult), writes=[B_hv])
    mk.op(DVE, lambda e: e.tensor_scalar(out=hv[:, 3 * H:3 * H + 128], in0=hv[:, 3 * H:3 * H + 128], scalar1=float(HD) ** 0.5,
                                         scalar2=None, op0=ALU.mult), writes=[B_hv])
    dtb = hv[:, 0:H]
    nA = hv[:, H:2 * H]
    gn = hv[:, 3 * H:3 * H + 128]

    zg = [sb(f"zg{i}", [128, 24, 128]) for i in range(2)]
    B_zgt = [Buf(f"zgt{i}") for i in range(2)]
    zq0 = sb("zq0", [128, H, 128], BF16)
    zq = [zq0, zq0]
    B_zq0 = Buf("zq0")
    B_zq = [B_zq0, B_zq0]
    kring = sb("kring", [128, 5, H, 128], BF16)
    B_kr = [Buf(f"kr{i}") for i in range(5)]
    vring = sb("vring", [128, 5, H, 132], BF16)
    B_vr = [Buf(f"vr{i}") for i in range(5)]
    zt = [sb(f"zt{i}", [128, 3088]) for i in range(2)]
    B_zt = [Buf(f"zt{i}") for i in range(2)]
    mtile = [sb(f"mtile{i}", [128, D], BF16) for i in range(2)]
    B_mtile = [Buf(f"mtile{i}") for i in range(2)]
    for s in range(5):
        mk.op(POOL, lambda e, s=s: e.memset(vring[:, s, :, 128:132], 1.0), writes=[B_vr[s]])

    gsc = [sb(f"gsc{i}", [128, 12 * H]) for i in range(2)]
    B_gsc = [Buf(f"gsc{i}") for i in range(2)]
    Gs = [sb(f"Gs{i}", [128, 3 * H]) for i in range(2)]
    B_Gs = [Buf(f"Gs{i}") for i in range(2)]
    pg = ps("pg", [128, 512])
    B_pgG = B_pgO = B_pgS = Buf("pg")
    psA = ps("psA", [128, 512])
    B_psA = Buf("psA")
    NQ = 12
    pq_t = [ps(f"pq{i}", [128, 4, 128]) for i in range(NQ // 4)]
    NBK = NQ // 4
    pq = [pq_t[i % NBK][:, (i // NBK) % 4, :] for i in range(NQ)]
    B_bank = [Buf(f"pqb{i}") for i in range(NBK)]
    B_pq = [B_bank[i % NBK] for i in range(NQ)]
    state = {"q": 0}

    def getq():
        i = state["q"] % NQ
        state["q"] += 1
        return pq[i], B_pq[i]

    qkb0 = sb("qkb0", [128, 24, 128], BF16)
    qkb = [qkb0, qkb0]
    B_qkb0 = Buf("qkb0")
    B_qkb = [B_qkb0, B_qkb0]
    NW = 1
    def hb(name, shape, dt=F32):
        t = [sb(f"{name}{i}", [128, H] + list(shape), dt) for i in range(NW)]
        b = [[Buf(f"{name}{i}_{h}") for h in range(H)] for i in range(NW)]
        return t, b
    grep_r = [sb(f"grepr{i}", [128, 128]) for i in range(2)]
    B_grep_r = [Buf(f"grepr{i}") for i in range(2)]
    eGr_r = [sb(f"eGrr{i}", [128, 128]) for i in range(2)]
    B_eGr_r = [Buf(f"eGrr{i}") for i in range(2)]
    egk, B_egk = hb("egk", [128], BF16)
    kdec, B_kdec = hb("kdec", [128], BF16)
    vb, B_vb = hb("vb", [128], BF16)
    tG, B_tG = hb("tG", [128])
    DT, B_DT = hb("DT", [128])
    attnT, B_attnT = hb("attnT", [128], BF16)
    X0, B_X0 = tG, B_tG
    Xa, B_Xa = hb("Xa", [128], BF16)
    XTa, B_XTa = hb("XTa", [128], BF16)
    Da, B_Da = hb("Da", [128], BF16)
    Db, B_Db = hb("Db", [128], BF16)
    DTa, B_DTa = hb("DTa", [128], BF16)
    DTb, B_DTb = hb("DTb", [128], BF16)
    C1m, B_C1m = hb("C1m", [128], BF16)
    C2m, B_C2m = hb("C2m", [128], BF16)
    C1T, B_C1T = hb("C1T", [128], BF16)
    C2T, B_C2T = hb("C2T", [128], BF16)
    Pa, B_Pa = hb("Pa", [128], BF16)
    Pb, B_Pb = hb("Pb", [128], BF16)
    PTa, B_PTa = hb("PTa", [128], BF16)
    PTb, B_PTb = hb("PTb", [128], BF16)
    Yb, B_Yb = hb("Yb", [128], BF16)
    Ypb, B_Ypb = hb("Ypb", [128], BF16)
    usb, B_usb = hb("usb", [128])
    wTb, B_wTb = hb("wTb", [128], BF16)
    qdT, B_qdT = hb("qdT", [128], BF16)
    vn, B_vn = hb("vn", [128], BF16)
    GW, B_GW = hb("GW", [128])
    t1, B_t1 = hb("t1", [128])
    mb, B_mb = hb("mb", [128])
    esb = [sb(f"esb{i}", [128, 640]) for i in range(2)]
    B_esb = [Buf(f"esb{i}") for i in range(2)]
    PT = [sb(f"PT{i}", [128, 640], BF16) for i in range(2)]
    B_PT = [Buf(f"PT{i}") for i in range(2)]
    sm, B_sm = hb("sm", [4])
    S32 = sb("S32", [128, H, 128])
    B_S32 = [Buf(f"S32_{h}") for h in range(H)]
    Sb = sb("Sb", [128, H, 3, 128], BF16)
    B_Sb = [[Buf(f"Sb{h}_{s}") for s in range(3)] for h in range(H)]
    mk.op(POOL, lambda e: e.memset(S32[:], 0.0), writes=B_S32)
    mk.op(POOL, lambda e: e.memset(Sb[:], 0.0), writes=[b for r in B_Sb for b in r])
    ptk = ps("ptk", [128, 8, 128], BF16)
    ptv = ps("ptv", [128, 8, 128], BF16)
    B_ptk, B_ptv = Buf("ptk"), Buf("ptv")
    ptb = ps("ptb", [128, 8, 128], BF16)
    B_ptb1 = Buf("ptb")
    B_ptb = [B_ptb1 for h in range(H)]

    zgv = g["zg_d"].rearrange("(g d) s -> d g s", d=128)
    zqv = g["zqk_d"].rearrange("(g d) s -> d g s", d=128)
    vav = g["va_d"].rearrange("s (h d) -> s h d", d=128)

    def do_tile(t):
        i = t % 2
        w = 0
        sl = t % 5
        ts = slice(t * 128, (t + 1) * 128)
        mk.dma(SP, f"p2zg{i}", zg[i][:], zgv[:, :, ts], reads=[g["B_zg"]], writes=[B_zgt[i]])
        mk.dma(SP, "p2zq", zq[i][:], zqv[:, 0:H, ts], reads=[g["B_zqk"]], writes=[B_zq[i]])
        mk.dma(SP, f"p2kr{sl}", kring[:, sl, :, :], zqv[:, H:2 * H, ts], reads=[g["B_zqk"]], writes=[B_kr[sl]])
        mk.dma(SP, f"p2vr{sl}", vring[:, sl, :, 0:128], vav[ts, :, :], reads=[g["B_va"]], writes=[B_vr[sl]])
        mk.dma(SP, f"p2zt{i}", zt[i][:], g["ztok_d"][ts, :], reads=[g["B_ztok"]], writes=[B_zt[i]])
        if DBG["p2_stage"] <= 1:
            return
        gs = gsc[i]
        a_ap = zt[i][:, 1024:1024 + H]
        b_ap = zt[i][:, 1024 + H:1024 + 2 * H]
        bet, nbet, xa, ax, ee, lg, gg, eG, dl, kd = [gs[:, k * H:(k + 1) * H] for k in range(10)]
        cdb = gs[:, 10 * H:12 * H]
        BG = B_gsc[i]
        mk.op(ACT, lambda e: e.activation(out=bet, in_=b_ap, func=AF.Sigmoid), reads=[B_zt[i]], writes=[BG])
        mk.op(DVE, lambda e: e.tensor_scalar(out=nbet, in0=bet, scalar1=-1.0, scalar2=None, op0=ALU.mult), writes=[BG])
        mk.op(DVE, lambda e: e.tensor_tensor(out=xa, in0=a_ap, in1=dtb, op=ALU.add), reads=[B_zt[i], B_hv], writes=[BG])
        mk.op(ACT, lambda e: e.activation(out=ax, in_=xa, func=AF.Abs), writes=[BG])
        mk.op(ACT, lambda e: e.activation(out=ee, in_=ax, func=AF.Exp, scale=-1.0), writes=[BG])
        mk.op(ACT, lambda e: e.activation(out=lg, in_=ee, func=AF.Ln, bias=1.0), writes=[BG])
        mk.op(DVE, lambda e: e.scalar_tensor_tensor(out=lg, in0=xa, scalar=0.0, in1=lg, op0=ALU.max, op1=ALU.add),
              writes=[BG])
        mk.op(DVE, lambda e: e.tensor_tensor(out=gg, in0=lg, in1=nA, op=ALU.mult), reads=[B_hv], writes=[BG])
        pG = pg[:, 0:3 * H]
        if DBG.get("skipG"):
            mk.op(POOL, lambda e: e.memset(Gs[i][:], 0.0), writes=[B_Gs[i]])
        else:
            mk.op(PE, lambda e: e.matmul(pG[:, 0:H], lhsT=tri, rhs=gg, start=True, stop=True),
                  reads=[BG, B_consts], writes=[B_pgG], signal=False)
            mk.op(PE, lambda e: e.matmul(pG[:, H:2 * H], lhsT=sel0, rhs=gg, start=True, stop=True),
                  reads=[BG, B_consts], writes=[B_pgG], signal=False)
            mk.op(PE, lambda e: e.matmul(pG[:, 2 * H:3 * H], lhsT=sel1, rhs=gg, start=True, stop=True),
                  reads=[BG, B_consts], writes=[B_pgG])
            mk.op(ACT, lambda e: e.activation(out=Gs[i][:], in_=pG, func=AF.Copy), reads=[B_pgG], writes=[B_Gs[i]])
        Gc = Gs[i][:, 0:H]
        mk.op(ACT, lambda e: e.activation(out=eG, in_=Gc, func=AF.Exp), reads=[B_Gs[i]], writes=[BG])
        mk.op(DVE, lambda e: e.tensor_tensor(out=dl[0:64, :], in0=Gs[i][0:64, H:2 * H], in1=Gs[i][0:64, 0:H],
                                             op=ALU.subtract), reads=[B_Gs[i]], writes=[BG])
        mk.op(DVE, lambda e: e.tensor_tensor(out=dl[64:128, :], in0=Gs[i][64:128, 2 * H:3 * H], in1=Gs[i][64:128, 0:H],
                                             op=ALU.subtract), reads=[B_Gs[i]], writes=[BG])
        mk.op(ACT, lambda e: e.activation(out=kd, in_=dl, func=AF.Exp), writes=[BG])
        mk.op(ACT, lambda e: e.activation(out=cdb, in_=Gs[i][:, H:3 * H], func=AF.Exp), reads=[B_Gs[i]], writes=[BG])
        mk.op(ACT, lambda e: e.activation(out=qkb[i][:], in_=zg[i][:], func=AF.Copy),
              reads=[B_zgt[i]], writes=[B_qkb[i]])
        if DBG["p2_stage"] <= 2:
            return

        m_lo = max(0, t - 4)
        mis = [m - (t - 4) for m in range(m_lo, t + 1)]
        mlo = mis[0]
        sc = HD ** -0.5

        def s_att(h):
            e2 = (t * H + h) % 2
            for mi in mis:
                m = t - 4 + mi
                dst = psA[:, mi * 128:(mi + 1) * 128] if mi < 4 else pg[:, 384:512]
                Bd = B_psA if mi < 4 else B_pgS
                mk.op(PE, lambda e, m=m, dst=dst: e.matmul(dst, lhsT=kring[:, m % 5, h, :], rhs=zq[i][:, h, :],
                                                            start=True, stop=True),
                      reads=[B_kr[m % 5], B_zq[i]], writes=[Bd], signal=(mi == mis[-1] or mi == 3))
            if DBG["p2_stage"] <= 2.1:
                return
            if mlo < 4:
                mk.op(ACT, lambda e: e.activation(out=esb[e2][:, mlo * 128:512], in_=psA[:, mlo * 128:512],
                                                  func=AF.Exp, scale=sc),
                      reads=[B_psA], writes=[B_esb[e2]])
            mk.op(ACT, lambda e: e.activation(out=esb[e2][:, 512:640], in_=pg[:, 384:512], func=AF.Exp, scale=sc),
                  reads=[B_pgS], writes=[B_esb[e2]])
            if DBG["p2_stage"] <= 2.2:
                return
            mk.op(POOL, lambda e: e.tensor_tensor(
                out=PT[e2][:, mlo * 128:640], in0=esb[e2][:, mlo * 128:640],
                in1=expB[:, h * 640 + mlo * 128:(h + 1) * 640], op=ALU.mult),
                reads=[B_esb[e2], B_expB], writes=[B_PT[e2]])
            if DBG["p2_stage"] <= 2.3:
                return
            for mi in mis:
                m = t - 4 + mi
                mk.op(PE, lambda e, m=m, mi=mi: e.matmul(pg[:, 128:258], lhsT=PT[e2][:, mi * 128:(mi + 1) * 128],
                                                         rhs=vring[:, m % 5, h, 0:130],
                                                         start=(mi == mis[0]), stop=(mi == mis[-1])),
                      reads=[B_PT[e2], B_vr[m % 5]], writes=[B_pgO], signal=(mi == mis[-1]))
            if DBG["p2_stage"] <= 2.4:
                return
            mk.op(DVE, lambda e: e.reciprocal(out=sm[w][:, h, 0:1], in_=pg[:, 256:257]),
                  reads=[B_pgO], writes=[B_sm[w][h]])
            gb_ap = zt[i][:, 1040 + 1024 + h * 128:1040 + 1024 + (h + 1) * 128]
            mk.op(DVE, lambda e: e.scalar_tensor_tensor(
                out=mb[w][:, h, :], in0=pg[:, 128:256], scalar=sm[w][:, h, 0:1], in1=gb_ap,
                op0=ALU.mult, op1=ALU.mult), reads=[B_pgO, B_sm[w][h], B_zt[i]], writes=[B_mb[w][h]])
        for h in range(H):
            s_att(h)
        if DBG["p2_stage"] <= 3:
            return

        def stage(fn):
            for h in range(H):
                fn(h)

        hq = {}

        def s_pre(h):
            pk, Bpk = ptk[:, h, :], B_ptk
            mk.op(PE, lambda e: e.transpose(pk, qkb[i][:, 8 + h, :], ident_bf[:]), reads=[B_qkb[i], B_identbf], writes=[Bpk])
            mk.op(ACT, lambda e: e.activation(out=egk[w][:, h, :], in_=pk, func=AF.Copy, scale=eG[:, h:h + 1]),
                  reads=[Bpk, BG], writes=[B_egk[w][h]])
            mk.op(DVE, lambda e: e.tensor_scalar(out=kdec[w][:, h, :], in0=pk, scalar1=kd[:, h:h + 1], scalar2=None,
                                                 op0=ALU.mult), reads=[Bpk, BG], writes=[B_kdec[w][h]])
            pv, Bpv = ptv[:, h, :], B_ptv
            mk.op(PE, lambda e: e.transpose(pv, qkb[i][:, 16 + h, :], ident_bf[:]), reads=[B_qkb[i], B_identbf], writes=[Bpv])
            mk.op(ACT, lambda e: e.activation(out=vb[w][:, h, :], in_=pv, func=AF.Copy), reads=[Bpv], writes=[B_vb[w][h]])
            gr, Bgr = grep_r[h % 2], B_grep_r[h % 2]
            er, Ber = eGr_r[h % 2], B_eGr_r[h % 2]
            mk.op(POOL, lambda e: e.tensor_scalar(out=gr[:], in0=ones, scalar1=gg[:, h:h + 1], scalar2=None,
                                                  op0=ALU.mult), reads=[BG, B_consts], writes=[Bgr])
            pgr, Bpgr = getq()
            mk.op(PE, lambda e: e.matmul(pgr, lhsT=gr[:], rhs=tri, start=True, stop=True),
                  reads=[Bgr, B_consts], writes=[Bpgr])
            mk.op(DVE, lambda e: e.scalar_tensor_tensor(out=tG[w][:, h, :], in0=pgr, scalar=Gs[i][:, h:h + 1], in1=maskA,
                                                        op0=ALU.subtract, op1=ALU.add),
                  reads=[Bpgr, B_Gs[i], B_consts], writes=[B_tG[w][h]])
            mk.op(ACT, lambda e: e.activation(out=DT[w][:, h, :], in_=tG[w][:, h, :], func=AF.Exp),
                  reads=[B_tG[w][h]], writes=[B_DT[w][h]])
            mk.op(ACT, lambda e: e.activation(out=er[:], in_=pgr, func=AF.Exp), reads=[Bpgr], writes=[Ber])
            mk.op(POOL, lambda e: e.tensor_tensor(out=qdT[w][:, h, :], in0=zg[i][:, h, :], in1=er[:], op=ALU.mult),
                  reads=[B_zgt[i], Ber], writes=[B_qdT[w][h]])
            pkk, Bpkk = getq()
            mk.op(PE, lambda e: e.matmul(pkk, lhsT=qkb[i][:, 8 + h, :], rhs=qkb[i][:, 8 + h, :], start=True, stop=True),
                  reads=[B_qkb[i]], writes=[Bpkk])
            pat, Bpat = getq()
            mk.op(PE, lambda e: e.matmul(pat, lhsT=qkb[i][:, 8 + h, :], rhs=qkb[i][:, h, :], start=True, stop=True),
                  reads=[B_qkb[i]], writes=[Bpat])
            mk.op(DVE, lambda e: e.tensor_tensor(out=attnT[w][:, h, :], in0=pat, in1=DT[w][:, h, :], op=ALU.mult),
                  reads=[Bpat, B_DT[w][h]], writes=[B_attnT[w][h]])
            mk.op(DVE, lambda e: e.scalar_tensor_tensor(out=X0[w][:, h, :], in0=pkk, scalar=nbet[:, h:h + 1],
                                                        in1=DT[w][:, h, :], op0=ALU.mult, op1=ALU.mult),
                  reads=[Bpkk, BG, B_DT[w][h]], writes=[B_X0[w][h]])
            def msk(dst, Bd, srcb, Bs, col):
                mk.op(POOL, lambda e: e.tensor_tensor(out=dst[w][:, h, :], in0=srcb[w][:, h, :],
                                                      in1=consts[:, col:col + 128], op=ALU.mult),
                      reads=[Bs[w][h], B_consts], writes=[Bd[w][h]])
            msk(Xa, B_Xa, X0, B_X0, C_ST)
            msk(Da, B_Da, X0, B_X0, C_MD)
            msk(C1m, B_C1m, X0, B_X0, C_MC1)
            msk(C2m, B_C2m, X0, B_X0, C_MC2)
            mk.op(PE, lambda e: e.transpose(ptb[:, h, :], Xa[w][:, h, :], ident_bf[:]),
                  reads=[B_Xa[w][h], B_identbf], writes=[B_ptb[h]])
            mk.op(ACT, lambda e: e.activation(out=XTa[w][:, h, :], in_=ptb[:, h, :], func=AF.Copy),
                  reads=[B_ptb[h]], writes=[B_XTa[w][h]])
            msk(DTa, B_DTa, XTa, B_XTa, C_MDT)
            msk(C1T, B_C1T, XTa, B_XTa, C_MC1T)
            msk(C2T, B_C2T, XTa, B_XTa, C_MC2T)
            mk.op(POOL, lambda e: e.tensor_tensor(out=Pa[w][:, h, :], in0=Da[w][:, h, :], in1=ident_bf[:], op=ALU.add),
                  reads=[B_Da[w][h], B_identbf], writes=[B_Pa[w][h]])
            mk.op(POOL, lambda e: e.tensor_tensor(out=PTa[w][:, h, :], in0=DTa[w][:, h, :], in1=ident_bf[:], op=ALU.add),
                  reads=[B_DTa[w][h], B_identbf], writes=[B_PTa[w][h]])
            hq[h] = dict(D=(Da, B_Da), DT=(DTa, B_DTa), Dn=(Db, B_Db), DTn=(DTb, B_DTb),
                         P=(Pa, B_Pa), Pn=(Pb, B_Pb), PT=(PTa, B_PTa), PTn=(PTb, B_PTb))

        stage(s_pre)
        if DBG["p2_stage"] <= 4:
            return

        if DBG.get("ginv", 1):
            def getbank():
                b = state["q"] % NBK
                state["q"] += 1
                return pq_t[b], B_bank[b]

            def grp(buf, B, hg):
                return buf[w][:, 4 * hg:4 * hg + 4, :], [B[w][4 * hg + q] for q in range(4)]

            def gmm(hg, L_, BL, R_, BR):
                pb, Bb = getbank()
                for q in range(4):
                    h = 4 * hg + q
                    mk.op(PE, lambda e, h=h, q=q: e.matmul(pb[:, q, :], lhsT=L_[w][:, h, :], rhs=R_[w][:, h, :], start=True, stop=True),
                          reads=[BL[w][h], BR[w][h]], writes=[Bb], signal=(q == 3))
                return pb, Bb

            def gevac(E, dst, Bd, hg, pb, Bb):
                o_, Bo = grp(dst, Bd, hg)
                if E is ACT:
                    mk.op(ACT, lambda e: e.activation(out=o_, in_=pb[:], func=AF.Copy), reads=[Bb], writes=Bo)
                else:
                    mk.op(DVE, lambda e: e.tensor_copy(out=o_, in_=pb[:]), reads=[Bb], writes=Bo)

            def gadd(dst, Bd, base, Bbase, hg, pb, Bb):
                o_, Bo = grp(dst, Bd, hg)
                b_, Bbs = grp(base, Bbase, hg)
                mk.op(DVE, lambda e: e.tensor_tensor(out=o_, in0=pb[:], in1=b_, op=ALU.add), reads=[Bb] + Bbs, writes=Bo)

            d = hq[0]
            for lev in range(1, 4):
                (Dm, BD), (DT_, BDT), (Dn, BDn), (DTn, BDTn) = d["D"], d["DT"], d["Dn"], d["DTn"]
                (Px, BPx), (Pnx, BPnx), (PTx, BPTx), (PTnx, BPTnx) = d["P"], d["Pn"], d["PT"], d["PTn"]
                for hg in range(2):
                    pb, Bb = gmm(hg, Dm, BD, DT_, BDT)
                    gevac(ACT, DTn, BDTn, hg, pb, Bb)
                if lev < 3:
                    for hg in range(2):
                        pb, Bb = gmm(hg, DT_, BDT, Dm, BD)
                        gevac(DVE, Dn, BDn, hg, pb, Bb)
                for hg in range(2):
                    pb, Bb = gmm(hg, DTn, BDTn, Px, BPx)
                    gadd(Pnx, BPnx, Px, BPx, hg, pb, Bb)
                for hg in range(2):
                    pb, Bb = gmm(hg, Px, BPx, DTn, BDTn)
                    gadd(PTnx, BPTnx, PTx, BPTx, hg, pb, Bb)
                d["D"], d["Dn"] = d["Dn"], d["D"]
                d["DT"], d["DTn"] = d["DTn"], d["DT"]
                d["P"], d["Pn"] = d["Pn"], d["P"]
                d["PT"], d["PTn"] = d["PTn"], d["PT"]
            (Px, BPx), (Pnx, BPnx), (PTx, BPTx), (PTnx, BPTnx) = d["P"], d["Pn"], d["PT"], d["PTn"]
            for hg in range(2):
                pb, Bb = gmm(hg, C1T, B_C1T, Px, BPx)
                gevac(ACT, Yb, B_Yb, hg, pb, Bb)
            for hg in range(2):
                pb, Bb = gmm(hg, C1m, B_C1m, PTx, BPTx)
                gevac(ACT, Ypb, B_Ypb, hg, pb, Bb)
            for hg in range(2):
                pb, Bb = gmm(hg, PTx, BPTx, Yb, B_Yb)
                gadd(Pnx, BPnx, Px, BPx, hg, pb, Bb)
            for hg in range(2):
                pb, Bb = gmm(hg, Px, BPx, Ypb, B_Ypb)
                gadd(PTnx, BPTnx, PTx, BPTx, hg, pb, Bb)
            d["P"], d["Pn"] = d["Pn"], d["P"]
            d["PT"], d["PTn"] = d["PTn"], d["PT"]
            (Px, BPx), (Pnx, BPnx), (PTx, BPTx) = d["P"], d["Pn"], d["PT"]
            for hg in range(2):
                pb, Bb = gmm(hg, C2T, B_C2T, Px, BPx)
                gevac(ACT, Yb, B_Yb, hg, pb, Bb)
            for hg in range(2):
                pb, Bb = gmm(hg, PTx, BPTx, Yb, B_Yb)
                gadd(Pnx, BPnx, Px, BPx, hg, pb, Bb)
            d["P"], d["Pn"] = d["Pn"], d["P"]
            for h in range(H):
                hq[h] = d

            if DBG["p2_stage"] <= 5:
                return

            (Px, BPx) = hq[0]["P"]
            for hg in range(2):
                pb, Bb = gmm(hg, Px, BPx, vb, B_vb)
                for q in range(4):
                    h = 4 * hg + q
                    mk.op(ACT, lambda e, h=h, q=q, pb=pb: e.activation(out=usb[w][:, h, :], in_=pb[:, q, :], func=AF.Copy,
                                                                     scale=bet[:, h:h + 1]),
                          reads=[Bb, BG], writes=[B_usb[w][h]])
            for hg in range(2):
                pb, Bb = gmm(hg, egk, B_egk, Px, BPx)
                gevac(ACT, wTb, B_wTb, hg, pb, Bb)
        else:
            def evac(E, dst, Bd, psrc, Bp):
                if E is ACT:
                    mk.op(ACT, lambda e: e.activation(out=dst, in_=psrc, func=AF.Copy), reads=[Bp], writes=[Bd])
                else:
                    mk.op(DVE, lambda e: e.tensor_copy(out=dst, in_=psrc), reads=[Bp], writes=[Bd])

            def mm(lhsT, Bl, rhs, Br):
                p_, Bp_ = getq()
                mk.op(PE, lambda e: e.matmul(p_, lhsT=lhsT, rhs=rhs, start=True, stop=True), reads=[Bl, Br], writes=[Bp_])
                return p_, Bp_

            def addto(dst, Bd, p_, Bp_, base, Bb):
                mk.op(DVE, lambda e: e.tensor_tensor(out=dst, in0=p_, in1=base, op=ALU.add), reads=[Bp_, Bb], writes=[Bd])

            for lev in range(1, 4):
                def s_sq(h, lev=lev):
                    d = hq[h]
                    (Dm, BD), (DT_, BDT), (Dn, BDn), (DTn, BDTn) = d["D"], d["DT"], d["Dn"], d["DTn"]
                    p1, Bp1 = mm(Dm[w][:, h, :], BD[w][h], DT_[w][:, h, :], BDT[w][h])
                    evac(ACT, DTn[w][:, h, :], BDTn[w][h], p1, Bp1)
                    if lev < 3:
                        p2, Bp2 = mm(DT_[w][:, h, :], BDT[w][h], Dm[w][:, h, :], BD[w][h])
                        evac(DVE, Dn[w][:, h, :], BDn[w][h], p2, Bp2)
                stage(s_sq)

                def s_p(h, lev=lev):
                    d = hq[h]
                    (DTn, BDTn), (P, BP), (Pn, BPn), (PT, BPT), (PTn, BPTn) = d["DTn"], d["P"], d["Pn"], d["PT"], d["PTn"]
                    p3, Bp3 = mm(DTn[w][:, h, :], BDTn[w][h], P[w][:, h, :], BP[w][h])
                    addto(Pn[w][:, h, :], BPn[w][h], p3, Bp3, P[w][:, h, :], BP[w][h])
                    p4, Bp4 = mm(P[w][:, h, :], BP[w][h], DTn[w][:, h, :], BDTn[w][h])
                    addto(PTn[w][:, h, :], BPTn[w][h], p4, Bp4, PT[w][:, h, :], BPT[w][h])
                    d["D"], d["Dn"] = d["Dn"], d["D"]
                    d["DT"], d["DTn"] = d["DTn"], d["DT"]
                    d["P"], d["Pn"] = d["Pn"], d["P"]
                    d["PT"], d["PTn"] = d["PTn"], d["PT"]
                stage(s_p)

            def s_m1(h):
                d = hq[h]
                (P, BP), (Pn, BPn), (PT, BPT), (PTn, BPTn) = d["P"], d["Pn"], d["PT"], d["PTn"]
                py, Bpy = mm(C1T[w][:, h, :], B_C1T[w][h], P[w][:, h, :], BP[w][h])
                evac(ACT, Yb[w][:, h, :], B_Yb[w][h], py, Bpy)
                py2, Bpy2 = mm(C1m[w][:, h, :], B_C1m[w][h], PT[w][:, h, :], BPT[w][h])
                evac(ACT, Ypb[w][:, h, :], B_Ypb[w][h], py2, Bpy2)
                pz, Bpz = mm(PT[w][:, h, :], BPT[w][h], Yb[w][:, h, :], B_Yb[w][h])
                addto(Pn[w][:, h, :], BPn[w][h], pz, Bpz, P[w][:, h, :], BP[w][h])
                pz2, Bpz2 = mm(P[w][:, h, :], BP[w][h], Ypb[w][:, h, :], B_Ypb[w][h])
                addto(PTn[w][:, h, :], BPTn[w][h], pz2, Bpz2, PT[w][:, h, :], BPT[w][h])
                d["P"], d["Pn"] = d["Pn"], d["P"]
                d["PT"], d["PTn"] = d["PTn"], d["PT"]
            stage(s_m1)

            def s_m2(h):
                d = hq[h]
                (P, BP), (Pn, BPn), (PT, BPT) = d["P"], d["Pn"], d["PT"]
                py, Bpy = mm(C2T[w][:, h, :], B_C2T[w][h], P[w][:, h, :], BP[w][h])
                evac(ACT, Yb[w][:, h, :], B_Yb[w][h], py, Bpy)
                pz, Bpz = mm(PT[w][:, h, :], BPT[w][h], Yb[w][:, h, :], B_Yb[w][h])
                addto(Pn[w][:, h, :], BPn[w][h], pz, Bpz, P[w][:, h, :], BP[w][h])
                d["P"], d["Pn"] = d["Pn"], d["P"]
            stage(s_m2)

            if DBG["p2_stage"] <= 5:
                return

            def s_uw(h):
                (P, BP) = hq[h]["P"]
                pu, Bpu = getq()
                mk.op(PE, lambda e: e.matmul(pu, lhsT=P[w][:, h, :], rhs=vb[w][:, h, :], start=True, stop=True),
                      reads=[BP[w][h], B_vb[w][h]], writes=[Bpu])
                mk.op(ACT, lambda e: e.activation(out=usb[w][:, h, :], in_=pu, func=AF.Copy, scale=bet[:, h:h + 1]),
                      reads=[Bpu, BG], writes=[B_usb[w][h]])
                pw, Bpw = getq()
                mk.op(PE, lambda e: e.matmul(pw, lhsT=egk[w][:, h, :], rhs=P[w][:, h, :], start=True, stop=True),
                      reads=[BP[w][h], B_egk[w][h]], writes=[Bpw])
                mk.op(ACT, lambda e: e.activation(out=wTb[w][:, h, :], in_=pw, func=AF.Copy), reads=[Bpw], writes=[B_wTb[w][h]])
            stage(s_uw)
        if DBG["p2_stage"] <= 6:
            return

        pws = {}
        for c in range(2):
            n = 2 * t + c
            cur, nxt = n % 3, (n + 1) % 3
            rs = slice(c * 64, (c + 1) * 64)

            def s_ws(h, c=c, cur=cur, rs=rs):
                if c == 0:
                    pws[h] = getq()
                pw_, Bpw_ = pws[h]
                mk.op(PE, lambda e: e.matmul(pw_[rs, :], lhsT=wTb[w][:, h, rs], rhs=Sb[:, h, cur, :], start=True, stop=True),
                      reads=[B_wTb[w][h], B_Sb[h][cur]], writes=[Bpw_])
                mk.op(DVE, lambda e: e.scalar_tensor_tensor(out=vn[w][rs, h, :], in0=pw_[rs, :], scalar=nbet[rs, h:h + 1],
                                                            in1=usb[w][rs, h, :], op0=ALU.mult, op1=ALU.add),
                      reads=[Bpw_, BG, B_usb[w][h]], writes=[B_vn[w][h]])
            stage(s_ws)

            def s_ds(h, c=c, nxt=nxt, rs=rs):
                pd, Bpd = getq()
                mk.op(PE, lambda e: e.matmul(pd, lhsT=kdec[w][rs, h, :], rhs=vn[w][rs, h, :], start=True, stop=True),
                      reads=[B_kdec[w][h], B_vn[w][h]], writes=[Bpd])
                mk.op(DVE, lambda e: e.scalar_tensor_tensor(out=S32[:, h, :], in0=S32[:, h, :],
                                                            scalar=cdb[:, c * H + h:c * H + h + 1], in1=pd,
                                                            op0=ALU.mult, op1=ALU.add),
                      reads=[Bpd, BG], writes=[B_S32[h]])
                mk.op(ACT, lambda e: e.activation(out=Sb[:, h, nxt, :], in_=S32[:, h, :], func=AF.Copy),
                      reads=[B_S32[h]], writes=[B_Sb[h][nxt]])
            stage(s_ds)
        if DBG["p2_stage"] <= 7:
            return

        def s_out(h):
            po, Bpo = getq()
            s0, s1 = (2 * t) % 3, (2 * t + 1) % 3
            mk.op(PE, lambda e: e.matmul(po[0:64, :], lhsT=qdT[w][:, h, 0:64], rhs=Sb[:, h, s0, :], start=True, stop=False,
                                         skip_group_check=True),
                  reads=[B_qdT[w][h], B_Sb[h][s0]], writes=[Bpo], signal=False)
            mk.op(PE, lambda e: e.matmul(po[64:128, :], lhsT=qdT[w][:, h, 64:128], rhs=Sb[:, h, s1, :], start=True, stop=False,
                                         skip_group_check=True),
                  reads=[B_qdT[w][h], B_Sb[h][s1]], writes=[Bpo], signal=False)
            mk.op(PE, lambda e: e.matmul(po, lhsT=attnT[w][:, h, :], rhs=vn[w][:, h, :], start=False, stop=True,
                                         skip_group_check=True),
                  reads=[B_attnT[w][h], B_vn[w][h]], writes=[Bpo])
            mk.op(ACT, lambda e: e.activation(out=t1[w][:, h, :], in_=po, func=AF.Square, accum_out=sm[w][:, h, 1:2]),
                  reads=[Bpo], writes=[B_t1[w][h], B_sm[w][h]])
            mk.op(ACT, lambda e: e.activation(out=sm[w][:, h, 2:3], in_=sm[w][:, h, 1:2], func=AF.Sqrt, bias=EPS * HD),
                  writes=[B_sm[w][h]])
            mk.op(DVE, lambda e: e.reciprocal(out=sm[w][:, h, 2:3], in_=sm[w][:, h, 2:3]), writes=[B_sm[w][h]])
            ga_ap = zt[i][:, 1040 + h * 128:1040 + (h + 1) * 128]
            mk.op(POOL, lambda e: e.tensor_tensor(out=GW[w][:, h, :], in0=zt[i][:, h * 128:(h + 1) * 128], in1=gn, op=ALU.mult),
                  reads=[B_zt[i], B_hv], writes=[B_GW[w][h]])
            mk.op(POOL, lambda e: e.tensor_tensor(out=GW[w][:, h, :], in0=GW[w][:, h, :], in1=ga_ap, op=ALU.mult),
                  reads=[B_zt[i]], writes=[B_GW[w][h]])
            mk.op(DVE, lambda e: e.scalar_tensor_tensor(out=t1[w][:, h, :], in0=po, scalar=sm[w][:, h, 2:3], in1=GW[w][:, h, :],
                                                        op0=ALU.mult, op1=ALU.mult),
                  reads=[Bpo, B_sm[w][h], B_GW[w][h]], writes=[B_t1[w][h]])
            mk.op(POOL, lambda e: e.tensor_tensor(out=mtile[i][:, h * 128:(h + 1) * 128], in0=t1[w][:, h, :], in1=mb[w][:, h, :],
                                                  op=ALU.add),
                  reads=[B_t1[w][h], B_mb[w][h]], writes=[B_mtile[i]])
        stage(s_out)
        mk.dma(SP, f"p2m{i}", g["m_d"][ts, :], mtile[i][:], reads=[B_mtile[i]], writes=[g["B_m"]])

    for t in range(NT):
        do_tile(t)


def make_consts():
    c = np.zeros((128, C_N), np.float32)
    p = np.arange(128)[:, None]
    f = np.arange(128)[None, :]
    same = (p // 64) == (f // 64)
    c[:, C_ID:C_ID + 128] = np.eye(128)
    c[:, C_ONE:C_ONE + 128] = 1.0
    c[:, C_MA:C_MA + 128] = np.where(same & (f >= p), 0.0, NEG)
    c[:, C_ST:C_ST + 128] = (same & (f > p))
    c[:, C_TRI:C_TRI + 128] = (same & (p <= f))
    kl = np.arange(128)[:, None, None]
    mi = np.arange(5)[None, :, None]
    ql = np.arange(128)[None, None, :]
    dch = 2 * (mi - 4) + kl // 64 - ql // 64
    c[:, C_AM:C_AM + 640] = ((dch >= -8) & (dch <= 0)).reshape(128, 640)
    st = same & (f > p)
    bd16 = (p // 16) == (f // 16)
    bd32 = (p // 32) == (f // 32)
    md, mc1, mc2 = st & bd16, st & bd32 & ~bd16, st & ~bd32
    c[:, C_MD:C_MD + 128] = md
    c[:, C_MC1:C_MC1 + 128] = mc1
    c[:, C_MC2:C_MC2 + 128] = mc2
    c[:, C_MDT:C_MDT + 128] = md.T
    c[:, C_MC1T:C_MC1T + 128] = mc1.T
    c[:, C_MC2T:C_MC2T + 128] = mc2.T
    c[:, C_S0:C_S0 + 128] = (p < 64)
    c[:, C_S1:C_S1 + 128] = (p >= 64)
    return c


def layout_inputs(x, c, w_ada, b_ada, norm_mix, norm_mlp, w_in, conv_w, a_log, dt_bias,
                  gdn_norm, rel_bias, w_out, w_ff_in, w_ff_out, final_norm):
    f = lambda a: np.ascontiguousarray(np.asarray(a, dtype=np.float32))
    L = w_ada.shape[0]
    kl = np.arange(128)[:, None, None]
    mi = np.arange(5)[None, :, None]
    ql = np.arange(128)[None, None, :]
    idx = np.clip(ql + 128 * (4 - mi) - kl, -256, 256) + 256
    rb = np.asarray(rel_bias, np.float32)
    biasT = rb[:, :, idx]
    biasT = f(biasT.transpose(0, 2, 1, 3, 4).reshape(L, 128, H * 640))
    cw = np.asarray(conv_w, np.float32).reshape(L, 4, 24, 128).transpose(3, 0, 2, 1).reshape(128, L * 96)
    shared = dict(
        w_ada=f(w_ada), b_ada=f(b_ada), norm_mix=f(norm_mix), norm_mlp=f(norm_mlp), w_in=f(w_in),
        cw=f(cw), a_log=f(np.asarray(a_log).reshape(1, -1)), dt_bias=f(np.asarray(dt_bias).reshape(1, -1)),
        gdn_norm=f(np.asarray(gdn_norm).reshape(1, -1)), biasT=biasT, w_out=f(w_out), w_ff_in=f(w_ff_in),
        w_ff_out=f(w_ff_out), final_norm=f(np.asarray(final_norm).reshape(1, -1)), consts=make_consts())
    x = np.asarray(x, np.float32)
    c = np.asarray(c, np.float32)
    per = []
    for b in range(x.shape[0]):
        d = dict(shared)
        d["x"] = f(x[b])
        d["cT"] = f(c[b].reshape(KC, 128).T)
        per.append(d)
    return per


def kernel(**inputs):
    x = np.asarray(inputs["x"])
    B, S, _ = x.shape
    L = np.asarray(inputs["w_ada"]).shape[0]
    per = layout_inputs(**inputs)
    nc, mk = build_program(S, L)
    n = B
    in_maps = [per[j] for j in range(n)]
    res = run_bass_kernel_spmd(nc, in_maps, core_ids=list(range(n)))
    out = np.stack([np.asarray(res.results[b]["out"], np.float32) for b in range(B)], axis=0)
    return out
```
